# Optimizing a Trainium2 kernel written in Bass

```python
import jax, jax.numpy as jnp
from jax import lax
import numpy as np

D_MODEL = 1024
BATCH = 8
SEQ = 2048
DEPTH = 4

GRID_W = 64
CTX_LEN = 256
HEAD_DIM = 64
N_Q_HEADS = 8
N_KV_HEADS = 2
GQA_GROUP = N_Q_HEADS // N_KV_HEADS
ATTN_WIDTH = N_Q_HEADS * HEAD_DIM
KV_WIDTH = N_KV_HEADS * HEAD_DIM
CONV_WIDTH = D_MODEL - ATTN_WIDTH
CONV_KERNEL = 31
Q_BLOCK = 128
D_FF = -(-8 * D_MODEL // (3 * 256)) * 256
ROPE_THETA = 10000.0
DN_ALPHA = (2 * DEPTH) ** 0.25
DN_BETA = (8 * DEPTH) ** -0.25
LN_EPS = 1e-6
RMS_EPS = 1e-6
ATTN_SCALE = HEAD_DIM ** -0.5
IN_WIDTH = ATTN_WIDTH + 2 * KV_WIDTH + 2 * CONV_WIDTH
SPLITS = (ATTN_WIDTH, ATTN_WIDTH + KV_WIDTH, ATTN_WIDTH + 2 * KV_WIDTH,
          ATTN_WIDTH + 2 * KV_WIDTH + CONV_WIDTH)

kernel_name = "hymba_conformer_gqa_deepnorm_dit"


def layer_norm(x, g, b):
    xf = x.astype(jnp.float32)
    mu = xf.mean(-1, keepdims=True)
    var = jnp.square(xf - mu).mean(-1, keepdims=True)
    return ((xf - mu) * lax.rsqrt(var + LN_EPS) * g + b).astype(x.dtype)


def rms_norm(x, g):
    xf = x.astype(jnp.float32)
    return (xf * lax.rsqrt(jnp.square(xf).mean(-1, keepdims=True) + RMS_EPS) * g).astype(x.dtype)


def axial_rope_tables(rows):
    r, col = jnp.meshgrid(jnp.arange(rows), jnp.arange(GRID_W), indexing="ij")
    pos = jnp.stack([r.reshape(-1), col.reshape(-1)], axis=-1).astype(jnp.float32)
    n_freq = HEAD_DIM // 4
    freqs = ROPE_THETA ** (-jnp.arange(n_freq, dtype=jnp.float32) / n_freq)
    ang = pos[:, :, None] * freqs
    return jnp.cos(ang), jnp.sin(ang)


def apply_axial_rope(x, cos, sin):
    lead = x.shape[:-1]
    xa = x.astype(jnp.float32).reshape(*lead, 2, 2, HEAD_DIM // 4)
    x1, x2 = xa[..., 0, :], xa[..., 1, :]
    expand = (1, cos.shape[0]) + (1,) * (x.ndim - 3) + cos.shape[1:]
    cs, sn = cos.reshape(expand), sin.reshape(expand)
    out = jnp.stack([x1 * cs - x2 * sn, x2 * cs + x1 * sn], axis=-2)
    return out.reshape(x.shape).astype(x.dtype)


def project(h, w):
    B, L, _ = h.shape
    q, k, v, u, g = jnp.split(h @ w, SPLITS, axis=-1)
    q = q.reshape(B, L, N_KV_HEADS, GQA_GROUP, HEAD_DIM)
    k = k.reshape(B, L, N_KV_HEADS, HEAD_DIM)
    v = v.reshape(B, L, N_KV_HEADS, HEAD_DIM)
    return q, k, v, u, g


def project_kv(h, w):
    B, L, _ = h.shape
    k, v = jnp.split(h @ w[:, SPLITS[0]:SPLITS[2]], 2, axis=-1)
    return k.reshape(B, L, N_KV_HEADS, HEAD_DIM), v.reshape(B, L, N_KV_HEADS, HEAD_DIM)


def gqa_attend(q, k, v):
    s = jnp.einsum("bqhgd,bkhd->bhgqk", q, k, preferred_element_type=jnp.float32) * ATTN_SCALE
    p = jax.nn.softmax(s, axis=-1).astype(v.dtype)
    return jnp.einsum("bhgqk,bkhd->bqhgd", p, v)


def latent_attention(q, k_all, v_all):
    B, n = q.shape[:2]
    nblk = n // Q_BLOCK
    qb = q.reshape(B, nblk, Q_BLOCK, N_KV_HEADS, GQA_GROUP, HEAD_DIM).transpose(1, 0, 2, 3, 4, 5)
    ob = lax.map(lambda blk: gqa_attend(blk, k_all, v_all), qb)
    return ob.transpose(1, 0, 2, 3, 4, 5).reshape(B, n, ATTN_WIDTH)


def conformer_conv(u, g, dw_w, dw_b, ln_g, ln_b):
    a = u * jax.nn.sigmoid(g)
    y = lax.conv_general_dilated(
        a, dw_w.astype(a.dtype), window_strides=(1,),
        padding=[(CONV_KERNEL // 2, CONV_KERNEL // 2)],
        dimension_numbers=("NWC", "WIO", "NWC"), feature_group_count=a.shape[-1]) + dw_b
    return jax.nn.silu(layer_norm(y, ln_g, ln_b))


def swiglu(h, w1, w3, w2):
    return (jax.nn.silu(h @ w1) * (h @ w3)) @ w2


def setup_inputs(seed: int = 0) -> dict:
    key = jax.random.key(seed)
    ks = jax.random.split(key, 24)
    f32 = jnp.float32
    nrm = lambda k, shape, s: jax.random.normal(k, shape, f32) * s
    D = D_MODEL
    return {
        "x": nrm(ks[0], (BATCH, SEQ, D), 1.0),
        "c": nrm(ks[1], (BATCH, D), 1.0),
        "ctx": nrm(ks[2], (BATCH, CTX_LEN, D), 1.0),
        "c_ctx": nrm(ks[3], (D,), 1.0),
        "w_ada": nrm(ks[4], (DEPTH, D, 6 * D), 0.5 * D ** -0.5),
        "b_ada": nrm(ks[5], (DEPTH, 6 * D), 0.02),
        "w_in": nrm(ks[6], (DEPTH, D, IN_WIDTH), D ** -0.5),
        "q_norm_g": 1.0 + nrm(ks[7], (DEPTH, HEAD_DIM), 0.02),
        "k_norm_g": 1.0 + nrm(ks[8], (DEPTH, HEAD_DIM), 0.02),
        "dw_w": nrm(ks[9], (DEPTH, CONV_KERNEL, 1, CONV_WIDTH), CONV_KERNEL ** -0.5),
        "dw_b": nrm(ks[10], (DEPTH, CONV_WIDTH), 0.02),
        "conv_ln_g": 1.0 + nrm(ks[11], (DEPTH, CONV_WIDTH), 0.02),
        "conv_ln_b": nrm(ks[12], (DEPTH, CONV_WIDTH), 0.02),
        "w_out": nrm(ks[13], (DEPTH, D, D), DN_BETA * D ** -0.5),
        "ln1_g": 1.0 + nrm(ks[14], (DEPTH, D), 0.02),
        "ln1_b": nrm(ks[15], (DEPTH, D), 0.02),
        "w_ff1": nrm(ks[16], (DEPTH, D, D_FF), D ** -0.5),
        "w_ff3": nrm(ks[17], (DEPTH, D, D_FF), D ** -0.5),
        "w_ff2": nrm(ks[18], (DEPTH, D_FF, D), DN_BETA * D_FF ** -0.5),
        "ln2_g": 1.0 + nrm(ks[19], (DEPTH, D), 0.02),
        "ln2_b": nrm(ks[20], (DEPTH, D), 0.02),
    }


def reference(x, c, ctx, c_ctx, w_ada, b_ada, w_in, q_norm_g, k_norm_g, dw_w, dw_b,
              conv_ln_g, conv_ln_b, w_out, ln1_g, ln1_b, w_ff1, w_ff3, w_ff2, ln2_g, ln2_b):
    B, n, _ = x.shape
    Lc = ctx.shape[1]
    rows = n // GRID_W
    cos, sin = axial_rope_tables(rows)
    s_lat = jax.nn.silu(c)
    s_ctx = jax.nn.silu(c_ctx)
    for l in range(DEPTH):
        last = l == DEPTH - 1
        mod_l = (s_lat @ w_ada[l] + b_ada[l])[:, None, :]
        mod_c = (s_ctx @ w_ada[l] + b_ada[l])[None, None, :]
        sh1, sc1, g1, sh2, sc2, g2 = jnp.split(mod_l, 6, axis=-1)
        csh1, csc1, cg1, csh2, csc2, cg2 = jnp.split(mod_c, 6, axis=-1)

        hl = x * (1.0 + sc1) + sh1
        hc = ctx * (1.0 + csc1) + csh1
        ql, kl, vl, ul, gl = project(hl, w_in[l])
        ql = apply_axial_rope(rms_norm(ql, q_norm_g[l]), cos, sin)
        kl = apply_axial_rope(rms_norm(kl, k_norm_g[l]), cos, sin)
        if last:
            kc, vc = project_kv(hc, w_in[l])
        else:
            qc, kc, vc, uc, gc = project(hc, w_in[l])
        kc = rms_norm(kc, k_norm_g[l])
        k_all = jnp.concatenate([kc, kl], axis=1)
        v_all = jnp.concatenate([vc, vl], axis=1)
        attn_l = latent_attention(ql, k_all, v_all)
        conv_l = conformer_conv(ul, gl, dw_w[l], dw_b[l], conv_ln_g[l], conv_ln_b[l])
        y_l = jnp.concatenate([attn_l, conv_l], axis=-1) @ w_out[l]
        x_new = layer_norm(DN_ALPHA * x + g1 * y_l, ln1_g[l], ln1_b[l])

        hl2 = x_new * (1.0 + sc2) + sh2
        x = layer_norm(DN_ALPHA * x_new + g2 * swiglu(hl2, w_ff1[l], w_ff3[l], w_ff2[l]),
                       ln2_g[l], ln2_b[l])

        if not last:
            qc = rms_norm(qc, q_norm_g[l])
            attn_c = gqa_attend(qc, kc, vc).reshape(B, Lc, ATTN_WIDTH)
            conv_c = conformer_conv(uc, gc, dw_w[l], dw_b[l], conv_ln_g[l], conv_ln_b[l])
            y_c = jnp.concatenate([attn_c, conv_c], axis=-1) @ w_out[l]
            ctx_new = layer_norm(DN_ALPHA * ctx + cg1 * y_c, ln1_g[l], ln1_b[l])
            hc2 = ctx_new * (1.0 + csc2) + csh2
            ctx = layer_norm(DN_ALPHA * ctx_new + cg2 * swiglu(hc2, w_ff1[l], w_ff3[l], w_ff2[l]),
                             ln2_g[l], ln2_b[l])
    return x
```

```python
import numpy as np
from contextlib import ExitStack
import concourse.bass as bass
import concourse.mybir as mybir
from concourse.bass_utils import run_bass_kernel_spmd

F32 = mybir.dt.float32
BF16 = mybir.dt.bfloat16
AF = mybir.ActivationFunctionType
ALU = mybir.AluOpType

P = 128
D = 1024
KT = 8
T = 2304
LC = 256
NL = 2048
DEPTH = 4
DFF = 2816
FT = 22
ALPHA = float(8 ** 0.25)
EPS = 1e-6
CHUNKS = [(0, 256), (256, 512), (768, 512), (1280, 256), (1536, 512), (2048, 256)]
GROUPS = [[0, 1], [2, 3], [4, 5]]
QCH = [(0, 256), (256, 512), (768, 512), (1280, 512), (1792, 512)]
LNCH = [(i * 256, 256) for i in range(9)]
APAD = 286 + 2078
NPAR = 218
ENG = ['pe', 'act', 'dve', 'pool', 'sp']


def units(s, n):
    return range(s // 256, (s + n - 1) // 256 + 1)


def aidx(tok):
    return 15 + tok if tok < 256 else 45 + tok


class _RecInst:
    def __init__(self, idx):
        self.idx = idx


class _Rec:
    def __init__(self):
        self.calls = []

    def __getattr__(self, name):
        def f(*a, **k):
            self.calls.append((name, a, k))
            return _RecInst(len(self.calls) - 1)
        return f


class Lane:
    def __init__(self, sem):
        self.sem = sem
        self.val = 0


class Res:
    __slots__ = ('w', 'r')

    def __init__(self):
        self.w = None
        self.r = {}


class Sched:
    def __init__(self, nc, stack):
        self.nc = nc
        self.stack = stack
        self.progs = {e: [] for e in ENG}
        self.lane = {e: self.newlane('L' + e) for e in ENG}
        self.known = {e: {} for e in ENG}
        self.res = {}
        self.dlanes = []

    def newlane(self, name):
        return Lane(self.stack.enter_context(self.nc.semaphore(name)))

    def dmalane(self, name):
        ln = self.newlane(name)
        self.dlanes.append(ln)
        return ln

    def R(self, *key):
        r = self.res.get(key)
        if r is None:
            r = self.res[key] = Res()
        return r

    def op(self, eng, fn, reads=(), writes=(), lane=None, ndma=0):
        waits = {}

        def need(ev):
            if ev is None:
                return
            ln, v = ev
            if waits.get(ln, 0) < v:
                waits[ln] = v
        for r in reads:
            need(r.w)
        for r in writes:
            need(r.w)
            for ln, v in r.r.items():
                need((ln, v))
        k = self.known[eng]
        wl = []
        for ln, v in waits.items():
            if k.get(ln, 0) < v:
                k[ln] = v
                wl.append((ln.sem, v))
        if lane is None:
            lane = self.lane[eng]
            lane.val += 1
            inc = 1
        else:
            lane.val += 16 * ndma
            inc = 16
        ev = (lane, lane.val)
        for r in reads:
            if r.r.get(lane, 0) < lane.val:
                r.r[lane] = lane.val
        for r in writes:
            r.w = ev
            r.r = {}
        sem = lane.sem
        rec = _Rec()
        out = fn(rec)
        calls = rec.calls
        inc_idx = [o_.idx for o_ in out] if inc == 16 else [out.idx]

        def run(e):
            for s_, v_ in wl:
                e.wait_ge(s_, v_)
            insts = [getattr(e, nm)(*a, **kw) for (nm, a, kw) in calls]
            for i_ in inc_idx:
                insts[i_].then_inc(sem, inc)
        self.progs[eng].append(run)

    def barrier(self):
        for e in ENG:
            k = self.known[e]
            wl = []
            for ln in [self.lane[x] for x in ENG if x != e] + self.dlanes:
                if k.get(ln, 0) < ln.val:
                    k[ln] = ln.val
                    wl.append((ln.sem, ln.val))

            def run(en, wl=wl):
                for s_, v_ in wl:
                    en.wait_ge(s_, v_)
            self.progs[e].append(run)
        for r in self.res.values():
            r.w = None
            r.r = {}

    def final_wait(self, eng, lanes):
        wl = [(ln.sem, ln.val) for ln in lanes]

        def run(en):
            for s_, v_ in wl:
                en.wait_ge(s_, v_)
        self.progs[eng].append(run)


def build(nlayers=DEPTH, dbg=None):
    nc = bass.Bass("TRN2", target_bir_lowering=False)
    dt = nc.dram_tensor
    xT_d = dt("xT", [D, T], F32, kind="ExternalInput").ap()
    cs_d = dt("cs", [P, KT * 2], F32, kind="ExternalInput").ap()
    par_d = dt("par", [P, DEPTH * NPAR], F32, kind="ExternalInput").ap()
    cm_d = dt("cmat", [P, 5 * P], F32, kind="ExternalInput").ap()
    tab_d = dt("tabs", [P, 2 * T], F32, kind="ExternalInput").ap()
    wada_d = dt("w_ada", [DEPTH, D, 6 * D], F32, kind="ExternalInput").ap()
    win_d = dt("w_in", [DEPTH, D, 1792], F32, kind="ExternalInput").ap()
    wout_d = dt("w_out", [DEPTH, D, D], F32, kind="ExternalInput").ap()
    wf1_d = dt("w_ff1", [DEPTH, D, DFF], F32, kind="ExternalInput").ap()
    wf3_d = dt("w_ff3", [DEPTH, D, DFF], F32, kind="ExternalInput").ap()
    wf2_d = dt("w_ff2", [DEPTH, DFF, D], F32, kind="ExternalInput").ap()
    out_d = dt("outT", [D, NL], F32, kind="ExternalOutput").ap()
    xa_d = dt("xa_scr", [D, T], F32).ap()
    dbg_d = None
    if dbg:
        dbg_d = dt("dbg", [P, dbg], F32, kind="ExternalOutput").ap()

    stack = ExitStack()
    with stack:
        def sb(name, shape, dtp):
            return stack.enter_context(nc.sbuf_tensor(name, shape, dtp))
        S = Sched(nc, stack)
        R = S.R
        hT = sb("hT", [P, KT, T], BF16)
        BIGN = 39424
        big = sb("big", [P, BIGN], BF16)
        qT = big[:, 0:9216].rearrange("p (a t) -> p a t", a=4)
        Kz = big[:, 9216:18432].rearrange("p (a t) -> p a t", a=4)
        Vaug = big[:, 18432:25344].rearrange("p (t h c) -> p t h c", t=18, h=2)
        abuf = big[:, 25344:34800].rearrange("p (a t) -> p a t", a=4)
        tabC = big[:, 34800:37104]
        tabS = big[:, 37104:39408]
        act = big[:, 0:16896].rearrange("p (f t) -> p f t", f=FT)
        w2s = big[:, 16896:39424].rearrange("p (f c) -> p f c", f=FT)
        NSLOT = 3
        wsl = [sb(f"ws{i}", [P, KT, 512], BF16) for i in range(NSLOT)]
        wsl_lane = [S.dmalane(f"wsl{i}") for i in range(NSLOT)]
        NSTG = 4
        stg = [sb(f"stg{i}", [P, KT, 256], F32) for i in range(NSTG)]
        stg_in = [S.dmalane(f"sti{i}") for i in range(NSTG)]
        stg_out = [S.dmalane(f"sto{i}") for i in range(NSTG)]
        cmat = sb("cmatsb", [P, 3, P], F32)
        bdm = sb("bdm", [P, P], BF16)
        identb = sb("identb", [P, P], BF16)
        onesb = sb("onesb", [P, P], BF16)
        stgb = [stg[i][:, :, :].bitcast(BF16) for i in range(NSTG)]

        def dgv(d):
            return stgb[d // 32][:, (d % 32) // 4, (d % 4) * 128:(d % 4) * 128 + 128]

        def dgres(d):
            return R('stg', d // 32, (d % 32) // 4)
        par = sb("parsb", [P, DEPTH, NPAR], F32)
        pal = sb("pal", [P, DEPTH, 32], F32)
        cs = sb("cssb", [P, KT, 2], F32)
        epsc = sb("epsc", [P, 2], F32)
        sbf = sb("sbf", [P, KT, 2], BF16)
        modT = [sb(f"modT{i}", [P, 48, 2], F32) for i in range(2)]
        onesc = [sb(f"onesc{i}", [P, 2, KT, 2], F32) for i in range(2)]
        hA = [sb(f"hA{i}", [P, 2, KT, 2], F32) for i in range(2)]
        hB = [sb(f"hB{i}", [P, 2, KT, 2], F32) for i in range(2)]
        NTMP = 6
        tmp = [sb(f"tmp{i}", [P, 512], F32) for i in range(NTMP)]
        sqb = [sb(f"sqb{i}", [P, 512], BF16) for i in range(2)]
        NPT = 2
        PT = [sb(f"pt{i}", [P, 2, 512], BF16) for i in range(NPT)]
        convy = sb("convy", [P, 4, 512], F32)
        RR = [sb(f"rr{i}", [P, 512], F32) for i in range(2)]
        ps = stack.enter_context(nc.psum_tensor("ps", [P, 8, 512], F32))
        lane_const = S.dmalane("const")
        lane_tab = S.dmalane("tab")
        lane_bdm = S.dmalane("bdm")
        lane_rr = [S.dmalane("rr0"), S.dmalane("rr1")]
        lane_wo = S.dmalane("wo")
        woA = convy[:, :, :].bitcast(BF16).rearrange("p a (b c) -> p (a b) c", b=2)
        woB = [PT[0], PT[1], RR[0][:, :].bitcast(BF16).rearrange("p (a c) -> p a c", a=2),
               RR[1][:, :].bitcast(BF16).rearrange("p (a c) -> p a c", a=2)]
        lane_w2 = S.dmalane("w2")
        lane_dbg = S.dmalane("dbg")

        cnt = {'tmp': 0, 'sqb': 0, 'pt': 0, 'ws': 0, 'stg': 0, 'bank': 0, 'dg': 0, 'cb': 0}

        def nxt(kind, n):
            i = cnt[kind] % n
            cnt[kind] += 1
            return i

        def PB(b, n):
            return ps[:, b, 0:n]

        def ld_const(e):
            return [e.dma_start(out=cmat[:, :, :], in_=cm_d[:, 0:3 * P].rearrange("p (a c) -> p a c", a=3)),
                    e.dma_start(out=par[:, :, :], in_=par_d.rearrange("p (l c) -> p l c", l=DEPTH)),
                    e.dma_start(out=cs[:, :, :], in_=cs_d.rearrange("p (k c) -> p k c", k=KT))]
        S.op('sp', ld_const, writes=[R('const')], lane=lane_const, ndma=3)
        S.op('pool', lambda e: [e.dma_start(out=bdm[:, :], in_=cm_d[:, 3 * P:4 * P]),
                                e.dma_start(out=identb[:, :], in_=cm_d[:, 4 * P:5 * P])],
             writes=[R('bdm')], lane=lane_bdm, ndma=2)
        S.op('act', lambda e: e.activation(out=sbf[:, :, :], in_=cs[:, :, :], func=AF.Silu),
             reads=[R('const')], writes=[R('sbf')])
        S.op('dve', lambda e: e.tensor_scalar(out=pal[:, :, :], in0=par[:, :, 48:80], scalar1=ALPHA,
                                              scalar2=None, op0=ALU.mult),
             reads=[R('const')], writes=[R('pal')])
        S.op('dve', lambda e: e.memset(onesb[:, :], 1.0), writes=[R('onesb')])
        S.op('dve', lambda e: e.memset(epsc[:, 0:1], EPS), writes=[R('epsc')])
        S.op('dve', lambda e: e.memset(epsc[:, 1:2], 64 * EPS), writes=[R('epsc')])
        S.op('dve', lambda e: e.memset(RR[0][:, :], 0.0), writes=[R('rr', 0)])
        S.op('dve', lambda e: e.memset(RR[1][:, :], 0.0), writes=[R('rr', 1)])

        def pc(l, off, n=1):
            return par[:, l, off:off + n]
        O_BADA, O_LN1G, O_LN1B, O_LN2G, O_LN2B, O_DWW, O_DWB, O_CLG, O_CLB, O_QG, O_KG = \
            0, 48, 56, 64, 72, 80, 204, 208, 212, 216, 217

        def wload(dmas, nd):
            si = nxt('ws', NSLOT)
            pairs = dmas(wsl[si])

            def fn(e):
                return [e.dma_start(out=o, in_=i) for (o, i) in pairs]
            S.op('pool', fn, writes=[R('ws', si)], lane=wsl_lane[si], ndma=len(pairs))
            return si

        def wcols(wd, l, c0, w):
            return wd[l, :, c0:c0 + w].rearrange("(k p) c -> p k c", p=P)

        def mods_gen(l):
            m = modT[l % 2]
            mres = R('mod', l % 2)
            bank = 7
            for blk in range(12):
                si = wload(lambda slot: [(slot[:, :, :], wcols(wada_d, l, blk * 512, 512))], 1)

                def fn(e):
                    last_ = None
                    for f4 in range(4):
                        for k in range(KT):
                            last_ = e.matmul(ps[:, bank, 2 * f4:2 * f4 + 2], wsl[si][:, k, f4 * 128:(f4 + 1) * 128],
                                             sbf[:, k, :], start=(k == 0), stop=(k == KT - 1))
                    return last_
                S.op('pe', fn, reads=[R('ws', si), R('sbf')], writes=[R('ps', bank)])
                for col in range(2):
                    S.op('dve', lambda e: e.tensor_tensor(
                        out=m[:, blk * 4:blk * 4 + 4, col],
                        in0=ps[:, bank, 0:8].rearrange("p (f c) -> p f c", c=2)[:, :, col],
                        in1=par[:, l, blk * 4:blk * 4 + 4], op=ALU.add),
                        reads=[R('ps', bank), R('const')], writes=[mres])
                yield
            osc = onesc[l % 2]
            for j, g in enumerate((1, 4)):
                S.op('dve', lambda e: e.tensor_scalar(
                    out=osc[:, j, :, :], in0=m[:, g * 8:(g + 1) * 8, :], scalar1=1.0, scalar2=None, op0=ALU.add),
                    reads=[mres], writes=[R('osc', l % 2)])

        def emit_mods(l):
            for _ in mods_gen(l):
                pass

        def emit_hcoef(l, which, goff, boff, lprev):
            m = modT[l % 2]
            osc = onesc[l % 2]
            A = hA[l % 2]
            B = hB[l % 2]
            shg = 0 if which == 0 else 3
            rd = [R('mod', l % 2), R('osc', l % 2), R('const')]
            wr = [R('hc', l % 2, which)]
            for col in range(2):
                if lprev is None:
                    S.op('dve', lambda e, col=col: e.tensor_copy(out=A[:, which, :, col], in_=osc[:, which, :, col]),
                         reads=rd, writes=wr)
                    S.op('dve', lambda e, col=col: e.tensor_copy(out=B[:, which, :, col],
                                                                  in_=m[:, shg * 8:shg * 8 + 8, col]),
                         reads=rd, writes=wr)
                else:
                    S.op('dve', lambda e, col=col: e.tensor_tensor(
                        out=A[:, which, :, col], in0=osc[:, which, :, col], in1=par[:, lprev, goff:goff + 8],
                        op=ALU.mult), reads=rd, writes=wr)
                    S.op('dve', lambda e, col=col: e.tensor_tensor(
                        out=B[:, which, :, col], in0=osc[:, which, :, col], in1=par[:, lprev, boff:boff + 8],
                        op=ALU.mult), reads=rd, writes=wr)
                    S.op('dve', lambda e, col=col: e.tensor_tensor(
                        out=B[:, which, :, col], in0=B[:, which, :, col], in1=m[:, shg * 8:shg * 8 + 8, col],
                        op=ALU.add), reads=rd + wr, writes=wr)

        def layernorm(nt, vt, vres, n, nfeat, outfn, banks=(6, 7)):
            b1, b2 = banks
            ones = cmat[:, 0, :]
            def f1(e):
                last = None
                for k in range(nt):
                    last = e.matmul(PB(b1, n), ones, vt(k), start=(k == 0), stop=(k == nt - 1))
                return last
            S.op('pe', f1, reads=[vres(k) for k in range(nt)] + [R('const')], writes=[R('ps', b1)])
            for k in range(nt):
                ti = nxt('tmp', NTMP)
                S.op('act', lambda e, k=k, ti=ti: e.activation(out=tmp[ti][:, 0:n], in_=vt(k), func=AF.Square),
                     reads=[vres(k)], writes=[R('tmp', ti)])
                S.op('pe', lambda e, k=k, ti=ti: e.matmul(PB(b2, n), ones, tmp[ti][:, 0:n], start=(k == 0),
                                                          stop=(k == nt - 1)),
                     reads=[R('tmp', ti), R('const')], writes=[R('ps', b2)])
            tm = nxt('tmp', NTMP)
            tr = nxt('tmp', NTMP)
            inv = 1.0 / nfeat
            S.op('dve', lambda e: e.tensor_scalar(out=tmp[tm][:, 0:n], in0=PB(b1, n), scalar1=inv, scalar2=None,
                                                  op0=ALU.mult),
                 reads=[R('ps', b1)], writes=[R('tmp', tm)])
            S.op('dve', lambda e: e.tensor_tensor(out=tmp[tr][:, 0:n], in0=tmp[tm][:, 0:n], in1=tmp[tm][:, 0:n],
                                                  op=ALU.mult),
                 reads=[R('tmp', tm)], writes=[R('tmp', tr)])
            S.op('dve', lambda e: e.scalar_tensor_tensor(out=tmp[tr][:, 0:n], in0=PB(b2, n), scalar=inv,
                                                         in1=tmp[tr][:, 0:n], op0=ALU.mult, op1=ALU.subtract),
                 reads=[R('ps', b2), R('tmp', tr)], writes=[R('tmp', tr)])
            S.op('act', lambda e: e.activation(out=tmp[tr][:, 0:n], in_=tmp[tr][:, 0:n], func=AF.Ln, bias=epsc[:, 0:1]),
                 reads=[R('tmp', tr), R('epsc')], writes=[R('tmp', tr)])
            S.op('act', lambda e: e.activation(out=tmp[tr][:, 0:n], in_=tmp[tr][:, 0:n], func=AF.Exp, scale=-0.5),
                 reads=[R('tmp', tr)], writes=[R('tmp', tr)])
            for k in range(nt):
                S.op('dve', lambda e, k=k: e.tensor_tensor(out=vt(k), in0=vt(k), in1=tmp[tm][:, 0:n], op=ALU.subtract),
                     reads=[vres(k), R('tmp', tm)], writes=[vres(k)])
                S.op('dve', lambda e, k=k: e.tensor_tensor(out=vt(k), in0=vt(k), in1=tmp[tr][:, 0:n], op=ALU.mult),
                     reads=[vres(k), R('tmp', tr)], writes=[vres(k)])
                outfn(k)

        dbg_off = [0]

        def tap(ap_f32, n, reads, parts=P):
            if dbg_d is None:
                return
            o = dbg_off[0]
            dbg_off[0] += n
            S.op('pool', lambda e: [e.dma_start(out=dbg_d[0:parts, o:o + n], in_=ap_f32)], reads=reads,
                 lane=lane_dbg, ndma=1)

        def stage_in(src_d, s, n):
            si = nxt('stg', NSTG)
            S.op('sp', lambda e: [e.dma_start(out=stg[si][:, :, 0:n],
                                              in_=src_d[:, s:s + n].rearrange("(k p) t -> p k t", p=P))],
                 reads=[R('xa', u) for u in units(s, n)] if src_d is xa_d else [],
                 writes=[R('stg', si, k) for k in range(KT)], lane=stg_in[si], ndma=1)
            return si

        def stage_out(dst_d, si, s_dst, n, s_tok):
            S.op('act', lambda e: [e.dma_start(out=dst_d[:, s_dst:s_dst + n].rearrange("(k p) t -> p k t", p=P),
                                              in_=stg[si][:, :, 0:n])],
                 reads=[R('stg', si, k) for k in range(KT)],
                 writes=[R('xa', u) for u in units(s_tok, n)] if dst_d is xa_d else [R('outd')],
                 lane=stg_out[si], ndma=1)

        def hres(k, s, n):
            return [R('h', k, u) for u in units(s, n)]

        def ln_outputs(si, s, n, l_next_slot, which, ga_ap, ba_ap, col, final=False, g_ap=None, b_ap=None,
                       skip_h=False):
            def outfn(k):
                t_ap = stg[si][:, k, 0:n]
                tres = R('stg', si, k)
                if final:
                    S.op('act', lambda e: e.activation(out=t_ap, in_=t_ap, func=AF.Identity,
                                                       scale=g_ap(k), bias=b_ap(k)),
                         reads=[tres, R('const')], writes=[tres])
                    return
                A = hA[l_next_slot]
                B = hB[l_next_slot]
                if not skip_h:
                    S.op('act', lambda e: e.activation(out=hT[:, k, s:s + n], in_=t_ap, func=AF.Identity,
                                                       scale=A[:, which, k, col:col + 1],
                                                       bias=B[:, which, k, col:col + 1]),
                         reads=[tres, R('hc', l_next_slot, which)], writes=hres(k, s, n))
                S.op('act', lambda e: e.activation(out=t_ap, in_=t_ap, func=AF.Identity,
                                                   scale=ga_ap(k), bias=ba_ap(k)),
                     reads=[tres, R('pal'), R('const')], writes=[tres])
            return outfn

        emit_mods(0)
        emit_hcoef(0, 0, None, None, None)
        for ci, (s, n) in enumerate(LNCH):
            si = stage_in(xT_d, s, n)
            col = 1 if ci == 0 else 0
            for k in range(KT):
                t_ap = stg[si][:, k, 0:n]
                tres = R('stg', si, k)
                S.op('act', lambda e, t_ap=t_ap, k=k, col=col, s=s, n=n: e.activation(
                    out=hT[:, k, s:s + n], in_=t_ap, func=AF.Identity,
                    scale=hA[0][:, 0, k, col:col + 1], bias=hB[0][:, 0, k, col:col + 1]),
                    reads=[tres, R('hc', 0, 0)], writes=hres(k, s, n))
                S.op('dve', lambda e, t_ap=t_ap: e.tensor_scalar(out=t_ap, in0=t_ap, scalar1=ALPHA, scalar2=None,
                                                                 op0=ALU.mult),
                     reads=[tres], writes=[tres])
            stage_out(xa_d, si, s, n, s)

        for l in range(nlayers):
            last = (l == DEPTH - 1)
            sl = l % 2
            chunks = [c for ci, c in enumerate(CHUNKS) if not (last and ci == 0)]
            if l > 0:
                S.barrier()
            S.op('pool', lambda e: [e.dma_start(out=tabC, in_=tab_d[:, 0:T]),
                                    e.dma_start(out=tabS, in_=tab_d[:, T:2 * T])],
                 writes=[R('tab')], lane=lane_tab, ndma=2)
            S.op('dve', lambda e: e.memset(abuf[:, :, :], 0.0), writes=[R('apad')])
            S.op('dve', lambda e: e.memset(Kz[:, :, :], 0.0), writes=[R('k', h_, u) for h_ in range(2) for u in range(9)])
            S.op('dve', lambda e: e.memset(Vaug[:, :, :, 0:64], 1.0), writes=[R('v', tt) for tt in range(18)])
            S.op('dve', lambda e: e.memset(Vaug[:, :, :, 128:192], 1.0), writes=[R('v', tt) for tt in range(18)])

            dg_next = [0]

            def emit_diags(k_):
                for _ in range(k_):
                    d_ = dg_next[0]
                    if d_ >= 124:
                        return
                    dg_next[0] += 1
                    if d_ % 2 == 0:
                        S.op('dve', lambda e: e.tensor_scalar(out=dgv(d_), in0=identb[:, :], scalar1=pc(l, O_DWW + d_),
                                                              scalar2=None, op0=ALU.mult),
                             reads=[R('bdm'), R('const')], writes=[dgres(d_)])
                    else:
                        S.op('act', lambda e: e.activation(out=dgv(d_), in_=identb[:, :], func=AF.Identity,
                                                           scale=pc(l, O_DWW + d_)),
                             reads=[R('bdm'), R('const')], writes=[dgres(d_)])
            rp = [(tmp[i_], R('tmp', i_)) for i_ in range(6)] + \
                 [(convy[:, j_, :], R('cy', j_)) for j_ in range(3)]
            inst_no = [0]
            stageA = []
            stageB = []

            def nr_1b(it):
                bank, s, n, gcol, qi = it['bank'], it['s'], it['n'], it['gcol'], it['qi']
                (T0, r0), (T1, r1), _ = it['tiles']
                b2 = 4 + nxt('bank', 4)
                S.op('pe', lambda e: e.matmul(PB(b2, n), bdm[:, :], sqb[qi][:, 0:n], start=True, stop=True),
                     reads=[R('sqb', qi), R('bdm')], writes=[R('ps', b2)])
                S.op('act', lambda e: e.activation(out=T0[:, 0:n], in_=PB(b2, n), func=AF.Ln, bias=epsc[:, 1:2]),
                     reads=[R('ps', b2), R('epsc')], writes=[r0])
                S.op('act', lambda e: e.activation(out=T0[:, 0:n], in_=T0[:, 0:n], func=AF.Exp, scale=-0.5),
                     reads=[r0], writes=[r0])
                S.op('dve', lambda e: e.scalar_tensor_tensor(out=T1[:, 0:n], in0=PB(bank, n), scalar=gcol,
                                                             in1=T0[:, 0:n], op0=ALU.mult, op1=ALU.mult),
                     reads=[R('ps', bank), r0, R('const')], writes=[r1])

            def nr_2(it):
                s, n, dst_ap, dst_res = it['s'], it['n'], it['dst_ap'], it['dst_res']
                (T0, r0), (T1, r1), (T2, r2) = it['tiles']
                b3 = 4 + nxt('bank', 4)
                S.op('pe', lambda e: e.matmul(PB(b3, n), cmat[:, 2, :], T1[:, 0:n], start=True, stop=True),
                     reads=[r1, R('const')], writes=[R('ps', b3)])
                S.op('dve', lambda e: e.tensor_tensor(out=T0[:, 0:n], in0=T1[:, 0:n], in1=tabC[:, s:s + n], op=ALU.mult),
                     reads=[r1, R('tab')], writes=[r0])
                S.op('dve', lambda e: e.tensor_tensor(out=T2[:, 0:n], in0=PB(b3, n), in1=tabS[:, s:s + n], op=ALU.mult),
                     reads=[R('ps', b3), R('tab')], writes=[r2])
                if isinstance(dst_ap, tuple):
                    for hp, d_ap in enumerate(dst_ap):
                        rows = slice(64 * hp, 64 * hp + 64)
                        S.op('dve', lambda e: e.tensor_tensor(out=d_ap, in0=T0[rows, 0:n], in1=T2[rows, 0:n], op=ALU.add),
                             reads=[r0, r2], writes=dst_res)
                else:
                    S.op('dve', lambda e: e.tensor_tensor(out=dst_ap, in0=T0[:, 0:n], in1=T2[:, 0:n], op=ALU.add),
                         reads=[r0, r2], writes=dst_res)

            def nr_submit(bank, s, n, gcol, dst_ap, dst_res):
                k_ = inst_no[0]
                inst_no[0] += 1
                qi = nxt('sqb', 2)
                S.op('act', lambda e: e.activation(out=sqb[qi][:, 0:n], in_=PB(bank, n), func=AF.Square),
                     reads=[R('ps', bank)], writes=[R('sqb', qi)])
                stageA.append(dict(bank=bank, s=s, n=n, gcol=gcol, qi=qi, dst_ap=dst_ap, dst_res=dst_res,
                                   tiles=[rp[(3 * k_ + j_) % 9] for j_ in range(3)]))
                if len(stageA) > 1:
                    it = stageA.pop(0)
                    nr_1b(it)
                    stageB.append(it)
                if len(stageB) > 1:
                    nr_2(stageB.pop(0))
                emit_diags(4)

            def flush_pend():
                while stageA or stageB:
                    if stageA:
                        it = stageA.pop(0)
                        nr_1b(it)
                        stageB.append(it)
                    if stageB and (len(stageB) > 1 or not stageA):
                        nr_2(stageB.pop(0))

            def proj(si, c0, s, n, bank):
                pairs = [(wsl[si][:, k, c0:c0 + 128], hT[:, k, s:s + n]) for k in range(KT)]

                def fn(e):
                    last_ = None
                    for i, (a_, b_) in enumerate(pairs):
                        last_ = e.matmul(PB(bank, n), a_, b_, start=(i == 0), stop=(i == KT - 1))
                    return last_
                S.op('pe', fn, reads=[R('ws', si)] + [r for k in range(KT) for r in hres(k, s, n)],
                     writes=[R('ps', bank)])

            def dm_A(slot):
                prs = []
                for h in range(2):
                    for d2 in range(2):
                        prs.append((slot[:, :, (2 * h + d2) * 64:(2 * h + d2 + 1) * 64], wcols(win_d, l, 512 + 64 * h, 64)))
                prs.append((slot[:, :, 256:384], wcols(win_d, l, 640, 128)))
                return prs
            si = wload(dm_A, 5)
            for h in range(2):
                for (s, n) in CHUNKS:
                    bank = nxt('bank', 4)
                    proj(si, 128 * h, s, n, bank)
                    nr_submit(bank, s, n, pc(l, O_KG), (Kz[0:64, 2 * h, s:s + n], Kz[64:128, 2 * h + 1, s:s + n]),
                              [R('k', h, u) for u in units(s, n)])
            flush_pend()
            for tt in range(18):
                bank = nxt('bank', 4)
                pairs = [(hT[:, k, tt * 128:(tt + 1) * 128], wsl[si][:, k, 256:384]) for k in range(KT)]

                def fnv(e, pairs=pairs, bank=bank):
                    last_ = None
                    for i, (a_, b_) in enumerate(pairs):
                        last_ = e.matmul(PB(bank, 128), a_, b_, start=(i == 0), stop=(i == KT - 1))
                    return last_
                S.op('pe', fnv, reads=[R('ws', si)] + [r for k in range(KT) for r in hres(k, tt * 128, 128)],
                     writes=[R('ps', bank)])
                S.op('act', lambda e, tt=tt, bank=bank: e.activation(
                    out=Vaug[:, tt, :, 64:128], in_=PB(bank, 128).rearrange("p (h c) -> p h c", h=2), func=AF.Copy),
                    reads=[R('ps', bank)], writes=[R('v', tt)])
            for jb in range(2):
                def dm_B(slot, jb=jb):
                    prs = []
                    for jj in range(2):
                        j = 2 * jb + jj
                        prs.append((slot[:, :, jj * 256:jj * 256 + 128], wcols(win_d, l, 768 + 128 * j, 128)))
                        prs.append((slot[:, :, jj * 256 + 128:jj * 256 + 256], wcols(win_d, l, 1280 + 128 * j, 128)))
                    return prs
                si = wload(dm_B, 4)
                for jj in range(2):
                    j = 2 * jb + jj
                    for (s, n) in chunks:
                        bu = nxt('bank', 4)
                        bg = nxt('bank', 4)
                        proj(si, jj * 256, s, n, bu)
                        proj(si, jj * 256 + 128, s, n, bg)
                        ti = nxt('tmp', NTMP)
                        S.op('act', lambda e, bg=bg, ti=ti, n=n: e.activation(out=tmp[ti][:, 0:n], in_=PB(bg, n),
                                                                                func=AF.Sigmoid),
                             reads=[R('ps', bg)], writes=[R('tmp', ti)])
                        a0 = aidx(s)
                        S.op('dve', lambda e, bu=bu, ti=ti, n=n, j=j, a0=a0: e.tensor_tensor(
                            out=abuf[:, j, a0:a0 + n], in0=PB(bu, n), in1=tmp[ti][:, 0:n], op=ALU.mult),
                            reads=[R('ps', bu), R('tmp', ti), R('apad')], writes=[R('a', j, u) for u in units(s, n)])
            si = wload(lambda slot: [(slot[:, :, :], wcols(win_d, l, 0, 512))], 1)
            for i in range(4):
                for (s, n) in chunks:
                    bank = nxt('bank', 4)
                    proj(si, 128 * i, s, n, bank)
                    nr_submit(bank, s, n, pc(l, O_QG), qT[:, i, s:s + n], [R('q', i, u) for u in units(s, n)])
            flush_pend()

            if l == 0:
                allh = [R('h', k, u) for k in range(KT) for u in range(9)]
                tap(hT[:, 0, :], T, allh)
                tap(Kz[:, 0, :], T, [R('k', 0, u) for u in range(9)])
                tap(Kz[:, 1, :], T, [R('k', 0, u) for u in range(9)])
                tap(qT[:, 0, :], T, [R('q', 0, u) for u in range(9)])
                tap(Vaug[:, :, 0, 64:128], 18 * 64, [R('v', tt) for tt in range(18)])
                tap(abuf[:, 0, :], APAD, [R('a', 0, u) for u in range(9)] + [R('apad')])

            emit_diags(124)
            def conv_gen():
                for ci, (s, n) in enumerate(CHUNKS):
                    if last and ci == 0:
                        continue
                    seg0, seg1 = (0, 256) if s < 256 else (256, T)
                    rd_units = units(max(seg0, s - 15), min(seg1, s + n + 15) - max(seg0, s - 15))
                    base = aidx(s) - 15
                    for j in range(4):
                        rd = [R('a', j, u) for u in rd_units] + [R('apad')]
                        cb = 6
                        for k0 in range(0, 31, 3):
                            ks = list(range(k0, min(31, k0 + 3)))

                            def fnc(e):
                                last_ = None
                                for k in ks:
                                    last_ = e.matmul(PB(cb, n), dgv(j * 31 + k), abuf[:, j, base + k:base + k + n],
                                                     start=(k == 0), stop=(k == 30))
                                return last_
                            S.op('pe', fnc, reads=rd + [dgres(j * 31 + k) for k in ks], writes=[R('ps', cb)])
                            yield
                        S.op('act', lambda e: e.activation(out=convy[:, j, 0:n], in_=PB(cb, n), func=AF.Identity,
                                                           bias=pc(l, O_DWB + j)),
                             reads=[R('ps', cb), R('const')], writes=[R('cy', j)])

                    def outfn(j, s=s, n=n):
                        S.op('act', lambda e: e.activation(out=hT[:, 4 + j, s:s + n], in_=convy[:, j, 0:n], func=AF.Silu,
                                                           scale=pc(l, O_CLG + j), bias=pc(l, O_CLB + j)),
                             reads=[R('cy', j), R('const')], writes=hres(4 + j, s, n))
                    hold['ln'] = True
                    ones = cmat[:, 0, :]

                    def f1c(e):
                        last_ = None
                        for k in range(4):
                            last_ = e.matmul(PB(7, n), ones, convy[:, k, 0:n], start=(k == 0), stop=(k == 3))
                        return last_
                    S.op('pe', f1c, reads=[R('cy', k) for k in range(4)] + [R('const')], writes=[R('ps', 7)])
                    yield
                    for k in range(4):
                        ti = nxt('sqb', 2)
                        S.op('act', lambda e: e.activation(out=sqb[ti][:, 0:n], in_=convy[:, k, 0:n], func=AF.Square),
                             reads=[R('cy', k)], writes=[R('sqb', ti)])
                        S.op('pe', lambda e: e.matmul(PB(6, n), onesb[:, :], sqb[ti][:, 0:n], start=(k == 0), stop=(k == 3)),
                             reads=[R('sqb', ti), R('onesb')], writes=[R('ps', 6)])
                        yield
                    tm, tr = 2, 3
                    inv = 1.0 / 512.0
                    S.op('dve', lambda e: e.tensor_scalar(out=tmp[tm][:, 0:n], in0=PB(7, n), scalar1=inv, scalar2=None,
                                                          op0=ALU.mult),
                         reads=[R('ps', 7)], writes=[R('tmp', tm)])
                    S.op('dve', lambda e: e.tensor_tensor(out=tmp[tr][:, 0:n], in0=tmp[tm][:, 0:n], in1=tmp[tm][:, 0:n],
                                                          op=ALU.mult),
                         reads=[R('tmp', tm)], writes=[R('tmp', tr)])
                    S.op('dve', lambda e: e.scalar_tensor_tensor(out=tmp[tr][:, 0:n], in0=PB(6, n), scalar=inv,
                                                                 in1=tmp[tr][:, 0:n], op0=ALU.mult, op1=ALU.subtract),
                         reads=[R('ps', 6), R('tmp', tr)], writes=[R('tmp', tr)])
                    hold['ln'] = False
                    yield
                    S.op('act', lambda e: e.activation(out=tmp[tr][:, 0:n], in_=tmp[tr][:, 0:n], func=AF.Ln,
                                                       bias=epsc[:, 0:1]),
                         reads=[R('tmp', tr), R('epsc')], writes=[R('tmp', tr)])
                    S.op('act', lambda e: e.activation(out=tmp[tr][:, 0:n], in_=tmp[tr][:, 0:n], func=AF.Exp, scale=-0.5),
                         reads=[R('tmp', tr)], writes=[R('tmp', tr)])
                    yield
                    for k in range(4):
                        S.op('dve', lambda e: e.tensor_tensor(out=convy[:, k, 0:n], in0=convy[:, k, 0:n],
                                                              in1=tmp[tm][:, 0:n], op=ALU.subtract),
                             reads=[R('cy', k), R('tmp', tm)], writes=[R('cy', k)])
                        S.op('dve', lambda e: e.tensor_tensor(out=convy[:, k, 0:n], in0=convy[:, k, 0:n],
                                                              in1=tmp[tr][:, 0:n], op=ALU.mult),
                             reads=[R('cy', k), R('tmp', tr)], writes=[R('cy', k)])
                        yield
                    for k in range(4):
                        outfn(k)
                    yield
            def mods_filler():
                if l + 1 < nlayers:
                    for _ in mods_gen(l + 1):
                        yield
                    emit_hcoef(l + 1, 0, O_LN2G, O_LN2B, l)
                emit_hcoef(l, 1, O_LN1G, O_LN1B, l)
                yield
            hold = {'ln': False}
            fillers = [conv_gen(), mods_filler()]
            fstate = {'i': 0}

            def conv_step(k=1):
                for _ in range(k):
                    if not fillers:
                        return
                    fstate['i'] = (fstate['i'] + 1) % len(fillers)
                    if hold['ln']:
                        fstate['i'] = 0
                    g_ = fillers[fstate['i']]
                    try:
                        next(g_)
                    except StopIteration:
                        fillers.remove(g_)

            for qi_, (s, n) in enumerate(QCH):
                if last and qi_ == 0:
                    continue
                ktiles = [0, 1] if qi_ == 0 else list(range(18))
                for i in range(4):
                    h = i // 2
                    stages = [(kp, p) for kp in range(len(ktiles) // 2) for p in range(2)]
                    oacc = [4, 5]
                    sc_state = {}

                    def emit_scores(idx, stg_):
                        kp, p = stg_
                        bpair = 2 * (idx % 2)
                        kts = [ktiles[2 * kp], ktiles[2 * kp + 1]]

                        def fs(e):
                            last_ = None
                            for jj in range(2):
                                last_ = e.matmul(PB(bpair + jj, n), Kz[:, 2 * h + p, kts[jj] * 128:(kts[jj] + 1) * 128],
                                                 qT[:, i, s:s + n], start=True, stop=True)
                            return last_
                        S.op('pe', fs, reads=[R('k', h, kt_ // 2) for kt_ in kts] + [R('q', i, u) for u in units(s, n)],
                             writes=[R('ps', bpair), R('ps', bpair + 1)])
                        pi = nxt('pt', NPT)
                        S.op('act', lambda e: e.activation(
                            out=PT[pi][:, :, 0:n], in_=ps[:, bpair:bpair + 2, 0:n], func=AF.Exp, scale=8.0),
                            reads=[R('ps', bpair), R('ps', bpair + 1)], writes=[R('pt', pi)])
                        sc_state[idx] = pi

                    def emit_pv(idx, stg_):
                        kp, p = stg_
                        pi = sc_state[idx]
                        c0 = 64 if p == 0 else 0
                        kts = [ktiles[2 * kp], ktiles[2 * kp + 1]]
                        nkp = len(ktiles) // 2

                        def fp(e):
                            last_ = None
                            for jj in range(2):
                                last_ = e.matmul(PB(oacc[p], n), Vaug[:, kts[jj], h, c0:c0 + 128], PT[pi][:, jj, 0:n],
                                                 start=(kp == 0 and jj == 0), stop=(kp == nkp - 1 and jj == 1))
                            return last_
                        S.op('pe', fp, reads=[R('v', kt_) for kt_ in kts] + [R('pt', pi)], writes=[R('ps', oacc[p])])
                    for idx, stg_ in enumerate(stages):
                        emit_scores(idx, stg_)
                        if idx > 0:
                            emit_pv(idx - 1, stages[idx - 1])
                        conv_step(1)
                    emit_pv(len(stages) - 1, stages[-1])
                    for p in range(2):
                        srow = slice(64, 128) if p == 0 else slice(0, 64)
                        orow = slice(0, 64) if p == 0 else slice(64, 128)
                        ob = oacc[p]
                        ti = 4 + p
                        S.op('act', lambda e: e.activation(out=tmp[ti][:, 0:n], in_=ps[:, ob, 0:n], func=AF.Copy),
                             reads=[R('ps', ob)], writes=[R('tmp', ti)])
                        S.op('dve', lambda e: e.reciprocal(out=RR[p][srow, 0:n], in_=tmp[ti][srow, 0:n]),
                             reads=[R('tmp', ti)], writes=[R('rr', p)])
                        S.op('sp', lambda e: [e.dma_start(out=RR[p][orow, 0:n], in_=RR[p][srow, 0:n])],
                             reads=[R('rr', p)], writes=[R('rrm', p)], lane=lane_rr[p], ndma=1)
                        S.op('dve', lambda e: e.tensor_tensor(
                            out=hT[orow, i, s:s + n], in0=tmp[ti][orow, 0:n], in1=RR[p][orow, 0:n], op=ALU.mult),
                            reads=[R('tmp', ti), R('rrm', p)], writes=hres(i, s, n))
            conv_step(10 ** 6)

            if l == 0:
                allh = [R('h', k, u) for k in range(KT) for u in range(9)]
                tap(hT[:, 0, :], T, allh)
                tap(hT[:, 4, :], T, allh)
            S.op('pool', lambda e: [e.dma_start(out=woA[:, :, :], in_=wcols(wout_d, l, 0, 512))] +
                 [e.dma_start(out=woB[q_][:, :, :], in_=wout_d[l, 2 * q_ * P:(2 * q_ + 2) * P, 512:1024]
                              .rearrange("(k p) c -> p k c", p=P)) for q_ in range(4)],
                 writes=[R('cy', j_) for j_ in range(4)] + [R('pt', 0), R('pt', 1), R('rr', 0), R('rr', 1),
                                                            R('rrm', 0), R('rrm', 1), R('wo')],
                 lane=lane_wo, ndma=5)
            S.barrier()
            S.op('pool', lambda e: [e.dma_start(out=w2s[:, 0:11, :],
                                                in_=wf2_d[l, 0:11 * P, :].rearrange("(f p) c -> p f c", p=P)),
                                    e.dma_start(out=w2s[:, 11:22, :],
                                                in_=wf2_d[l, 11 * P:22 * P, :].rearrange("(f p) c -> p f c", p=P))],
                 writes=[R('w2')], lane=lane_w2, ndma=2)
            ones = cmat[:, 0, :]
            sbk = [0]

            def ln_job(s, n, col, mm_fn, gate_off, outfn_maker, out_dst):
                st = {}

                def A():
                    si = stage_in(xa_d, s, n)
                    st['si'] = si
                    for ot in range(KT):
                        bank = nxt('bank', 4)
                        S.op('pe', lambda e: mm_fn(e, ot, bank), reads=st_reads(ot), writes=[R('ps', bank)])
                        S.op('dve', lambda e: e.scalar_tensor_tensor(
                            out=stg[si][:, ot, 0:n], in0=PB(bank, n), scalar=modT[sl][:, gate_off + ot, col:col + 1],
                            in1=stg[si][:, ot, 0:n], op0=ALU.mult, op1=ALU.add),
                            reads=[R('ps', bank), R('mod', sl), R('stg', si, ot)], writes=[R('stg', si, ot)])
                st_reads = mm_fn.reads

                def B1():
                    si = st['si']
                    b1, b2 = [(6, 7), (4, 5)][sbk[0] % 2]
                    sbk[0] += 1
                    st['banks'] = (b1, b2)

                    def f1(e):
                        last_ = None
                        for k in range(KT):
                            last_ = e.matmul(PB(b1, n), ones, stg[si][:, k, 0:n], start=(k == 0), stop=(k == KT - 1))
                        return last_
                    S.op('pe', f1, reads=[R('stg', si, k) for k in range(KT)] + [R('const')], writes=[R('ps', b1)])
                    for k in range(KT):
                        ti = nxt('sqb', 2)
                        S.op('act', lambda e: e.activation(out=sqb[ti][:, 0:n], in_=stg[si][:, k, 0:n], func=AF.Square),
                             reads=[R('stg', si, k)], writes=[R('sqb', ti)])
                        S.op('pe', lambda e: e.matmul(PB(b2, n), onesb[:, :], sqb[ti][:, 0:n], start=(k == 0),
                                                      stop=(k == KT - 1)),
                             reads=[R('sqb', ti), R('onesb')], writes=[R('ps', b2)])

                def B2():
                    si = st['si']
                    b1, b2 = st['banks']
                    tm = nxt('tmp', NTMP)
                    tr = nxt('tmp', NTMP)
                    inv = 1.0 / 1024.0
                    S.op('dve', lambda e: e.tensor_scalar(out=tmp[tm][:, 0:n], in0=PB(b1, n), scalar1=inv, scalar2=None,
                                                          op0=ALU.mult),
                         reads=[R('ps', b1)], writes=[R('tmp', tm)])
                    S.op('dve', lambda e: e.tensor_tensor(out=tmp[tr][:, 0:n], in0=tmp[tm][:, 0:n], in1=tmp[tm][:, 0:n],
                                                          op=ALU.mult),
                         reads=[R('tmp', tm)], writes=[R('tmp', tr)])
                    S.op('dve', lambda e: e.scalar_tensor_tensor(out=tmp[tr][:, 0:n], in0=PB(b2, n), scalar=inv,
                                                                 in1=tmp[tr][:, 0:n], op0=ALU.mult, op1=ALU.subtract),
                         reads=[R('ps', b2), R('tmp', tr)], writes=[R('tmp', tr)])
                    S.op('act', lambda e: e.activation(out=tmp[tr][:, 0:n], in_=tmp[tr][:, 0:n], func=AF.Ln,
                                                       bias=epsc[:, 0:1]),
                         reads=[R('tmp', tr), R('epsc')], writes=[R('tmp', tr)])
                    S.op('act', lambda e: e.activation(out=tmp[tr][:, 0:n], in_=tmp[tr][:, 0:n], func=AF.Exp, scale=-0.5),
                         reads=[R('tmp', tr)], writes=[R('tmp', tr)])
                    outfn = outfn_maker(si)
                    for k in range(KT):
                        S.op('dve', lambda e: e.tensor_tensor(out=stg[si][:, k, 0:n], in0=stg[si][:, k, 0:n],
                                                              in1=tmp[tm][:, 0:n], op=ALU.subtract),
                             reads=[R('stg', si, k), R('tmp', tm)], writes=[R('stg', si, k)])
                        S.op('dve', lambda e: e.tensor_tensor(out=stg[si][:, k, 0:n], in0=stg[si][:, k, 0:n],
                                                              in1=tmp[tr][:, 0:n], op=ALU.mult),
                             reads=[R('stg', si, k), R('tmp', tr)], writes=[R('stg', si, k)])
                        outfn(k)
                    if out_dst is out_d:
                        stage_out(out_d, si, s - LC, n, s)
                    else:
                        stage_out(xa_d, si, s, n, s)
                return A, B1, B2

            def pipeline_steps(jobs):
                m = len(jobs)
                steps = []
                for i_ in range(m + 2):
                    fs = []
                    if i_ < m:
                        fs.append(jobs[i_][0])
                    if 0 <= i_ - 1 < m:
                        fs.append(jobs[i_ - 1][1])
                    if 0 <= i_ - 2 < m:
                        fs.append(jobs[i_ - 2][2])
                    steps.append(fs)
                return steps

            def run_step(fs):
                for f_ in fs:
                    f_()

            def mk_wout_mm(s, n):
                def mm(e, ot, bank):
                    c0 = (ot % 4) * 128
                    last_ = None
                    for k in range(KT):
                        w_ap = woA[:, k, c0:c0 + 128] if ot < 4 else woB[k // 2][:, k % 2, c0:c0 + 128]
                        last_ = e.matmul(PB(bank, n), w_ap, hT[:, k, s:s + n], start=(k == 0), stop=(k == KT - 1))
                    return last_
                mm.reads = lambda ot: [R('wo')] + [r for k in range(KT) for r in hres(k, s, n)]
                return mm
            ln1_jobs = []
            for ci, (s, n) in enumerate(LNCH):
                if last and ci == 0:
                    continue
                col = 1 if ci == 0 else 0
                ln1_jobs.append(ln_job(
                    s, n, col, mk_wout_mm(s, n), 16,
                    lambda si, s=s, n=n, col=col: ln_outputs(si, s, n, sl, 1, lambda k: pal[:, l, k:k + 1],
                                                             lambda k: pal[:, l, 8 + k:9 + k], col),
                    xa_d))
            lnq = pipeline_steps(ln1_jobs)
            n_pre = 5 if not last else 4
            for _ in range(n_pre):
                run_step(lnq.pop(0))

            for gi, grp in enumerate(GROUPS):
                gch = [(ci, CHUNKS[ci]) for ci in grp if not (last and ci == 0)]
                g0 = gch[0][1][0]
                g1 = gch[-1][1][0] + gch[-1][1][1]
                for blk in range(11):
                    def dm_F(slot, blk=blk):
                        return [(slot[:, :, 0:256], wcols(wf1_d, l, blk * 256, 256)),
                                (slot[:, :, 256:512], wcols(wf3_d, l, blk * 256, 256))]
                    si = wload(dm_F, 2)
                    for f2 in range(2):
                        ft = blk * 2 + f2
                        for ci, (s, n) in gch:
                            b1 = nxt('bank', 4)
                            b3 = nxt('bank', 4)
                            proj(si, f2 * 128, s, n, b1)
                            proj(si, 256 + f2 * 128, s, n, b3)
                            ti = nxt('tmp', NTMP)
                            S.op('act', lambda e: e.activation(out=tmp[ti][:, 0:n], in_=PB(b1, n), func=AF.Silu),
                                 reads=[R('ps', b1)], writes=[R('tmp', ti)])
                            lo = s - g0
                            S.op('dve', lambda e: e.tensor_tensor(
                                out=act[:, ft, lo:lo + n], in0=PB(b3, n), in1=tmp[ti][:, 0:n], op=ALU.mult),
                                reads=[R('ps', b3), R('tmp', ti)],
                                writes=[R('act', ft, lo // 256), R('act', ft, (lo + n - 1) // 256)])
                    if lnq and (gi > 0 or blk % 2 == 1 or len(lnq) > 11 - blk):
                        run_step(lnq.pop(0))
                while lnq:
                    run_step(lnq.pop(0))
                jobs = []
                for (s, n) in LNCH:
                    if s < g0 or s >= g1:
                        continue
                    col = 1 if s == 0 else 0
                    lo = s - g0

                    def mk_w2_mm(lo=lo, n=n):
                        def mm(e, ot, bank):
                            last_ = None
                            for f in range(FT):
                                last_ = e.matmul(PB(bank, n), w2s[:, f, ot * 128:(ot + 1) * 128], act[:, f, lo:lo + n],
                                                 start=(f == 0), stop=(f == FT - 1))
                            return last_
                        mm.reads = lambda ot: [R('w2')] + [R('act', f, lo // 256) for f in range(FT)]
                        return mm
                    if last:
                        om = lambda si, s=s, n=n, col=col: ln_outputs(
                            si, s, n, None, None, None, None, col, final=True,
                            g_ap=lambda k: par[:, l, O_LN2G + k:O_LN2G + k + 1],
                            b_ap=lambda k: par[:, l, O_LN2B + k:O_LN2B + k + 1])
                    else:
                        om = lambda si, s=s, n=n, col=col: ln_outputs(
                            si, s, n, (l + 1) % 2, 0, lambda k: pal[:, l, 16 + k:17 + k],
                            lambda k: pal[:, l, 24 + k:25 + k], col, skip_h=(l == nlayers - 1))
                    jobs.append(ln_job(s, n, col, mk_w2_mm(), 40, om, out_d if last else xa_d))
                for jb in jobs:
                    jb[0]()
                m_ = len(jobs)
                lnq = []
                for i_ in range(m_ + 1):
                    fs = []
                    if i_ < m_:
                        fs.append(jobs[i_][1])
                    if 0 <= i_ - 1 < m_:
                        fs.append(jobs[i_ - 1][2])
                    lnq.append(fs)
                if gi == len(GROUPS) - 1:
                    while lnq:
                        run_step(lnq.pop(0))
            if l == nlayers - 1 and not last:
                S.barrier()
                for (s, n) in LNCH[1:]:
                    si = stage_in(xa_d, s, n)
                    stage_out(out_d, si, s - LC, n, s)

        S.final_wait('sp', stg_out + [lane_dbg])

        with nc.Block() as block:
            @block.tensor
            def _(e):
                for f in S.progs['pe']:
                    f(e)

            @block.scalar
            def _(e):
                for f in S.progs['act']:
                    f(e)

            @block.vector
            def _(e):
                for f in S.progs['dve']:
                    f(e)

            @block.gpsimd
            def _(e):
                for f in S.progs['pool']:
                    f(e)

            @block.sync
            def _(e):
                for f in S.progs['sp']:
                    f(e)
    return nc


def _host_consts():
    ones = np.ones((P, P), np.float32)
    sw = np.zeros((P, P), np.float32)
    for i in range(P):
        sw[i, (i + 64) % P] = 1.0
    rot = np.zeros((P, P), np.float32)
    for blk in range(4):
        for f in range(16):
            a = blk * 32 + f
            rot[a + 16, a] = -1.0
            rot[a, a + 16] = 1.0
    bd = np.zeros((P, P), np.float32)
    bd[:64, :64] = 1.0
    bd[64:, 64:] = 1.0
    cmat = np.concatenate([ones, sw, rot, bd, np.eye(P, dtype=np.float32)], axis=1)
    tC = np.ones((P, T), np.float32)
    tS = np.zeros((P, T), np.float32)
    tok = np.arange(NL)
    pos = np.stack([tok // 64, tok % 64], 0).astype(np.float32)
    freqs = (10000.0 ** (-np.arange(16, dtype=np.float32) / 16.0)).astype(np.float32)
    for p_ in range(P):
        d = p_ % 64
        ang = (pos[d // 32] * freqs[d % 16]).astype(np.float32)
        tC[p_, LC:] = np.cos(ang)
        tS[p_, LC:] = np.sin(ang)
    tabs = np.concatenate([tC, tS], axis=1)
    return cmat, tabs


def _pack_params(b_ada, q_norm_g, k_norm_g, dw_w, dw_b, conv_ln_g, conv_ln_b, ln1_g, ln1_b, ln2_g, ln2_b):
    par = np.zeros((P, DEPTH, NPAR), np.float32)
    for l in range(DEPTH):
        par[:, l, 0:48] = b_ada[l].reshape(48, P).T
        par[:, l, 48:56] = ln1_g[l].reshape(8, P).T
        par[:, l, 56:64] = ln1_b[l].reshape(8, P).T
        par[:, l, 64:72] = ln2_g[l].reshape(8, P).T
        par[:, l, 72:80] = ln2_b[l].reshape(8, P).T
        w = dw_w[l, :, 0, :]
        for j in range(4):
            par[:, l, 80 + j * 31:80 + (j + 1) * 31] = w[:, j * P:(j + 1) * P].T
        par[:, l, 204:208] = dw_b[l].reshape(4, P).T
        par[:, l, 208:212] = conv_ln_g[l].reshape(4, P).T
        par[:, l, 212:216] = conv_ln_b[l].reshape(4, P).T
        par[:, l, 216] = np.tile(q_norm_g[l], 2)
        par[:, l, 217] = np.tile(k_norm_g[l], 2)
    return par.reshape(P, DEPTH * NPAR)


_NC_CACHE = {}


def kernel(x, c, ctx, c_ctx, w_ada, b_ada, w_in, q_norm_g, k_norm_g, dw_w, dw_b,
           conv_ln_g, conv_ln_b, w_out, ln1_g, ln1_b, w_ff1, w_ff3, w_ff2, ln2_g, ln2_b,
           _nlayers=DEPTH, _dbg=None):
    f = lambda a: np.ascontiguousarray(np.asarray(a, dtype=np.float32))
    x, c, ctx, c_ctx = f(x), f(c), f(ctx), f(c_ctx)
    cmat, tabs = _host_consts()
    par = _pack_params(f(b_ada), f(q_norm_g), f(k_norm_g), f(dw_w), f(dw_b), f(conv_ln_g), f(conv_ln_b),
                       f(ln1_g), f(ln1_b), f(ln2_g), f(ln2_b))
    key = (_nlayers, _dbg)
    if key not in _NC_CACHE:
        _NC_CACHE[key] = build(_nlayers, _dbg)
    nc = _NC_CACHE[key]
    shared = {"par": par, "cmat": cmat, "tabs": tabs, "w_ada": f(w_ada), "w_in": f(w_in), "w_out": f(w_out),
              "w_ff1": f(w_ff1), "w_ff3": f(w_ff3), "w_ff2": f(w_ff2)}
    in_maps = []
    for b in range(8):
        xT = np.ascontiguousarray(np.concatenate([ctx[b], x[b]], axis=0).T)
        cs = np.ascontiguousarray(np.stack([c[b], c_ctx], -1).reshape(KT, P, 2).transpose(1, 0, 2).reshape(P, KT * 2))
        m = dict(shared)
        m["xT"] = xT
        m["cs"] = cs
        in_maps.append(m)
    res = run_bass_kernel_spmd(nc, in_maps, core_ids=list(range(8)))
    out = np.stack([np.ascontiguousarray(r["outT"].T) for r in res.results], axis=0)
    if _dbg:
        kernel.dbg = [r["dbg"] for r in res.results]
    return out.astype(np.float32)
```

```python
import numpy as np
from contextlib import ExitStack
import concourse.bass as bass
import concourse.mybir as mybir
from concourse.bass_utils import run_bass_kernel_spmd

F32 = mybir.dt.float32
BF16 = mybir.dt.bfloat16
AF = mybir.ActivationFunctionType
ALU = mybir.AluOpType

P = 128
D = 1024
KT = 8
T = 2304
LC = 256
NL = 2048
DEPTH = 4
DFF = 2816
FT = 22
ALPHA = float(8 ** 0.25)
EPS = 1e-6
CHUNKS = [(0, 256), (256, 512), (768, 512), (1280, 256), (1536, 512), (2048, 256)]
GROUPS = [[0, 1], [2, 3], [4, 5]]
QCH = [(0, 256), (256, 512), (768, 512), (1280, 512), (1792, 512)]
LNCH = [(i * 256, 256) for i in range(9)]
APAD = 286 + 2078
NPAR = 218
ENG = ['pe', 'act', 'dve', 'pool', 'sp']


def units(s, n):
    return range(s // 256, (s + n - 1) // 256 + 1)


def aidx(tok):
    return 15 + tok if tok < 256 else 45 + tok


class _RecInst:
    def __init__(self, idx):
        self.idx = idx


class _Rec:
    def __init__(self):
        self.calls = []

    def __getattr__(self, name):
        def f(*a, **k):
            self.calls.append((name, a, k))
            return _RecInst(len(self.calls) - 1)
        return f


class Lane:
    def __init__(self, sem):
        self.sem = sem
        self.val = 0


class Res:
    __slots__ = ('w', 'r')

    def __init__(self):
        self.w = None
        self.r = {}


class Sched:
    def __init__(self, nc, stack):
        self.nc = nc
        self.stack = stack
        self.progs = {e: [] for e in ENG}
        self.lane = {e: self.newlane('L' + e) for e in ENG}
        self.known = {e: {} for e in ENG}
        self.res = {}
        self.dlanes = []

    def newlane(self, name):
        return Lane(self.stack.enter_context(self.nc.semaphore(name)))

    def dmalane(self, name):
        ln = self.newlane(name)
        self.dlanes.append(ln)
        return ln

    def R(self, *key):
        r = self.res.get(key)
        if r is None:
            r = self.res[key] = Res()
        return r

    def op(self, eng, fn, reads=(), writes=(), lane=None, ndma=0):
        waits = {}

        def need(ev):
            if ev is None:
                return
            ln, v = ev
            if waits.get(ln, 0) < v:
                waits[ln] = v
        for r in reads:
            need(r.w)
        for r in writes:
            need(r.w)
            for ln, v in r.r.items():
                need((ln, v))
        k = self.known[eng]
        wl = []
        for ln, v in waits.items():
            if k.get(ln, 0) < v:
                k[ln] = v
                wl.append((ln.sem, v))
        if lane is None:
            lane = self.lane[eng]
            lane.val += 1
            inc = 1
        else:
            lane.val += 16 * ndma
            inc = 16
        ev = (lane, lane.val)
        for r in reads:
            if r.r.get(lane, 0) < lane.val:
                r.r[lane] = lane.val
        for r in writes:
            r.w = ev
            r.r = {}
        sem = lane.sem
        rec = _Rec()
        out = fn(rec)
        calls = rec.calls
        inc_idx = [o_.idx for o_ in out] if inc == 16 else [out.idx]

        def run(e):
            for s_, v_ in wl:
                e.wait_ge(s_, v_)
            insts = [getattr(e, nm)(*a, **kw) for (nm, a, kw) in calls]
            for i_ in inc_idx:
                insts[i_].then_inc(sem, inc)
        self.progs[eng].append(run)

    def barrier(self):
        for e in ENG:
            k = self.known[e]
            wl = []
            for ln in [self.lane[x] for x in ENG if x != e] + self.dlanes:
                if k.get(ln, 0) < ln.val:
                    k[ln] = ln.val
                    wl.append((ln.sem, ln.val))

            def run(en, wl=wl):
                for s_, v_ in wl:
                    en.wait_ge(s_, v_)
            self.progs[e].append(run)
        for r in self.res.values():
            r.w = None
            r.r = {}

    def final_wait(self, eng, lanes):
        wl = [(ln.sem, ln.val) for ln in lanes]

        def run(en):
            for s_, v_ in wl:
                en.wait_ge(s_, v_)
        self.progs[eng].append(run)


def build(nlayers=DEPTH, dbg=None):
    nc = bass.Bass("TRN2", target_bir_lowering=False)
    dt = nc.dram_tensor
    xT_d = dt("xT", [D, T], F32, kind="ExternalInput").ap()
    cs_d = dt("cs", [P, KT * 2], F32, kind="ExternalInput").ap()
    par_d = dt("par", [P, DEPTH * NPAR], F32, kind="ExternalInput").ap()
    cm_d = dt("cmat", [P, 5 * P], F32, kind="ExternalInput").ap()
    tab_d = dt("tabs", [P, 2 * T], F32, kind="ExternalInput").ap()
    wada_d = dt("w_ada", [DEPTH, D, 6 * D], F32, kind="ExternalInput").ap()
    win_d = dt("w_in", [DEPTH, D, 1792], F32, kind="ExternalInput").ap()
    wout_d = dt("w_out", [DEPTH, D, D], F32, kind="ExternalInput").ap()
    wf1_d = dt("w_ff1", [DEPTH, D, DFF], F32, kind="ExternalInput").ap()
    wf3_d = dt("w_ff3", [DEPTH, D, DFF], F32, kind="ExternalInput").ap()
    wf2_d = dt("w_ff2", [DEPTH, DFF, D], F32, kind="ExternalInput").ap()
    out_d = dt("outT", [D, NL], F32, kind="ExternalOutput").ap()
    xa_d = dt("xa_scr", [D, T], F32).ap()
    dbg_d = None
    if dbg:
        dbg_d = dt("dbg", [P, dbg], F32, kind="ExternalOutput").ap()

    stack = ExitStack()
    with stack:
        def sb(name, shape, dtp):
            return stack.enter_context(nc.sbuf_tensor(name, shape, dtp))
        S = Sched(nc, stack)
        R = S.R
        hT = sb("hT", [P, KT, T], BF16)
        BIGN = 39424
        big = sb("big", [P, BIGN], BF16)
        qT = big[:, 0:9216].rearrange("p (a t) -> p a t", a=4)
        Kz = big[:, 9216:18432].rearrange("p (a t) -> p a t", a=4)
        Vaug = big[:, 18432:25344].rearrange("p (t h c) -> p t h c", t=18, h=2)
        abuf = big[:, 25344:34800].rearrange("p (a t) -> p a t", a=4)
        tabC = big[:, 34800:37104]
        tabS = big[:, 37104:39408]
        act = big[:, 0:16896].rearrange("p (f t) -> p f t", f=FT)
        w2s = big[:, 16896:39424].rearrange("p (f c) -> p f c", f=FT)
        NSLOT = 3
        wsl = [sb(f"ws{i}", [P, KT, 512], BF16) for i in range(NSLOT)]
        wsl_lane = [S.dmalane(f"wsl{i}") for i in range(NSLOT)]
        NSTG = 4
        stg = [sb(f"stg{i}", [P, KT, 256], F32) for i in range(NSTG)]
        stg_in = [S.dmalane(f"sti{i}") for i in range(NSTG)]
        stg_out = [S.dmalane(f"sto{i}") for i in range(NSTG)]
        cmat = sb("cmatsb", [P, 3, P], F32)
        bdm = sb("bdm", [P, P], BF16)
        identb = sb("identb", [P, P], BF16)
        onesb = sb("onesb", [P, P], BF16)
        stgb = [stg[i][:, :, :].bitcast(BF16) for i in range(NSTG)]

        def dgv(d):
            return stgb[d // 32][:, (d % 32) // 4, (d % 4) * 128:(d % 4) * 128 + 128]

        def dgres(d):
            return R('stg', d // 32, (d % 32) // 4)
        par = sb("parsb", [P, DEPTH, NPAR], F32)
        pal = sb("pal", [P, DEPTH, 32], F32)
        cs = sb("cssb", [P, KT, 2], F32)
        epsc = sb("epsc", [P, 2], F32)
        sbf = sb("sbf", [P, KT, 2], BF16)
        modT = [sb(f"modT{i}", [P, 48, 2], F32) for i in range(2)]
        onesc = [sb(f"onesc{i}", [P, 2, KT, 2], F32) for i in range(2)]
        hA = [sb(f"hA{i}", [P, 2, KT, 2], F32) for i in range(2)]
        hB = [sb(f"hB{i}", [P, 2, KT, 2], F32) for i in range(2)]
        NTMP = 6
        tmp = [sb(f"tmp{i}", [P, 512], F32) for i in range(NTMP)]
        sqb = [sb(f"sqb{i}", [P, 512], BF16) for i in range(2)]
        NPT = 2
        PT = [sb(f"pt{i}", [P, 2, 512], BF16) for i in range(NPT)]
        convy = sb("convy", [P, 4, 512], F32)
        RR = [sb(f"rr{i}", [P, 512], F32) for i in range(2)]
        ps = stack.enter_context(nc.psum_tensor("ps", [P, 8, 512], F32))
        lane_const = S.dmalane("const")
        lane_tab = S.dmalane("tab")
        lane_bdm = S.dmalane("bdm")
        lane_rr = [S.dmalane("rr0"), S.dmalane("rr1")]
        lane_wo = S.dmalane("wo")
        woA = convy[:, :, :].bitcast(BF16).rearrange("p a (b c) -> p (a b) c", b=2)
        woB = [PT[0], PT[1], RR[0][:, :].bitcast(BF16).rearrange("p (a c) -> p a c", a=2),
               RR[1][:, :].bitcast(BF16).rearrange("p (a c) -> p a c", a=2)]
        lane_w2 = S.dmalane("w2")
        lane_dbg = S.dmalane("dbg")

        cnt = {'tmp': 0, 'sqb': 0, 'pt': 0, 'ws': 0, 'stg': 0, 'bank': 0, 'dg': 0, 'cb': 0}

        def nxt(kind, n):
            i = cnt[kind] % n
            cnt[kind] += 1
            return i

        def PB(b, n):
            return ps[:, b, 0:n]

        def ld_const(e):
            return [e.dma_start(out=cmat[:, :, :], in_=cm_d[:, 0:3 * P].rearrange("p (a c) -> p a c", a=3)),
                    e.dma_start(out=par[:, :, :], in_=par_d.rearrange("p (l c) -> p l c", l=DEPTH)),
                    e.dma_start(out=cs[:, :, :], in_=cs_d.rearrange("p (k c) -> p k c", k=KT))]
        S.op('sp', ld_const, writes=[R('const')], lane=lane_const, ndma=3)
        S.op('pool', lambda e: [e.dma_start(out=bdm[:, :], in_=cm_d[:, 3 * P:4 * P]),
                                e.dma_start(out=identb[:, :], in_=cm_d[:, 4 * P:5 * P])],
             writes=[R('bdm')], lane=lane_bdm, ndma=2)
        S.op('act', lambda e: e.activation(out=sbf[:, :, :], in_=cs[:, :, :], func=AF.Silu),
             reads=[R('const')], writes=[R('sbf')])
        S.op('dve', lambda e: e.tensor_scalar(out=pal[:, :, :], in0=par[:, :, 48:80], scalar1=ALPHA,
                                              scalar2=None, op0=ALU.mult),
             reads=[R('const')], writes=[R('pal')])
        S.op('dve', lambda e: e.memset(onesb[:, :], 1.0), writes=[R('onesb')])
        S.op('dve', lambda e: e.memset(epsc[:, 0:1], EPS), writes=[R('epsc')])
        S.op('dve', lambda e: e.memset(epsc[:, 1:2], 64 * EPS), writes=[R('epsc')])
        S.op('dve', lambda e: e.memset(RR[0][:, :], 0.0), writes=[R('rr', 0)])
        S.op('dve', lambda e: e.memset(RR[1][:, :], 0.0), writes=[R('rr', 1)])

        def pc(l, off, n=1):
            return par[:, l, off:off + n]
        O_BADA, O_LN1G, O_LN1B, O_LN2G, O_LN2B, O_DWW, O_DWB, O_CLG, O_CLB, O_QG, O_KG = \
            0, 48, 56, 64, 72, 80, 204, 208, 212, 216, 217

        def wload(dmas, nd):
            si = nxt('ws', NSLOT)
            pairs = dmas(wsl[si])

            def fn(e):
                return [e.dma_start(out=o, in_=i) for (o, i) in pairs]
            S.op('pool', fn, writes=[R('ws', si)], lane=wsl_lane[si], ndma=len(pairs))
            return si

        def wcols(wd, l, c0, w):
            return wd[l, :, c0:c0 + w].rearrange("(k p) c -> p k c", p=P)

        def mods_gen(l):
            m = modT[l % 2]
            mres = R('mod', l % 2)
            bank = 7
            for blk in range(12):
                si = wload(lambda slot: [(slot[:, :, :], wcols(wada_d, l, blk * 512, 512))], 1)

                def fn(e):
                    last_ = None
                    for f4 in range(4):
                        for k in range(KT):
                            last_ = e.matmul(ps[:, bank, 2 * f4:2 * f4 + 2], wsl[si][:, k, f4 * 128:(f4 + 1) * 128],
                                             sbf[:, k, :], start=(k == 0), stop=(k == KT - 1))
                    return last_
                S.op('pe', fn, reads=[R('ws', si), R('sbf')], writes=[R('ps', bank)])
                for col in range(2):
                    S.op('dve', lambda e: e.tensor_tensor(
                        out=m[:, blk * 4:blk * 4 + 4, col],
                        in0=ps[:, bank, 0:8].rearrange("p (f c) -> p f c", c=2)[:, :, col],
                        in1=par[:, l, blk * 4:blk * 4 + 4], op=ALU.add),
                        reads=[R('ps', bank), R('const')], writes=[mres])
                yield
            osc = onesc[l % 2]
            for j, g in enumerate((1, 4)):
                S.op('dve', lambda e: e.tensor_scalar(
                    out=osc[:, j, :, :], in0=m[:, g * 8:(g + 1) * 8, :], scalar1=1.0, scalar2=None, op0=ALU.add),
                    reads=[mres], writes=[R('osc', l % 2)])

        def emit_mods(l):
            for _ in mods_gen(l):
                pass

        def emit_hcoef(l, which, goff, boff, lprev):
            m = modT[l % 2]
            osc = onesc[l % 2]
            A = hA[l % 2]
            B = hB[l % 2]
            shg = 0 if which == 0 else 3
            rd = [R('mod', l % 2), R('osc', l % 2), R('const')]
            wr = [R('hc', l % 2, which)]
            for col in range(2):
                if lprev is None:
                    S.op('dve', lambda e, col=col: e.tensor_copy(out=A[:, which, :, col], in_=osc[:, which, :, col]),
                         reads=rd, writes=wr)
                    S.op('dve', lambda e, col=col: e.tensor_copy(out=B[:, which, :, col],
                                                                  in_=m[:, shg * 8:shg * 8 + 8, col]),
                         reads=rd, writes=wr)
                else:
                    S.op('dve', lambda e, col=col: e.tensor_tensor(
                        out=A[:, which, :, col], in0=osc[:, which, :, col], in1=par[:, lprev, goff:goff + 8],
                        op=ALU.mult), reads=rd, writes=wr)
                    S.op('dve', lambda e, col=col: e.tensor_tensor(
                        out=B[:, which, :, col], in0=osc[:, which, :, col], in1=par[:, lprev, boff:boff + 8],
                        op=ALU.mult), reads=rd, writes=wr)
                    S.op('dve', lambda e, col=col: e.tensor_tensor(
                        out=B[:, which, :, col], in0=B[:, which, :, col], in1=m[:, shg * 8:shg * 8 + 8, col],
                        op=ALU.add), reads=rd + wr, writes=wr)

        def layernorm(nt, vt, vres, n, nfeat, outfn, banks=(6, 7)):
            b1, b2 = banks
            ones = cmat[:, 0, :]
            def f1(e):
                last = None
                for k in range(nt):
                    last = e.matmul(PB(b1, n), ones, vt(k), start=(k == 0), stop=(k == nt - 1))
                return last
            S.op('pe', f1, reads=[vres(k) for k in range(nt)] + [R('const')], writes=[R('ps', b1)])
            for k in range(nt):
                ti = nxt('tmp', NTMP)
                S.op('act', lambda e, k=k, ti=ti: e.activation(out=tmp[ti][:, 0:n], in_=vt(k), func=AF.Square),
                     reads=[vres(k)], writes=[R('tmp', ti)])
                S.op('pe', lambda e, k=k, ti=ti: e.matmul(PB(b2, n), ones, tmp[ti][:, 0:n], start=(k == 0),
                                                          stop=(k == nt - 1)),
                     reads=[R('tmp', ti), R('const')], writes=[R('ps', b2)])
            tm = nxt('tmp', NTMP)
            tr = nxt('tmp', NTMP)
            inv = 1.0 / nfeat
            S.op('dve', lambda e: e.tensor_scalar(out=tmp[tm][:, 0:n], in0=PB(b1, n), scalar1=inv, scalar2=None,
                                                  op0=ALU.mult),
                 reads=[R('ps', b1)], writes=[R('tmp', tm)])
            S.op('dve', lambda e: e.tensor_tensor(out=tmp[tr][:, 0:n], in0=tmp[tm][:, 0:n], in1=tmp[tm][:, 0:n],
                                                  op=ALU.mult),
                 reads=[R('tmp', tm)], writes=[R('tmp', tr)])
            S.op('dve', lambda e: e.scalar_tensor_tensor(out=tmp[tr][:, 0:n], in0=PB(b2, n), scalar=inv,
                                                         in1=tmp[tr][:, 0:n], op0=ALU.mult, op1=ALU.subtract),
                 reads=[R('ps', b2), R('tmp', tr)], writes=[R('tmp', tr)])
            S.op('act', lambda e: e.activation(out=tmp[tr][:, 0:n], in_=tmp[tr][:, 0:n], func=AF.Ln, bias=epsc[:, 0:1]),
                 reads=[R('tmp', tr), R('epsc')], writes=[R('tmp', tr)])
            S.op('act', lambda e: e.activation(out=tmp[tr][:, 0:n], in_=tmp[tr][:, 0:n], func=AF.Exp, scale=-0.5),
                 reads=[R('tmp', tr)], writes=[R('tmp', tr)])
            for k in range(nt):
                S.op('dve', lambda e, k=k: e.tensor_tensor(out=vt(k), in0=vt(k), in1=tmp[tm][:, 0:n], op=ALU.subtract),
                     reads=[vres(k), R('tmp', tm)], writes=[vres(k)])
                S.op('dve', lambda e, k=k: e.tensor_tensor(out=vt(k), in0=vt(k), in1=tmp[tr][:, 0:n], op=ALU.mult),
                     reads=[vres(k), R('tmp', tr)], writes=[vres(k)])
                outfn(k)

        dbg_off = [0]

        def tap(ap_f32, n, reads, parts=P):
            if dbg_d is None:
                return
            o = dbg_off[0]
            dbg_off[0] += n
            S.op('pool', lambda e: [e.dma_start(out=dbg_d[0:parts, o:o + n], in_=ap_f32)], reads=reads,
                 lane=lane_dbg, ndma=1)

        def stage_in(src_d, s, n):
            si = nxt('stg', NSTG)
            S.op('sp', lambda e: [e.dma_start(out=stg[si][:, :, 0:n],
                                              in_=src_d[:, s:s + n].rearrange("(k p) t -> p k t", p=P))],
                 reads=[R('xa', u) for u in units(s, n)] if src_d is xa_d else [],
                 writes=[R('stg', si, k) for k in range(KT)], lane=stg_in[si], ndma=1)
            return si

        def stage_out(dst_d, si, s_dst, n, s_tok):
            S.op('act', lambda e: [e.dma_start(out=dst_d[:, s_dst:s_dst + n].rearrange("(k p) t -> p k t", p=P),
                                              in_=stg[si][:, :, 0:n])],
                 reads=[R('stg', si, k) for k in range(KT)],
                 writes=[R('xa', u) for u in units(s_tok, n)] if dst_d is xa_d else [R('outd')],
                 lane=stg_out[si], ndma=1)

        def hres(k, s, n):
            return [R('h', k, u) for u in units(s, n)]

        def ln_outputs(si, s, n, l_next_slot, which, ga_ap, ba_ap, col, final=False, g_ap=None, b_ap=None,
                       skip_h=False):
            def outfn(k):
                t_ap = stg[si][:, k, 0:n]
                tres = R('stg', si, k)
                if final:
                    S.op('act', lambda e: e.activation(out=t_ap, in_=t_ap, func=AF.Identity,
                                                       scale=g_ap(k), bias=b_ap(k)),
                         reads=[tres, R('const')], writes=[tres])
                    return
                A = hA[l_next_slot]
                B = hB[l_next_slot]
                if not skip_h:
                    S.op('act', lambda e: e.activation(out=hT[:, k, s:s + n], in_=t_ap, func=AF.Identity,
                                                       scale=A[:, which, k, col:col + 1],
                                                       bias=B[:, which, k, col:col + 1]),
                         reads=[tres, R('hc', l_next_slot, which)], writes=hres(k, s, n))
                S.op('act', lambda e: e.activation(out=t_ap, in_=t_ap, func=AF.Identity,
                                                   scale=ga_ap(k), bias=ba_ap(k)),
                     reads=[tres, R('pal'), R('const')], writes=[tres])
            return outfn

        emit_mods(0)
        emit_hcoef(0, 0, None, None, None)
        for ci, (s, n) in enumerate(LNCH):
            si = stage_in(xT_d, s, n)
            col = 1 if ci == 0 else 0
            for k in range(KT):
                t_ap = stg[si][:, k, 0:n]
                tres = R('stg', si, k)
                S.op('act', lambda e, t_ap=t_ap, k=k, col=col, s=s, n=n: e.activation(
                    out=hT[:, k, s:s + n], in_=t_ap, func=AF.Identity,
                    scale=hA[0][:, 0, k, col:col + 1], bias=hB[0][:, 0, k, col:col + 1]),
                    reads=[tres, R('hc', 0, 0)], writes=hres(k, s, n))
                S.op('dve', lambda e, t_ap=t_ap: e.tensor_scalar(out=t_ap, in0=t_ap, scalar1=ALPHA, scalar2=None,
                                                                 op0=ALU.mult),
                     reads=[tres], writes=[tres])
            stage_out(xa_d, si, s, n, s)

        for l in range(nlayers):
            last = (l == DEPTH - 1)
            sl = l % 2
            chunks = [c for ci, c in enumerate(CHUNKS) if not (last and ci == 0)]
            if l > 0:
                S.barrier()
            S.op('pool', lambda e: [e.dma_start(out=tabC, in_=tab_d[:, 0:T]),
                                    e.dma_start(out=tabS, in_=tab_d[:, T:2 * T])],
                 writes=[R('tab')], lane=lane_tab, ndma=2)
            S.op('dve', lambda e: e.memset(abuf[:, :, :], 0.0), writes=[R('apad')])
            S.op('dve', lambda e: e.memset(Kz[:, :, :], 0.0), writes=[R('k', h_, u) for h_ in range(2) for u in range(9)])
            S.op('dve', lambda e: e.memset(Vaug[:, :, :, 0:64], 1.0), writes=[R('v', tt) for tt in range(18)])
            S.op('dve', lambda e: e.memset(Vaug[:, :, :, 128:192], 1.0), writes=[R('v', tt) for tt in range(18)])

            dg_next = [0]

            def emit_diags(k_):
                for _ in range(k_):
                    d_ = dg_next[0]
                    if d_ >= 124:
                        return
                    dg_next[0] += 1
                    if (d_ // 4) % 2 == 0:
                        S.op('dve', lambda e: e.tensor_scalar(out=dgv(d_), in0=identb[:, :], scalar1=pc(l, O_DWW + d_),
                                                              scalar2=None, op0=ALU.mult),
                             reads=[R('bdm'), R('const')], writes=[dgres(d_)])
                    else:
                        S.op('act', lambda e: e.activation(out=dgv(d_), in_=identb[:, :], func=AF.Identity,
                                                           scale=pc(l, O_DWW + d_)),
                             reads=[R('bdm'), R('const')], writes=[dgres(d_)])
            rp = [(tmp[i_], R('tmp', i_)) for i_ in range(6)] + \
                 [(convy[:, j_, :], R('cy', j_)) for j_ in range(3)]
            inst_no = [0]
            stageA = []
            stageB = []

            def nr_1b(it):
                bank, s, n, gcol, qi = it['bank'], it['s'], it['n'], it['gcol'], it['qi']
                (T0, r0), (T1, r1), _ = it['tiles']
                b2 = 4 + nxt('bank', 4)
                S.op('pe', lambda e: e.matmul(PB(b2, n), bdm[:, :], sqb[qi][:, 0:n], start=True, stop=True),
                     reads=[R('sqb', qi), R('bdm')], writes=[R('ps', b2)])
                S.op('act', lambda e: e.activation(out=T0[:, 0:n], in_=PB(b2, n), func=AF.Ln, bias=epsc[:, 1:2]),
                     reads=[R('ps', b2), R('epsc')], writes=[r0])
                S.op('act', lambda e: e.activation(out=T0[:, 0:n], in_=T0[:, 0:n], func=AF.Exp, scale=-0.5),
                     reads=[r0], writes=[r0])
                S.op('dve', lambda e: e.scalar_tensor_tensor(out=T1[:, 0:n], in0=PB(bank, n), scalar=gcol,
                                                             in1=T0[:, 0:n], op0=ALU.mult, op1=ALU.mult),
                     reads=[R('ps', bank), r0, R('const')], writes=[r1])

            def nr_2(it):
                s, n, dst_ap, dst_res = it['s'], it['n'], it['dst_ap'], it['dst_res']
                (T0, r0), (T1, r1), (T2, r2) = it['tiles']
                b3 = 4 + nxt('bank', 4)
                S.op('pe', lambda e: e.matmul(PB(b3, n), cmat[:, 2, :], T1[:, 0:n], start=True, stop=True),
                     reads=[r1, R('const')], writes=[R('ps', b3)])
                S.op('dve', lambda e: e.tensor_tensor(out=T0[:, 0:n], in0=T1[:, 0:n], in1=tabC[:, s:s + n], op=ALU.mult),
                     reads=[r1, R('tab')], writes=[r0])
                S.op('dve', lambda e: e.tensor_tensor(out=T2[:, 0:n], in0=PB(b3, n), in1=tabS[:, s:s + n], op=ALU.mult),
                     reads=[R('ps', b3), R('tab')], writes=[r2])
                if isinstance(dst_ap, tuple):
                    for hp, d_ap in enumerate(dst_ap):
                        rows = slice(64 * hp, 64 * hp + 64)
                        S.op('dve', lambda e: e.tensor_tensor(out=d_ap, in0=T0[rows, 0:n], in1=T2[rows, 0:n], op=ALU.add),
                             reads=[r0, r2], writes=dst_res)
                else:
                    S.op('dve', lambda e: e.tensor_tensor(out=dst_ap, in0=T0[:, 0:n], in1=T2[:, 0:n], op=ALU.add),
                         reads=[r0, r2], writes=dst_res)

            def nr_submit(bank, s, n, gcol, dst_ap, dst_res):
                k_ = inst_no[0]
                inst_no[0] += 1
                qi = nxt('sqb', 2)
                S.op('act', lambda e: e.activation(out=sqb[qi][:, 0:n], in_=PB(bank, n), func=AF.Square),
                     reads=[R('ps', bank)], writes=[R('sqb', qi)])
                stageA.append(dict(bank=bank, s=s, n=n, gcol=gcol, qi=qi, dst_ap=dst_ap, dst_res=dst_res,
                                   tiles=[rp[(3 * k_ + j_) % 9] for j_ in range(3)]))
                if len(stageA) > 1:
                    it = stageA.pop(0)
                    nr_1b(it)
                    stageB.append(it)
                if len(stageB) > 1:
                    nr_2(stageB.pop(0))
                emit_diags(4)

            def flush_pend():
                while stageA or stageB:
                    if stageA:
                        it = stageA.pop(0)
                        nr_1b(it)
                        stageB.append(it)
                    if stageB and (len(stageB) > 1 or not stageA):
                        nr_2(stageB.pop(0))

            def proj(si, c0, s, n, bank):
                pairs = [(wsl[si][:, k, c0:c0 + 128], hT[:, k, s:s + n]) for k in range(KT)]

                def fn(e):
                    last_ = None
                    for i, (a_, b_) in enumerate(pairs):
                        last_ = e.matmul(PB(bank, n), a_, b_, start=(i == 0), stop=(i == KT - 1))
                    return last_
                S.op('pe', fn, reads=[R('ws', si)] + [r for k in range(KT) for r in hres(k, s, n)],
                     writes=[R('ps', bank)])

            def dm_A(slot):
                prs = []
                for h in range(2):
                    for d2 in range(2):
                        prs.append((slot[:, :, (2 * h + d2) * 64:(2 * h + d2 + 1) * 64], wcols(win_d, l, 512 + 64 * h, 64)))
                prs.append((slot[:, :, 256:384], wcols(win_d, l, 640, 128)))
                return prs
            si = wload(dm_A, 5)
            for h in range(2):
                for (s, n) in CHUNKS:
                    bank = nxt('bank', 4)
                    proj(si, 128 * h, s, n, bank)
                    nr_submit(bank, s, n, pc(l, O_KG), (Kz[0:64, 2 * h, s:s + n], Kz[64:128, 2 * h + 1, s:s + n]),
                              [R('k', h, u) for u in units(s, n)])
            flush_pend()
            for tt in range(18):
                bank = nxt('bank', 4)
                pairs = [(hT[:, k, tt * 128:(tt + 1) * 128], wsl[si][:, k, 256:384]) for k in range(KT)]

                def fnv(e, pairs=pairs, bank=bank):
                    last_ = None
                    for i, (a_, b_) in enumerate(pairs):
                        last_ = e.matmul(PB(bank, 128), a_, b_, start=(i == 0), stop=(i == KT - 1))
                    return last_
                S.op('pe', fnv, reads=[R('ws', si)] + [r for k in range(KT) for r in hres(k, tt * 128, 128)],
                     writes=[R('ps', bank)])
                S.op('act', lambda e, tt=tt, bank=bank: e.activation(
                    out=Vaug[:, tt, :, 64:128], in_=PB(bank, 128).rearrange("p (h c) -> p h c", h=2), func=AF.Copy),
                    reads=[R('ps', bank)], writes=[R('v', tt)])
            for jb in range(2):
                def dm_B(slot, jb=jb):
                    prs = []
                    for jj in range(2):
                        j = 2 * jb + jj
                        prs.append((slot[:, :, jj * 256:jj * 256 + 128], wcols(win_d, l, 768 + 128 * j, 128)))
                        prs.append((slot[:, :, jj * 256 + 128:jj * 256 + 256], wcols(win_d, l, 1280 + 128 * j, 128)))
                    return prs
                si = wload(dm_B, 4)
                for jj in range(2):
                    j = 2 * jb + jj
                    for (s, n) in chunks:
                        bu = nxt('bank', 4)
                        bg = nxt('bank', 4)
                        proj(si, jj * 256, s, n, bu)
                        proj(si, jj * 256 + 128, s, n, bg)
                        ti = nxt('tmp', NTMP)
                        S.op('act', lambda e, bg=bg, ti=ti, n=n: e.activation(out=tmp[ti][:, 0:n], in_=PB(bg, n),
                                                                                func=AF.Sigmoid),
                             reads=[R('ps', bg)], writes=[R('tmp', ti)])
                        a0 = aidx(s)
                        S.op('dve', lambda e, bu=bu, ti=ti, n=n, j=j, a0=a0: e.tensor_tensor(
                            out=abuf[:, j, a0:a0 + n], in0=PB(bu, n), in1=tmp[ti][:, 0:n], op=ALU.mult),
                            reads=[R('ps', bu), R('tmp', ti), R('apad')], writes=[R('a', j, u) for u in units(s, n)])
            si = wload(lambda slot: [(slot[:, :, :], wcols(win_d, l, 0, 512))], 1)
            for i in range(4):
                for (s, n) in chunks:
                    bank = nxt('bank', 4)
                    proj(si, 128 * i, s, n, bank)
                    nr_submit(bank, s, n, pc(l, O_QG), qT[:, i, s:s + n], [R('q', i, u) for u in units(s, n)])
            flush_pend()

            if l == 0:
                allh = [R('h', k, u) for k in range(KT) for u in range(9)]
                tap(hT[:, 0, :], T, allh)
                tap(Kz[:, 0, :], T, [R('k', 0, u) for u in range(9)])
                tap(Kz[:, 1, :], T, [R('k', 0, u) for u in range(9)])
                tap(qT[:, 0, :], T, [R('q', 0, u) for u in range(9)])
                tap(Vaug[:, :, 0, 64:128], 18 * 64, [R('v', tt) for tt in range(18)])
                tap(abuf[:, 0, :], APAD, [R('a', 0, u) for u in range(9)] + [R('apad')])

            emit_diags(124)
            def conv_gen():
                for ci, (s, n) in enumerate(CHUNKS):
                    if last and ci == 0:
                        continue
                    seg0, seg1 = (0, 256) if s < 256 else (256, T)
                    rd_units = units(max(seg0, s - 15), min(seg1, s + n + 15) - max(seg0, s - 15))
                    base = aidx(s) - 15
                    for j in range(4):
                        rd = [R('a', j, u) for u in rd_units] + [R('apad')]
                        cb = 6
                        for k0 in range(0, 31, 3):
                            ks = list(range(k0, min(31, k0 + 3)))

                            def fnc(e):
                                last_ = None
                                for k in ks:
                                    last_ = e.matmul(PB(cb, n), dgv(j * 31 + k), abuf[:, j, base + k:base + k + n],
                                                     start=(k == 0), stop=(k == 30))
                                return last_
                            S.op('pe', fnc, reads=rd + [dgres(j * 31 + k) for k in ks], writes=[R('ps', cb)])
                            yield
                        S.op('act', lambda e: e.activation(out=convy[:, j, 0:n], in_=PB(cb, n), func=AF.Identity,
                                                           bias=pc(l, O_DWB + j)),
                             reads=[R('ps', cb), R('const')], writes=[R('cy', j)])

                    def outfn(j, s=s, n=n):
                        S.op('act', lambda e: e.activation(out=hT[:, 4 + j, s:s + n], in_=convy[:, j, 0:n], func=AF.Silu,
                                                           scale=pc(l, O_CLG + j), bias=pc(l, O_CLB + j)),
                             reads=[R('cy', j), R('const')], writes=hres(4 + j, s, n))
                    hold['ln'] = True
                    ones = cmat[:, 0, :]

                    def f1c(e):
                        last_ = None
                        for k in range(4):
                            last_ = e.matmul(PB(7, n), ones, convy[:, k, 0:n], start=(k == 0), stop=(k == 3))
                        return last_
                    S.op('pe', f1c, reads=[R('cy', k) for k in range(4)] + [R('const')], writes=[R('ps', 7)])
                    yield
                    for k in range(4):
                        ti = nxt('sqb', 2)
                        S.op('act', lambda e: e.activation(out=sqb[ti][:, 0:n], in_=convy[:, k, 0:n], func=AF.Square),
                             reads=[R('cy', k)], writes=[R('sqb', ti)])
                        S.op('pe', lambda e: e.matmul(PB(6, n), onesb[:, :], sqb[ti][:, 0:n], start=(k == 0), stop=(k == 3)),
                             reads=[R('sqb', ti), R('onesb')], writes=[R('ps', 6)])
                        yield
                    tm, tr = 2, 3
                    inv = 1.0 / 512.0
                    S.op('dve', lambda e: e.tensor_scalar(out=tmp[tm][:, 0:n], in0=PB(7, n), scalar1=inv, scalar2=None,
                                                          op0=ALU.mult),
                         reads=[R('ps', 7)], writes=[R('tmp', tm)])
                    S.op('dve', lambda e: e.tensor_tensor(out=tmp[tr][:, 0:n], in0=tmp[tm][:, 0:n], in1=tmp[tm][:, 0:n],
                                                          op=ALU.mult),
                         reads=[R('tmp', tm)], writes=[R('tmp', tr)])
                    S.op('dve', lambda e: e.scalar_tensor_tensor(out=tmp[tr][:, 0:n], in0=PB(6, n), scalar=inv,
                                                                 in1=tmp[tr][:, 0:n], op0=ALU.mult, op1=ALU.subtract),
                         reads=[R('ps', 6), R('tmp', tr)], writes=[R('tmp', tr)])
                    hold['ln'] = False
                    yield
                    S.op('act', lambda e: e.activation(out=tmp[tr][:, 0:n], in_=tmp[tr][:, 0:n], func=AF.Ln,
                                                       bias=epsc[:, 0:1]),
                         reads=[R('tmp', tr), R('epsc')], writes=[R('tmp', tr)])
                    S.op('act', lambda e: e.activation(out=tmp[tr][:, 0:n], in_=tmp[tr][:, 0:n], func=AF.Exp, scale=-0.5),
                         reads=[R('tmp', tr)], writes=[R('tmp', tr)])
                    yield
                    for k in range(4):
                        S.op('dve', lambda e: e.tensor_tensor(out=convy[:, k, 0:n], in0=convy[:, k, 0:n],
                                                              in1=tmp[tm][:, 0:n], op=ALU.subtract),
                             reads=[R('cy', k), R('tmp', tm)], writes=[R('cy', k)])
                        S.op('dve', lambda e: e.tensor_tensor(out=convy[:, k, 0:n], in0=convy[:, k, 0:n],
                                                              in1=tmp[tr][:, 0:n], op=ALU.mult),
                             reads=[R('cy', k), R('tmp', tr)], writes=[R('cy', k)])
                        yield
                    for k in range(4):
                        outfn(k)
                    yield
            def mods_filler():
                if l + 1 < nlayers:
                    for _ in mods_gen(l + 1):
                        yield
                    emit_hcoef(l + 1, 0, O_LN2G, O_LN2B, l)
                emit_hcoef(l, 1, O_LN1G, O_LN1B, l)
                yield
            hold = {'ln': False}
            fillers = [conv_gen(), mods_filler()]
            fstate = {'i': 0}

            def conv_step(k=1):
                for _ in range(k):
                    if not fillers:
                        return
                    fstate['i'] = (fstate['i'] + 1) % len(fillers)
                    if hold['ln']:
                        fstate['i'] = 0
                    g_ = fillers[fstate['i']]
                    try:
                        next(g_)
                    except StopIteration:
                        fillers.remove(g_)

            for qi_, (s, n) in enumerate(QCH):
                if last and qi_ == 0:
                    continue
                ktiles = [0, 1] if qi_ == 0 else list(range(18))
                for i in range(4):
                    h = i // 2
                    stages = [(kp, p) for kp in range(len(ktiles) // 2) for p in range(2)]
                    oacc = [4, 5]
                    sc_state = {}

                    def emit_scores(idx, stg_):
                        kp, p = stg_
                        bpair = 2 * (idx % 2)
                        kts = [ktiles[2 * kp], ktiles[2 * kp + 1]]

                        def fs(e):
                            last_ = None
                            for jj in range(2):
                                last_ = e.matmul(PB(bpair + jj, n), Kz[:, 2 * h + p, kts[jj] * 128:(kts[jj] + 1) * 128],
                                                 qT[:, i, s:s + n], start=True, stop=True)
                            return last_
                        S.op('pe', fs, reads=[R('k', h, kt_ // 2) for kt_ in kts] + [R('q', i, u) for u in units(s, n)],
                             writes=[R('ps', bpair), R('ps', bpair + 1)])
                        pi = nxt('pt', NPT)
                        S.op('act', lambda e: e.activation(
                            out=PT[pi][:, :, 0:n], in_=ps[:, bpair:bpair + 2, 0:n], func=AF.Exp, scale=8.0),
                            reads=[R('ps', bpair), R('ps', bpair + 1)], writes=[R('pt', pi)])
                        sc_state[idx] = pi

                    def emit_pv(idx, stg_):
                        kp, p = stg_
                        pi = sc_state[idx]
                        c0 = 64 if p == 0 else 0
                        kts = [ktiles[2 * kp], ktiles[2 * kp + 1]]
                        nkp = len(ktiles) // 2

                        def fp(e):
                            last_ = None
                            for jj in range(2):
                                last_ = e.matmul(PB(oacc[p], n), Vaug[:, kts[jj], h, c0:c0 + 128], PT[pi][:, jj, 0:n],
                                                 start=(kp == 0 and jj == 0), stop=(kp == nkp - 1 and jj == 1))
                            return last_
                        S.op('pe', fp, reads=[R('v', kt_) for kt_ in kts] + [R('pt', pi)], writes=[R('ps', oacc[p])])
                    for idx, stg_ in enumerate(stages):
                        emit_scores(idx, stg_)
                        if idx > 0:
                            emit_pv(idx - 1, stages[idx - 1])
                        conv_step(1)
                    emit_pv(len(stages) - 1, stages[-1])
                    for p in range(2):
                        srow = slice(64, 128) if p == 0 else slice(0, 64)
                        orow = slice(0, 64) if p == 0 else slice(64, 128)
                        ob = oacc[p]
                        ti = 4 + p
                        S.op('act', lambda e: e.activation(out=tmp[ti][:, 0:n], in_=ps[:, ob, 0:n], func=AF.Copy),
                             reads=[R('ps', ob)], writes=[R('tmp', ti)])
                        S.op('dve', lambda e: e.reciprocal(out=RR[p][srow, 0:n], in_=tmp[ti][srow, 0:n]),
                             reads=[R('tmp', ti)], writes=[R('rr', p)])
                        S.op('sp', lambda e: [e.dma_start(out=RR[p][orow, 0:n], in_=RR[p][srow, 0:n])],
                             reads=[R('rr', p)], writes=[R('rrm', p)], lane=lane_rr[p], ndma=1)
                        S.op('dve', lambda e: e.tensor_tensor(
                            out=hT[orow, i, s:s + n], in0=tmp[ti][orow, 0:n], in1=RR[p][orow, 0:n], op=ALU.mult),
                            reads=[R('tmp', ti), R('rrm', p)], writes=hres(i, s, n))
            conv_step(10 ** 6)

            if l == 0:
                allh = [R('h', k, u) for k in range(KT) for u in range(9)]
                tap(hT[:, 0, :], T, allh)
                tap(hT[:, 4, :], T, allh)
            S.op('pool', lambda e: [e.dma_start(out=woA[:, :, :], in_=wcols(wout_d, l, 0, 512))] +
                 [e.dma_start(out=woB[q_][:, :, :], in_=wout_d[l, 2 * q_ * P:(2 * q_ + 2) * P, 512:1024]
                              .rearrange("(k p) c -> p k c", p=P)) for q_ in range(4)],
                 writes=[R('cy', j_) for j_ in range(4)] + [R('pt', 0), R('pt', 1), R('rr', 0), R('rr', 1),
                                                            R('rrm', 0), R('rrm', 1), R('wo')],
                 lane=lane_wo, ndma=5)
            S.barrier()
            S.op('pool', lambda e: [e.dma_start(out=w2s[:, 0:11, :],
                                                in_=wf2_d[l, 0:11 * P, :].rearrange("(f p) c -> p f c", p=P)),
                                    e.dma_start(out=w2s[:, 11:22, :],
                                                in_=wf2_d[l, 11 * P:22 * P, :].rearrange("(f p) c -> p f c", p=P))],
                 writes=[R('w2')], lane=lane_w2, ndma=2)
            ones = cmat[:, 0, :]
            sbk = [0]

            def ln_job(s, n, col, mm_fn, gate_off, outfn_maker, out_dst):
                st = {}

                def A():
                    si = stage_in(xa_d, s, n)
                    st['si'] = si
                    for ot in range(KT):
                        bank = nxt('bank', 4)
                        S.op('pe', lambda e: mm_fn(e, ot, bank), reads=st_reads(ot), writes=[R('ps', bank)])
                        S.op('dve', lambda e: e.scalar_tensor_tensor(
                            out=stg[si][:, ot, 0:n], in0=PB(bank, n), scalar=modT[sl][:, gate_off + ot, col:col + 1],
                            in1=stg[si][:, ot, 0:n], op0=ALU.mult, op1=ALU.add),
                            reads=[R('ps', bank), R('mod', sl), R('stg', si, ot)], writes=[R('stg', si, ot)])
                st_reads = mm_fn.reads

                def B1():
                    si = st['si']
                    b1, b2 = [(6, 7), (4, 5)][sbk[0] % 2]
                    sbk[0] += 1
                    st['banks'] = (b1, b2)

                    def f1(e):
                        last_ = None
                        for k in range(KT):
                            last_ = e.matmul(PB(b1, n), ones, stg[si][:, k, 0:n], start=(k == 0), stop=(k == KT - 1))
                        return last_
                    S.op('pe', f1, reads=[R('stg', si, k) for k in range(KT)] + [R('const')], writes=[R('ps', b1)])
                    for k in range(KT):
                        ti = nxt('sqb', 2)
                        S.op('act', lambda e: e.activation(out=sqb[ti][:, 0:n], in_=stg[si][:, k, 0:n], func=AF.Square),
                             reads=[R('stg', si, k)], writes=[R('sqb', ti)])
                        S.op('pe', lambda e: e.matmul(PB(b2, n), onesb[:, :], sqb[ti][:, 0:n], start=(k == 0),
                                                      stop=(k == KT - 1)),
                             reads=[R('sqb', ti), R('onesb')], writes=[R('ps', b2)])

                def B2():
                    si = st['si']
                    b1, b2 = st['banks']
                    tm = nxt('tmp', NTMP)
                    tr = nxt('tmp', NTMP)
                    inv = 1.0 / 1024.0
                    S.op('dve', lambda e: e.tensor_scalar(out=tmp[tm][:, 0:n], in0=PB(b1, n), scalar1=inv, scalar2=None,
                                                          op0=ALU.mult),
                         reads=[R('ps', b1)], writes=[R('tmp', tm)])
                    S.op('dve', lambda e: e.tensor_tensor(out=tmp[tr][:, 0:n], in0=tmp[tm][:, 0:n], in1=tmp[tm][:, 0:n],
                                                          op=ALU.mult),
                         reads=[R('tmp', tm)], writes=[R('tmp', tr)])
                    S.op('dve', lambda e: e.scalar_tensor_tensor(out=tmp[tr][:, 0:n], in0=PB(b2, n), scalar=inv,
                                                                 in1=tmp[tr][:, 0:n], op0=ALU.mult, op1=ALU.subtract),
                         reads=[R('ps', b2), R('tmp', tr)], writes=[R('tmp', tr)])
                    S.op('act', lambda e: e.activation(out=tmp[tr][:, 0:n], in_=tmp[tr][:, 0:n], func=AF.Ln,
                                                       bias=epsc[:, 0:1]),
                         reads=[R('tmp', tr), R('epsc')], writes=[R('tmp', tr)])
                    S.op('act', lambda e: e.activation(out=tmp[tr][:, 0:n], in_=tmp[tr][:, 0:n], func=AF.Exp, scale=-0.5),
                         reads=[R('tmp', tr)], writes=[R('tmp', tr)])
                    outfn = outfn_maker(si)
                    for k in range(KT):
                        S.op('dve', lambda e: e.tensor_tensor(out=stg[si][:, k, 0:n], in0=stg[si][:, k, 0:n],
                                                              in1=tmp[tm][:, 0:n], op=ALU.subtract),
                             reads=[R('stg', si, k), R('tmp', tm)], writes=[R('stg', si, k)])
                        S.op('dve', lambda e: e.tensor_tensor(out=stg[si][:, k, 0:n], in0=stg[si][:, k, 0:n],
                                                              in1=tmp[tr][:, 0:n], op=ALU.mult),
                             reads=[R('stg', si, k), R('tmp', tr)], writes=[R('stg', si, k)])
                        outfn(k)
                    if out_dst is out_d:
                        stage_out(out_d, si, s - LC, n, s)
                    else:
                        stage_out(xa_d, si, s, n, s)
                return A, B1, B2

            def pipeline_steps(jobs):
                m = len(jobs)
                steps = []
                for i_ in range(m + 2):
                    fs = []
                    if i_ < m:
                        fs.append(jobs[i_][0])
                    if 0 <= i_ - 1 < m:
                        fs.append(jobs[i_ - 1][1])
                    if 0 <= i_ - 2 < m:
                        fs.append(jobs[i_ - 2][2])
                    steps.append(fs)
                return steps

            def run_step(fs):
                for f_ in fs:
                    f_()

            def mk_wout_mm(s, n):
                def mm(e, ot, bank):
                    c0 = (ot % 4) * 128
                    last_ = None
                    for k in range(KT):
                        w_ap = woA[:, k, c0:c0 + 128] if ot < 4 else woB[k // 2][:, k % 2, c0:c0 + 128]
                        last_ = e.matmul(PB(bank, n), w_ap, hT[:, k, s:s + n], start=(k == 0), stop=(k == KT - 1))
                    return last_
                mm.reads = lambda ot: [R('wo')] + [r for k in range(KT) for r in hres(k, s, n)]
                return mm
            ln1_jobs = []
            for ci, (s, n) in enumerate(LNCH):
                if last and ci == 0:
                    continue
                col = 1 if ci == 0 else 0
                ln1_jobs.append(ln_job(
                    s, n, col, mk_wout_mm(s, n), 16,
                    lambda si, s=s, n=n, col=col: ln_outputs(si, s, n, sl, 1, lambda k: pal[:, l, k:k + 1],
                                                             lambda k: pal[:, l, 8 + k:9 + k], col),
                    xa_d))
            lnq = pipeline_steps(ln1_jobs)
            n_pre = 5 if not last else 4
            for _ in range(n_pre):
                run_step(lnq.pop(0))

            for gi, grp in enumerate(GROUPS):
                gch = [(ci, CHUNKS[ci]) for ci in grp if not (last and ci == 0)]
                g0 = gch[0][1][0]
                g1 = gch[-1][1][0] + gch[-1][1][1]
                for blk in range(11):
                    def dm_F(slot, blk=blk):
                        return [(slot[:, :, 0:256], wcols(wf1_d, l, blk * 256, 256)),
                                (slot[:, :, 256:512], wcols(wf3_d, l, blk * 256, 256))]
                    si = wload(dm_F, 2)
                    for f2 in range(2):
                        ft = blk * 2 + f2
                        for ci, (s, n) in gch:
                            b1 = nxt('bank', 4)
                            b3 = nxt('bank', 4)
                            proj(si, f2 * 128, s, n, b1)
                            proj(si, 256 + f2 * 128, s, n, b3)
                            ti = nxt('tmp', NTMP)
                            S.op('act', lambda e: e.activation(out=tmp[ti][:, 0:n], in_=PB(b1, n), func=AF.Silu),
                                 reads=[R('ps', b1)], writes=[R('tmp', ti)])
                            lo = s - g0
                            S.op('dve', lambda e: e.tensor_tensor(
                                out=act[:, ft, lo:lo + n], in0=PB(b3, n), in1=tmp[ti][:, 0:n], op=ALU.mult),
                                reads=[R('ps', b3), R('tmp', ti)],
                                writes=[R('act', ft, lo // 256), R('act', ft, (lo + n - 1) // 256)])
                    if lnq and (gi > 0 or blk % 2 == 1 or len(lnq) > 11 - blk):
                        run_step(lnq.pop(0))
                while lnq:
                    run_step(lnq.pop(0))
                jobs = []
                for (s, n) in LNCH:
                    if s < g0 or s >= g1:
                        continue
                    col = 1 if s == 0 else 0
                    lo = s - g0

                    def mk_w2_mm(lo=lo, n=n):
                        def mm(e, ot, bank):
                            last_ = None
                            for f in range(FT):
                                last_ = e.matmul(PB(bank, n), w2s[:, f, ot * 128:(ot + 1) * 128], act[:, f, lo:lo + n],
                                                 start=(f == 0), stop=(f == FT - 1))
                            return last_
                        mm.reads = lambda ot: [R('w2')] + [R('act', f, lo // 256) for f in range(FT)]
                        return mm
                    if last:
                        om = lambda si, s=s, n=n, col=col: ln_outputs(
                            si, s, n, None, None, None, None, col, final=True,
                            g_ap=lambda k: par[:, l, O_LN2G + k:O_LN2G + k + 1],
                            b_ap=lambda k: par[:, l, O_LN2B + k:O_LN2B + k + 1])
                    else:
                        om = lambda si, s=s, n=n, col=col: ln_outputs(
                            si, s, n, (l + 1) % 2, 0, lambda k: pal[:, l, 16 + k:17 + k],
                            lambda k: pal[:, l, 24 + k:25 + k], col, skip_h=(l == nlayers - 1))
                    jobs.append(ln_job(s, n, col, mk_w2_mm(), 40, om, out_d if last else xa_d))
                for jb in jobs:
                    jb[0]()
                m_ = len(jobs)
                lnq = []
                for i_ in range(m_ + 1):
                    fs = []
                    if i_ < m_:
                        fs.append(jobs[i_][1])
                    if 0 <= i_ - 1 < m_:
                        fs.append(jobs[i_ - 1][2])
                    lnq.append(fs)
                if gi == len(GROUPS) - 1:
                    while lnq:
                        run_step(lnq.pop(0))
            if l == nlayers - 1 and not last:
                S.barrier()
                for (s, n) in LNCH[1:]:
                    si = stage_in(xa_d, s, n)
                    stage_out(out_d, si, s - LC, n, s)

        S.final_wait('sp', stg_out + [lane_dbg])

        with nc.Block() as block:
            @block.tensor
            def _(e):
                for f in S.progs['pe']:
                    f(e)

            @block.scalar
            def _(e):
                for f in S.progs['act']:
                    f(e)

            @block.vector
            def _(e):
                for f in S.progs['dve']:
                    f(e)

            @block.gpsimd
            def _(e):
                for f in S.progs['pool']:
                    f(e)

            @block.sync
            def _(e):
                for f in S.progs['sp']:
                    f(e)
    return nc


def _host_consts():
    ones = np.ones((P, P), np.float32)
    sw = np.zeros((P, P), np.float32)
    for i in range(P):
        sw[i, (i + 64) % P] = 1.0
    rot = np.zeros((P, P), np.float32)
    for blk in range(4):
        for f in range(16):
            a = blk * 32 + f
            rot[a + 16, a] = -1.0
            rot[a, a + 16] = 1.0
    bd = np.zeros((P, P), np.float32)
    bd[:64, :64] = 1.0
    bd[64:, 64:] = 1.0
    cmat = np.concatenate([ones, sw, rot, bd, np.eye(P, dtype=np.float32)], axis=1)
    tC = np.ones((P, T), np.float32)
    tS = np.zeros((P, T), np.float32)
    tok = np.arange(NL)
    pos = np.stack([tok // 64, tok % 64], 0).astype(np.float32)
    freqs = (10000.0 ** (-np.arange(16, dtype=np.float32) / 16.0)).astype(np.float32)
    for p_ in range(P):
        d = p_ % 64
        ang = (pos[d // 32] * freqs[d % 16]).astype(np.float32)
        tC[p_, LC:] = np.cos(ang)
        tS[p_, LC:] = np.sin(ang)
    tabs = np.concatenate([tC, tS], axis=1)
    return cmat, tabs


def _pack_params(b_ada, q_norm_g, k_norm_g, dw_w, dw_b, conv_ln_g, conv_ln_b, ln1_g, ln1_b, ln2_g, ln2_b):
    par = np.zeros((P, DEPTH, NPAR), np.float32)
    for l in range(DEPTH):
        par[:, l, 0:48] = b_ada[l].reshape(48, P).T
        par[:, l, 48:56] = ln1_g[l].reshape(8, P).T
        par[:, l, 56:64] = ln1_b[l].reshape(8, P).T
        par[:, l, 64:72] = ln2_g[l].reshape(8, P).T
        par[:, l, 72:80] = ln2_b[l].reshape(8, P).T
        w = dw_w[l, :, 0, :]
        for j in range(4):
            par[:, l, 80 + j * 31:80 + (j + 1) * 31] = w[:, j * P:(j + 1) * P].T
        par[:, l, 204:208] = dw_b[l].reshape(4, P).T
        par[:, l, 208:212] = conv_ln_g[l].reshape(4, P).T
        par[:, l, 212:216] = conv_ln_b[l].reshape(4, P).T
        par[:, l, 216] = np.tile(q_norm_g[l], 2)
        par[:, l, 217] = np.tile(k_norm_g[l], 2)
    return par.reshape(P, DEPTH * NPAR)


_NC_CACHE = {}


def kernel(x, c, ctx, c_ctx, w_ada, b_ada, w_in, q_norm_g, k_norm_g, dw_w, dw_b,
           conv_ln_g, conv_ln_b, w_out, ln1_g, ln1_b, w_ff1, w_ff3, w_ff2, ln2_g, ln2_b,
           _nlayers=DEPTH, _dbg=None):
    f = lambda a: np.ascontiguousarray(np.asarray(a, dtype=np.float32))
    x, c, ctx, c_ctx = f(x), f(c), f(ctx), f(c_ctx)
    cmat, tabs = _host_consts()
    par = _pack_params(f(b_ada), f(q_norm_g), f(k_norm_g), f(dw_w), f(dw_b), f(conv_ln_g), f(conv_ln_b),
                       f(ln1_g), f(ln1_b), f(ln2_g), f(ln2_b))
    key = (_nlayers, _dbg)
    if key not in _NC_CACHE:
        _NC_CACHE[key] = build(_nlayers, _dbg)
    nc = _NC_CACHE[key]
    shared = {"par": par, "cmat": cmat, "tabs": tabs, "w_ada": f(w_ada), "w_in": f(w_in), "w_out": f(w_out),
              "w_ff1": f(w_ff1), "w_ff3": f(w_ff3), "w_ff2": f(w_ff2)}
    in_maps = []
    for b in range(8):
        xT = np.ascontiguousarray(np.concatenate([ctx[b], x[b]], axis=0).T)
        cs = np.ascontiguousarray(np.stack([c[b], c_ctx], -1).reshape(KT, P, 2).transpose(1, 0, 2).reshape(P, KT * 2))
        m = dict(shared)
        m["xT"] = xT
        m["cs"] = cs
        in_maps.append(m)
    res = run_bass_kernel_spmd(nc, in_maps, core_ids=list(range(8)))
    out = np.stack([np.ascontiguousarray(r["outT"].T) for r in res.results], axis=0)
    if _dbg:
        kernel.dbg = [r["dbg"] for r in res.results]
    return out.astype(np.float32)
```

```python
import numpy as np
from contextlib import ExitStack
import concourse.bass as bass
import concourse.mybir as mybir
from concourse.bass_utils import run_bass_kernel_spmd

F32 = mybir.dt.float32
BF16 = mybir.dt.bfloat16
AF = mybir.ActivationFunctionType
ALU = mybir.AluOpType

P = 128
D = 1024
KT = 8
T = 2304
LC = 256
NL = 2048
DEPTH = 4
DFF = 2816
FT = 22
ALPHA = float(8 ** 0.25)
EPS = 1e-6
CHUNKS = [(0, 256), (256, 512), (768, 512), (1280, 256), (1536, 512), (2048, 256)]
GROUPS = [[0, 1], [2, 3], [4, 5]]
QCH = [(0, 256), (256, 512), (768, 512), (1280, 512), (1792, 512)]
LNCH = [(i * 256, 256) for i in range(9)]
APAD = 286 + 2078
NPAR = 218
ENG = ['pe', 'act', 'dve', 'pool', 'sp']


def units(s, n):
    return range(s // 256, (s + n - 1) // 256 + 1)


def aidx(tok):
    return 15 + tok if tok < 256 else 45 + tok


class _RecInst:
    def __init__(self, idx):
        self.idx = idx


class _Rec:
    def __init__(self):
        self.calls = []

    def __getattr__(self, name):
        def f(*a, **k):
            self.calls.append((name, a, k))
            return _RecInst(len(self.calls) - 1)
        return f


class Lane:
    def __init__(self, sem):
        self.sem = sem
        self.val = 0


class Res:
    __slots__ = ('w', 'r')

    def __init__(self):
        self.w = None
        self.r = {}


class Sched:
    def __init__(self, nc, stack):
        self.nc = nc
        self.stack = stack
        self.progs = {e: [] for e in ENG}
        self.lane = {e: self.newlane('L' + e) for e in ENG}
        self.known = {e: {} for e in ENG}
        self.res = {}
        self.dlanes = []

    def newlane(self, name):
        return Lane(self.stack.enter_context(self.nc.semaphore(name)))

    def dmalane(self, name):
        ln = self.newlane(name)
        self.dlanes.append(ln)
        return ln

    def R(self, *key):
        r = self.res.get(key)
        if r is None:
            r = self.res[key] = Res()
        return r

    def op(self, eng, fn, reads=(), writes=(), lane=None, ndma=0):
        waits = {}

        def need(ev):
            if ev is None:
                return
            ln, v = ev
            if waits.get(ln, 0) < v:
                waits[ln] = v
        for r in reads:
            need(r.w)
        for r in writes:
            need(r.w)
            for ln, v in r.r.items():
                need((ln, v))
        k = self.known[eng]
        wl = []
        for ln, v in waits.items():
            if k.get(ln, 0) < v:
                k[ln] = v
                wl.append((ln.sem, v))
        if lane is None:
            lane = self.lane[eng]
            lane.val += 1
            inc = 1
        else:
            lane.val += 16 * ndma
            inc = 16
        ev = (lane, lane.val)
        for r in reads:
            if r.r.get(lane, 0) < lane.val:
                r.r[lane] = lane.val
        for r in writes:
            r.w = ev
            r.r = {}
        sem = lane.sem
        rec = _Rec()
        out = fn(rec)
        calls = rec.calls
        inc_idx = [o_.idx for o_ in out] if inc == 16 else [out.idx]

        def run(e):
            for s_, v_ in wl:
                e.wait_ge(s_, v_)
            insts = [getattr(e, nm)(*a, **kw) for (nm, a, kw) in calls]
            for i_ in inc_idx:
                insts[i_].then_inc(sem, inc)
        self.progs[eng].append(run)

    def barrier(self, exclude=()):
        for e in ENG:
            k = self.known[e]
            wl = []
            for ln in [self.lane[x] for x in ENG if x != e] + [d_ for d_ in self.dlanes if d_ not in exclude]:
                if k.get(ln, 0) < ln.val:
                    k[ln] = ln.val
                    wl.append((ln.sem, ln.val))

            def run(en, wl=wl):
                for s_, v_ in wl:
                    en.wait_ge(s_, v_)
            self.progs[e].append(run)
        for r in self.res.values():
            r.w = None
            r.r = {}

    def final_wait(self, eng, lanes):
        wl = [(ln.sem, ln.val) for ln in lanes]

        def run(en):
            for s_, v_ in wl:
                en.wait_ge(s_, v_)
        self.progs[eng].append(run)


def build(nlayers=DEPTH, dbg=None):
    nc = bass.Bass("TRN2", target_bir_lowering=False)
    dt = nc.dram_tensor
    xT_d = dt("xT", [D, T], F32, kind="ExternalInput").ap()
    cs_d = dt("cs", [P, KT * 2], F32, kind="ExternalInput").ap()
    par_d = dt("par", [P, DEPTH * NPAR], F32, kind="ExternalInput").ap()
    cm_d = dt("cmat", [P, 5 * P], F32, kind="ExternalInput").ap()
    tab_d = dt("tabs", [P, 2 * T], F32, kind="ExternalInput").ap()
    wada_d = dt("w_ada", [DEPTH, D, 6 * D], F32, kind="ExternalInput").ap()
    win_d = dt("w_in", [DEPTH, D, 1792], F32, kind="ExternalInput").ap()
    wout_d = dt("w_out", [DEPTH, D, D], F32, kind="ExternalInput").ap()
    wf1_d = dt("w_ff1", [DEPTH, D, DFF], F32, kind="ExternalInput").ap()
    wf3_d = dt("w_ff3", [DEPTH, D, DFF], F32, kind="ExternalInput").ap()
    wf2_d = dt("w_ff2", [DEPTH, DFF, D], F32, kind="ExternalInput").ap()
    out_d = dt("outT", [D, NL], F32, kind="ExternalOutput").ap()
    xa_d = dt("xa_scr", [D, T], F32).ap()
    dbg_d = None
    if dbg:
        dbg_d = dt("dbg", [P, dbg], F32, kind="ExternalOutput").ap()

    stack = ExitStack()
    with stack:
        def sb(name, shape, dtp):
            return stack.enter_context(nc.sbuf_tensor(name, shape, dtp))
        S = Sched(nc, stack)
        R = S.R
        hT = sb("hT", [P, KT, T], BF16)
        BIGN = 39424
        big = sb("big", [P, BIGN], BF16)
        qT = big[:, 0:9216].rearrange("p (a t) -> p a t", a=4)
        Kz = big[:, 9216:18432].rearrange("p (a t) -> p a t", a=4)
        Vaug = big[:, 18432:25344].rearrange("p (t h c) -> p t h c", t=18, h=2)
        abuf = big[:, 25344:34800].rearrange("p (a t) -> p a t", a=4)
        tabC = big[:, 34800:37104]
        tabS = big[:, 37104:39408]
        act = big[:, 0:16896].rearrange("p (f t) -> p f t", f=FT)
        w2s = big[:, 16896:39424].rearrange("p (f c) -> p f c", f=FT)
        NSLOT = 3
        wsl = [sb(f"ws{i}", [P, KT, 512], BF16) for i in range(NSLOT)]
        wsl_lane = [S.dmalane(f"wsl{i}") for i in range(NSLOT)]
        NSTG = 4
        stg = [sb(f"stg{i}", [P, KT, 256], F32) for i in range(NSTG)]
        stg_in = [S.dmalane(f"sti{i}") for i in range(NSTG)]
        stg_out = [S.dmalane(f"sto{i}") for i in range(NSTG)]
        cmat = sb("cmatsb", [P, 3, P], F32)
        bdm = sb("bdm", [P, P], BF16)
        identb = sb("identb", [P, P], BF16)
        onesb = sb("onesb", [P, P], BF16)
        stgb = [stg[i][:, :, :].bitcast(BF16) for i in range(NSTG)]

        def dgv(d):
            return stgb[d // 32][:, (d % 32) // 4, (d % 4) * 128:(d % 4) * 128 + 128]

        def dgres(d):
            return R('stg', d // 32, (d % 32) // 4)
        par = sb("parsb", [P, DEPTH, NPAR], F32)
        pal = sb("pal", [P, DEPTH, 32], F32)
        cs = sb("cssb", [P, KT, 2], F32)
        epsc = sb("epsc", [P, 2], F32)
        sbf = sb("sbf", [P, KT, 2], BF16)
        modT = [sb(f"modT{i}", [P, 48, 2], F32) for i in range(2)]
        onesc = [sb(f"onesc{i}", [P, 2, KT, 2], F32) for i in range(2)]
        hA = [sb(f"hA{i}", [P, 2, KT, 2], F32) for i in range(2)]
        hB = [sb(f"hB{i}", [P, 2, KT, 2], F32) for i in range(2)]
        NTMP = 6
        tmp = [sb(f"tmp{i}", [P, 512], F32) for i in range(NTMP)]
        sqb = [sb(f"sqb{i}", [P, 512], BF16) for i in range(2)]
        NPT = 2
        PT = [sb(f"pt{i}", [P, 2, 512], BF16) for i in range(NPT)]
        convy = sb("convy", [P, 4, 512], F32)
        RR = [sb(f"rr{i}", [P, 512], F32) for i in range(2)]
        ps = stack.enter_context(nc.psum_tensor("ps", [P, 8, 512], F32))
        lane_const = S.dmalane("const")
        lane_tab = S.dmalane("tab")
        lane_bdm = S.dmalane("bdm")
        lane_rr = [S.dmalane("rr0"), S.dmalane("rr1")]
        lane_wo = S.dmalane("wo")
        woA = convy[:, :, :].bitcast(BF16).rearrange("p a (b c) -> p (a b) c", b=2)
        woB = [PT[0], PT[1], RR[0][:, :].bitcast(BF16).rearrange("p (a c) -> p a c", a=2),
               RR[1][:, :].bitcast(BF16).rearrange("p (a c) -> p a c", a=2)]
        lane_w2 = S.dmalane("w2")
        lane_dbg = S.dmalane("dbg")

        cnt = {'tmp': 0, 'sqb': 0, 'pt': 0, 'ws': 0, 'stg': 0, 'bank': 0, 'dg': 0, 'cb': 0}

        def nxt(kind, n):
            i = cnt[kind] % n
            cnt[kind] += 1
            return i

        def PB(b, n):
            return ps[:, b, 0:n]

        def ld_const(e):
            return [e.dma_start(out=cmat[:, :, :], in_=cm_d[:, 0:3 * P].rearrange("p (a c) -> p a c", a=3)),
                    e.dma_start(out=par[:, :, :], in_=par_d.rearrange("p (l c) -> p l c", l=DEPTH)),
                    e.dma_start(out=cs[:, :, :], in_=cs_d.rearrange("p (k c) -> p k c", k=KT))]
        S.op('sp', ld_const, writes=[R('const')], lane=lane_const, ndma=3)
        S.op('pool', lambda e: [e.dma_start(out=bdm[:, :], in_=cm_d[:, 3 * P:4 * P]),
                                e.dma_start(out=identb[:, :], in_=cm_d[:, 4 * P:5 * P])],
             writes=[R('bdm')], lane=lane_bdm, ndma=2)
        S.op('act', lambda e: e.activation(out=sbf[:, :, :], in_=cs[:, :, :], func=AF.Silu),
             reads=[R('const')], writes=[R('sbf')])
        S.op('dve', lambda e: e.tensor_scalar(out=pal[:, :, :], in0=par[:, :, 48:80], scalar1=ALPHA,
                                              scalar2=None, op0=ALU.mult),
             reads=[R('const')], writes=[R('pal')])
        S.op('dve', lambda e: e.memset(onesb[:, :], 1.0), writes=[R('onesb')])
        S.op('dve', lambda e: e.memset(epsc[:, 0:1], EPS), writes=[R('epsc')])
        S.op('dve', lambda e: e.memset(epsc[:, 1:2], 64 * EPS), writes=[R('epsc')])
        S.op('dve', lambda e: e.memset(RR[0][:, :], 0.0), writes=[R('rr', 0)])
        S.op('dve', lambda e: e.memset(RR[1][:, :], 0.0), writes=[R('rr', 1)])

        def pc(l, off, n=1):
            return par[:, l, off:off + n]
        O_BADA, O_LN1G, O_LN1B, O_LN2G, O_LN2B, O_DWW, O_DWB, O_CLG, O_CLB, O_QG, O_KG = \
            0, 48, 56, 64, 72, 80, 204, 208, 212, 216, 217

        def wload(dmas, nd):
            si = nxt('ws', NSLOT)
            pairs = dmas(wsl[si])

            def fn(e):
                return [e.dma_start(out=o, in_=i) for (o, i) in pairs]
            S.op('pool', fn, writes=[R('ws', si)], lane=wsl_lane[si], ndma=len(pairs))
            return si

        def wcols(wd, l, c0, w):
            return wd[l, :, c0:c0 + w].rearrange("(k p) c -> p k c", p=P)

        def mods_gen(l):
            m = modT[l % 2]
            mres = R('mod', l % 2)
            bank = 7
            for blk in range(12):
                si = wload(lambda slot: [(slot[:, :, :], wcols(wada_d, l, blk * 512, 512))], 1)

                def fn(e):
                    last_ = None
                    for f4 in range(4):
                        for k in range(KT):
                            last_ = e.matmul(ps[:, bank, 2 * f4:2 * f4 + 2], wsl[si][:, k, f4 * 128:(f4 + 1) * 128],
                                             sbf[:, k, :], start=(k == 0), stop=(k == KT - 1))
                    return last_
                S.op('pe', fn, reads=[R('ws', si), R('sbf')], writes=[R('ps', bank)])
                for col in range(2):
                    S.op('dve', lambda e: e.tensor_tensor(
                        out=m[:, blk * 4:blk * 4 + 4, col],
                        in0=ps[:, bank, 0:8].rearrange("p (f c) -> p f c", c=2)[:, :, col],
                        in1=par[:, l, blk * 4:blk * 4 + 4], op=ALU.add),
                        reads=[R('ps', bank), R('const')], writes=[mres])
                yield
            osc = onesc[l % 2]
            for j, g in enumerate((1, 4)):
                S.op('dve', lambda e: e.tensor_scalar(
                    out=osc[:, j, :, :], in0=m[:, g * 8:(g + 1) * 8, :], scalar1=1.0, scalar2=None, op0=ALU.add),
                    reads=[mres], writes=[R('osc', l % 2)])

        def emit_mods(l):
            for _ in mods_gen(l):
                pass

        def emit_hcoef(l, which, goff, boff, lprev):
            m = modT[l % 2]
            osc = onesc[l % 2]
            A = hA[l % 2]
            B = hB[l % 2]
            shg = 0 if which == 0 else 3
            rd = [R('mod', l % 2), R('osc', l % 2), R('const')]
            wr = [R('hc', l % 2, which)]
            for col in range(2):
                if lprev is None:
                    S.op('dve', lambda e, col=col: e.tensor_copy(out=A[:, which, :, col], in_=osc[:, which, :, col]),
                         reads=rd, writes=wr)
                    S.op('dve', lambda e, col=col: e.tensor_copy(out=B[:, which, :, col],
                                                                  in_=m[:, shg * 8:shg * 8 + 8, col]),
                         reads=rd, writes=wr)
                else:
                    S.op('dve', lambda e, col=col: e.tensor_tensor(
                        out=A[:, which, :, col], in0=osc[:, which, :, col], in1=par[:, lprev, goff:goff + 8],
                        op=ALU.mult), reads=rd, writes=wr)
                    S.op('dve', lambda e, col=col: e.tensor_tensor(
                        out=B[:, which, :, col], in0=osc[:, which, :, col], in1=par[:, lprev, boff:boff + 8],
                        op=ALU.mult), reads=rd, writes=wr)
                    S.op('dve', lambda e, col=col: e.tensor_tensor(
                        out=B[:, which, :, col], in0=B[:, which, :, col], in1=m[:, shg * 8:shg * 8 + 8, col],
                        op=ALU.add), reads=rd + wr, writes=wr)

        def layernorm(nt, vt, vres, n, nfeat, outfn, banks=(6, 7)):
            b1, b2 = banks
            ones = cmat[:, 0, :]
            def f1(e):
                last = None
                for k in range(nt):
                    last = e.matmul(PB(b1, n), ones, vt(k), start=(k == 0), stop=(k == nt - 1))
                return last
            S.op('pe', f1, reads=[vres(k) for k in range(nt)] + [R('const')], writes=[R('ps', b1)])
            for k in range(nt):
                ti = nxt('tmp', NTMP)
                S.op('act', lambda e, k=k, ti=ti: e.activation(out=tmp[ti][:, 0:n], in_=vt(k), func=AF.Square),
                     reads=[vres(k)], writes=[R('tmp', ti)])
                S.op('pe', lambda e, k=k, ti=ti: e.matmul(PB(b2, n), ones, tmp[ti][:, 0:n], start=(k == 0),
                                                          stop=(k == nt - 1)),
                     reads=[R('tmp', ti), R('const')], writes=[R('ps', b2)])
            tm = nxt('tmp', NTMP)
            tr = nxt('tmp', NTMP)
            inv = 1.0 / nfeat
            S.op('dve', lambda e: e.tensor_scalar(out=tmp[tm][:, 0:n], in0=PB(b1, n), scalar1=inv, scalar2=None,
                                                  op0=ALU.mult),
                 reads=[R('ps', b1)], writes=[R('tmp', tm)])
            S.op('dve', lambda e: e.tensor_tensor(out=tmp[tr][:, 0:n], in0=tmp[tm][:, 0:n], in1=tmp[tm][:, 0:n],
                                                  op=ALU.mult),
                 reads=[R('tmp', tm)], writes=[R('tmp', tr)])
            S.op('dve', lambda e: e.scalar_tensor_tensor(out=tmp[tr][:, 0:n], in0=PB(b2, n), scalar=inv,
                                                         in1=tmp[tr][:, 0:n], op0=ALU.mult, op1=ALU.subtract),
                 reads=[R('ps', b2), R('tmp', tr)], writes=[R('tmp', tr)])
            S.op('act', lambda e: e.activation(out=tmp[tr][:, 0:n], in_=tmp[tr][:, 0:n], func=AF.Ln, bias=epsc[:, 0:1]),
                 reads=[R('tmp', tr), R('epsc')], writes=[R('tmp', tr)])
            S.op('act', lambda e: e.activation(out=tmp[tr][:, 0:n], in_=tmp[tr][:, 0:n], func=AF.Exp, scale=-0.5),
                 reads=[R('tmp', tr)], writes=[R('tmp', tr)])
            for k in range(nt):
                S.op('dve', lambda e, k=k: e.tensor_tensor(out=vt(k), in0=vt(k), in1=tmp[tm][:, 0:n], op=ALU.subtract),
                     reads=[vres(k), R('tmp', tm)], writes=[vres(k)])
                S.op('dve', lambda e, k=k: e.tensor_tensor(out=vt(k), in0=vt(k), in1=tmp[tr][:, 0:n], op=ALU.mult),
                     reads=[vres(k), R('tmp', tr)], writes=[vres(k)])
                outfn(k)

        dbg_off = [0]

        def tap(ap_f32, n, reads, parts=P):
            if dbg_d is None:
                return
            o = dbg_off[0]
            dbg_off[0] += n
            S.op('pool', lambda e: [e.dma_start(out=dbg_d[0:parts, o:o + n], in_=ap_f32)], reads=reads,
                 lane=lane_dbg, ndma=1)

        def stage_in(src_d, s, n):
            si = nxt('stg', NSTG)
            S.op('sp', lambda e: [e.dma_start(out=stg[si][:, :, 0:n],
                                              in_=src_d[:, s:s + n].rearrange("(k p) t -> p k t", p=P))],
                 reads=[R('xa', u) for u in units(s, n)] if src_d is xa_d else [],
                 writes=[R('stg', si, k) for k in range(KT)], lane=stg_in[si], ndma=1)
            return si

        def stage_out(dst_d, si, s_dst, n, s_tok):
            S.op('act', lambda e: [e.dma_start(out=dst_d[:, s_dst:s_dst + n].rearrange("(k p) t -> p k t", p=P),
                                              in_=stg[si][:, :, 0:n])],
                 reads=[R('stg', si, k) for k in range(KT)],
                 writes=[R('xa', u) for u in units(s_tok, n)] if dst_d is xa_d else [R('outd')],
                 lane=stg_out[si], ndma=1)

        def hres(k, s, n):
            return [R('h', k, u) for u in units(s, n)]

        def ln_outputs(si, s, n, l_next_slot, which, ga_ap, ba_ap, col, final=False, g_ap=None, b_ap=None,
                       skip_h=False):
            def outfn(k):
                t_ap = stg[si][:, k, 0:n]
                tres = R('stg', si, k)
                if final:
                    S.op('act', lambda e: e.activation(out=t_ap, in_=t_ap, func=AF.Identity,
                                                       scale=g_ap(k), bias=b_ap(k)),
                         reads=[tres, R('const')], writes=[tres])
                    return
                A = hA[l_next_slot]
                B = hB[l_next_slot]
                if not skip_h:
                    S.op('act', lambda e: e.activation(out=hT[:, k, s:s + n], in_=t_ap, func=AF.Identity,
                                                       scale=A[:, which, k, col:col + 1],
                                                       bias=B[:, which, k, col:col + 1]),
                         reads=[tres, R('hc', l_next_slot, which)], writes=hres(k, s, n))
                S.op('act', lambda e: e.activation(out=t_ap, in_=t_ap, func=AF.Identity,
                                                   scale=ga_ap(k), bias=ba_ap(k)),
                     reads=[tres, R('pal'), R('const')], writes=[tres])
            return outfn

        emit_mods(0)
        emit_hcoef(0, 0, None, None, None)
        for ci, (s, n) in enumerate(LNCH):
            si = stage_in(xT_d, s, n)
            col = 1 if ci == 0 else 0
            for k in range(KT):
                t_ap = stg[si][:, k, 0:n]
                tres = R('stg', si, k)
                S.op('act', lambda e, t_ap=t_ap, k=k, col=col, s=s, n=n: e.activation(
                    out=hT[:, k, s:s + n], in_=t_ap, func=AF.Identity,
                    scale=hA[0][:, 0, k, col:col + 1], bias=hB[0][:, 0, k, col:col + 1]),
                    reads=[tres, R('hc', 0, 0)], writes=hres(k, s, n))
                S.op('dve', lambda e, t_ap=t_ap: e.tensor_scalar(out=t_ap, in0=t_ap, scalar1=ALPHA, scalar2=None,
                                                                 op0=ALU.mult),
                     reads=[tres], writes=[tres])
            stage_out(xa_d, si, s, n, s)

        def dm_A(slot, ll):
            prs = []
            for h in range(2):
                for d2 in range(2):
                    prs.append((slot[:, :, (2 * h + d2) * 64:(2 * h + d2 + 1) * 64], wcols(win_d, ll, 512 + 64 * h, 64)))
            prs.append((slot[:, :, 256:384], wcols(win_d, ll, 640, 128)))
            return prs
        pre_A = [None]

        for l in range(nlayers):
            last = (l == DEPTH - 1)
            sl = l % 2
            chunks = [c for ci, c in enumerate(CHUNKS) if not (last and ci == 0)]
            if l > 0:
                S.barrier()
            S.op('pool', lambda e: [e.dma_start(out=tabC, in_=tab_d[:, 0:T]),
                                    e.dma_start(out=tabS, in_=tab_d[:, T:2 * T])],
                 writes=[R('tab')], lane=lane_tab, ndma=2)
            S.op('dve', lambda e: e.memset(abuf[:, :, :], 0.0), writes=[R('apad')])
            S.op('dve', lambda e: e.memset(Kz[:, :, :], 0.0), writes=[R('k', h_, u) for h_ in range(2) for u in range(9)])
            S.op('dve', lambda e: e.memset(Vaug[:, :, :, 0:64], 1.0), writes=[R('v', tt) for tt in range(18)])
            S.op('dve', lambda e: e.memset(Vaug[:, :, :, 128:192], 1.0), writes=[R('v', tt) for tt in range(18)])

            dg_next = [0]

            def emit_diags(k_):
                for _ in range(k_):
                    d_ = dg_next[0]
                    if d_ >= 124:
                        return
                    dg_next[0] += 1
                    if (d_ // 4) % 2 == 0:
                        S.op('dve', lambda e: e.tensor_scalar(out=dgv(d_), in0=identb[:, :], scalar1=pc(l, O_DWW + d_),
                                                              scalar2=None, op0=ALU.mult),
                             reads=[R('bdm'), R('const')], writes=[dgres(d_)])
                    else:
                        S.op('act', lambda e: e.activation(out=dgv(d_), in_=identb[:, :], func=AF.Identity,
                                                           scale=pc(l, O_DWW + d_)),
                             reads=[R('bdm'), R('const')], writes=[dgres(d_)])
            rp = [(tmp[i_], R('tmp', i_)) for i_ in range(6)] + \
                 [(convy[:, j_, :], R('cy', j_)) for j_ in range(3)]
            inst_no = [0]
            stageA = []
            stageB = []

            def nr_1b(it):
                bank, s, n, gcol, qi = it['bank'], it['s'], it['n'], it['gcol'], it['qi']
                (T0, r0), (T1, r1), _ = it['tiles']
                b2 = 4 + nxt('bank', 4)
                S.op('pe', lambda e: e.matmul(PB(b2, n), bdm[:, :], sqb[qi][:, 0:n], start=True, stop=True),
                     reads=[R('sqb', qi), R('bdm')], writes=[R('ps', b2)])
                S.op('act', lambda e: e.activation(out=T0[:, 0:n], in_=PB(b2, n), func=AF.Ln, bias=epsc[:, 1:2]),
                     reads=[R('ps', b2), R('epsc')], writes=[r0])
                S.op('act', lambda e: e.activation(out=T0[:, 0:n], in_=T0[:, 0:n], func=AF.Exp, scale=-0.5),
                     reads=[r0], writes=[r0])
                S.op('dve', lambda e: e.scalar_tensor_tensor(out=T1[:, 0:n], in0=PB(bank, n), scalar=gcol,
                                                             in1=T0[:, 0:n], op0=ALU.mult, op1=ALU.mult),
                     reads=[R('ps', bank), r0, R('const')], writes=[r1])

            def nr_2(it):
                s, n, dst_ap, dst_res = it['s'], it['n'], it['dst_ap'], it['dst_res']
                (T0, r0), (T1, r1), (T2, r2) = it['tiles']
                b3 = 4 + nxt('bank', 4)
                S.op('pe', lambda e: e.matmul(PB(b3, n), cmat[:, 2, :], T1[:, 0:n], start=True, stop=True),
                     reads=[r1, R('const')], writes=[R('ps', b3)])
                S.op('dve', lambda e: e.tensor_tensor(out=T0[:, 0:n], in0=T1[:, 0:n], in1=tabC[:, s:s + n], op=ALU.mult),
                     reads=[r1, R('tab')], writes=[r0])
                S.op('dve', lambda e: e.tensor_tensor(out=T2[:, 0:n], in0=PB(b3, n), in1=tabS[:, s:s + n], op=ALU.mult),
                     reads=[R('ps', b3), R('tab')], writes=[r2])
                if isinstance(dst_ap, tuple):
                    for hp, d_ap in enumerate(dst_ap):
                        rows = slice(64 * hp, 64 * hp + 64)
                        S.op('dve', lambda e: e.tensor_tensor(out=d_ap, in0=T0[rows, 0:n], in1=T2[rows, 0:n], op=ALU.add),
                             reads=[r0, r2], writes=dst_res)
                else:
                    S.op('dve', lambda e: e.tensor_tensor(out=dst_ap, in0=T0[:, 0:n], in1=T2[:, 0:n], op=ALU.add),
                         reads=[r0, r2], writes=dst_res)

            def nr_submit(bank, s, n, gcol, dst_ap, dst_res):
                k_ = inst_no[0]
                inst_no[0] += 1
                qi = nxt('sqb', 2)
                S.op('act', lambda e: e.activation(out=sqb[qi][:, 0:n], in_=PB(bank, n), func=AF.Square),
                     reads=[R('ps', bank)], writes=[R('sqb', qi)])
                stageA.append(dict(bank=bank, s=s, n=n, gcol=gcol, qi=qi, dst_ap=dst_ap, dst_res=dst_res,
                                   tiles=[rp[(3 * k_ + j_) % 9] for j_ in range(3)]))
                if len(stageA) > 1:
                    it = stageA.pop(0)
                    nr_1b(it)
                    stageB.append(it)
                if len(stageB) > 1:
                    nr_2(stageB.pop(0))
                emit_diags(4)

            def flush_pend():
                while stageA or stageB:
                    if stageA:
                        it = stageA.pop(0)
                        nr_1b(it)
                        stageB.append(it)
                    if stageB and (len(stageB) > 1 or not stageA):
                        nr_2(stageB.pop(0))

            def proj(si, c0, s, n, bank):
                pairs = [(wsl[si][:, k, c0:c0 + 128], hT[:, k, s:s + n]) for k in range(KT)]

                def fn(e):
                    last_ = None
                    for i, (a_, b_) in enumerate(pairs):
                        last_ = e.matmul(PB(bank, n), a_, b_, start=(i == 0), stop=(i == KT - 1))
                    return last_
                S.op('pe', fn, reads=[R('ws', si)] + [r for k in range(KT) for r in hres(k, s, n)],
                     writes=[R('ps', bank)])

            if pre_A[0] is not None:
                si = pre_A[0]
                pre_A[0] = None
            else:
                si = wload(lambda slot: dm_A(slot, l), 5)
            for h in range(2):
                for (s, n) in CHUNKS:
                    bank = nxt('bank', 4)
                    proj(si, 128 * h, s, n, bank)
                    nr_submit(bank, s, n, pc(l, O_KG), (Kz[0:64, 2 * h, s:s + n], Kz[64:128, 2 * h + 1, s:s + n]),
                              [R('k', h, u) for u in units(s, n)])
            flush_pend()
            for tt in range(18):
                bank = nxt('bank', 4)
                pairs = [(hT[:, k, tt * 128:(tt + 1) * 128], wsl[si][:, k, 256:384]) for k in range(KT)]

                def fnv(e, pairs=pairs, bank=bank):
                    last_ = None
                    for i, (a_, b_) in enumerate(pairs):
                        last_ = e.matmul(PB(bank, 128), a_, b_, start=(i == 0), stop=(i == KT - 1))
                    return last_
                S.op('pe', fnv, reads=[R('ws', si)] + [r for k in range(KT) for r in hres(k, tt * 128, 128)],
                     writes=[R('ps', bank)])
                S.op('act', lambda e, tt=tt, bank=bank: e.activation(
                    out=Vaug[:, tt, :, 64:128], in_=PB(bank, 128).rearrange("p (h c) -> p h c", h=2), func=AF.Copy),
                    reads=[R('ps', bank)], writes=[R('v', tt)])
            for jb in range(2):
                def dm_B(slot, jb=jb):
                    prs = []
                    for jj in range(2):
                        j = 2 * jb + jj
                        prs.append((slot[:, :, jj * 256:jj * 256 + 128], wcols(win_d, l, 768 + 128 * j, 128)))
                        prs.append((slot[:, :, jj * 256 + 128:jj * 256 + 256], wcols(win_d, l, 1280 + 128 * j, 128)))
                    return prs
                si = wload(dm_B, 4)
                for jj in range(2):
                    j = 2 * jb + jj
                    for (s, n) in chunks:
                        bu = nxt('bank', 4)
                        bg = nxt('bank', 4)
                        proj(si, jj * 256, s, n, bu)
                        proj(si, jj * 256 + 128, s, n, bg)
                        ti = nxt('tmp', NTMP)
                        S.op('act', lambda e, bg=bg, ti=ti, n=n: e.activation(out=tmp[ti][:, 0:n], in_=PB(bg, n),
                                                                                func=AF.Sigmoid),
                             reads=[R('ps', bg)], writes=[R('tmp', ti)])
                        a0 = aidx(s)
                        S.op('dve', lambda e, bu=bu, ti=ti, n=n, j=j, a0=a0: e.tensor_tensor(
                            out=abuf[:, j, a0:a0 + n], in0=PB(bu, n), in1=tmp[ti][:, 0:n], op=ALU.mult),
                            reads=[R('ps', bu), R('tmp', ti), R('apad')], writes=[R('a', j, u) for u in units(s, n)])
            si = wload(lambda slot: [(slot[:, :, :], wcols(win_d, l, 0, 512))], 1)
            for i in range(4):
                for (s, n) in chunks:
                    bank = nxt('bank', 4)
                    proj(si, 128 * i, s, n, bank)
                    nr_submit(bank, s, n, pc(l, O_QG), qT[:, i, s:s + n], [R('q', i, u) for u in units(s, n)])
            flush_pend()

            if l == 0:
                allh = [R('h', k, u) for k in range(KT) for u in range(9)]
                tap(hT[:, 0, :], T, allh)
                tap(Kz[:, 0, :], T, [R('k', 0, u) for u in range(9)])
                tap(Kz[:, 1, :], T, [R('k', 0, u) for u in range(9)])
                tap(qT[:, 0, :], T, [R('q', 0, u) for u in range(9)])
                tap(Vaug[:, :, 0, 64:128], 18 * 64, [R('v', tt) for tt in range(18)])
                tap(abuf[:, 0, :], APAD, [R('a', 0, u) for u in range(9)] + [R('apad')])

            emit_diags(124)
            def conv_gen():
                for ci, (s, n) in enumerate(CHUNKS):
                    if last and ci == 0:
                        continue
                    seg0, seg1 = (0, 256) if s < 256 else (256, T)
                    rd_units = units(max(seg0, s - 15), min(seg1, s + n + 15) - max(seg0, s - 15))
                    base = aidx(s) - 15
                    for j in range(4):
                        rd = [R('a', j, u) for u in rd_units] + [R('apad')]
                        cb = 6
                        for k0 in range(0, 31, 3):
                            ks = list(range(k0, min(31, k0 + 3)))

                            def fnc(e):
                                last_ = None
                                for k in ks:
                                    last_ = e.matmul(PB(cb, n), dgv(j * 31 + k), abuf[:, j, base + k:base + k + n],
                                                     start=(k == 0), stop=(k == 30))
                                return last_
                            S.op('pe', fnc, reads=rd + [dgres(j * 31 + k) for k in ks], writes=[R('ps', cb)])
                            yield
                        S.op('act', lambda e: e.activation(out=convy[:, j, 0:n], in_=PB(cb, n), func=AF.Identity,
                                                           bias=pc(l, O_DWB + j)),
                             reads=[R('ps', cb), R('const')], writes=[R('cy', j)])

                    def outfn(j, s=s, n=n):
                        S.op('act', lambda e: e.activation(out=hT[:, 4 + j, s:s + n], in_=convy[:, j, 0:n], func=AF.Silu,
                                                           scale=pc(l, O_CLG + j), bias=pc(l, O_CLB + j)),
                             reads=[R('cy', j), R('const')], writes=hres(4 + j, s, n))
                    hold['ln'] = True
                    ones = cmat[:, 0, :]

                    def f1c(e):
                        last_ = None
                        for k in range(4):
                            last_ = e.matmul(PB(7, n), ones, convy[:, k, 0:n], start=(k == 0), stop=(k == 3))
                        return last_
                    S.op('pe', f1c, reads=[R('cy', k) for k in range(4)] + [R('const')], writes=[R('ps', 7)])
                    yield
                    for k in range(4):
                        ti = nxt('sqb', 2)
                        S.op('act', lambda e: e.activation(out=sqb[ti][:, 0:n], in_=convy[:, k, 0:n], func=AF.Square),
                             reads=[R('cy', k)], writes=[R('sqb', ti)])
                        S.op('pe', lambda e: e.matmul(PB(6, n), onesb[:, :], sqb[ti][:, 0:n], start=(k == 0), stop=(k == 3)),
                             reads=[R('sqb', ti), R('onesb')], writes=[R('ps', 6)])
                        yield
                    tm, tr = 2, 3
                    inv = 1.0 / 512.0
                    S.op('dve', lambda e: e.tensor_scalar(out=tmp[tm][:, 0:n], in0=PB(7, n), scalar1=inv, scalar2=None,
                                                          op0=ALU.mult),
                         reads=[R('ps', 7)], writes=[R('tmp', tm)])
                    S.op('dve', lambda e: e.tensor_tensor(out=tmp[tr][:, 0:n], in0=tmp[tm][:, 0:n], in1=tmp[tm][:, 0:n],
                                                          op=ALU.mult),
                         reads=[R('tmp', tm)], writes=[R('tmp', tr)])
                    S.op('dve', lambda e: e.scalar_tensor_tensor(out=tmp[tr][:, 0:n], in0=PB(6, n), scalar=inv,
                                                                 in1=tmp[tr][:, 0:n], op0=ALU.mult, op1=ALU.subtract),
                         reads=[R('ps', 6), R('tmp', tr)], writes=[R('tmp', tr)])
                    hold['ln'] = False
                    yield
                    S.op('act', lambda e: e.activation(out=tmp[tr][:, 0:n], in_=tmp[tr][:, 0:n], func=AF.Ln,
                                                       bias=epsc[:, 0:1]),
                         reads=[R('tmp', tr), R('epsc')], writes=[R('tmp', tr)])
                    S.op('act', lambda e: e.activation(out=tmp[tr][:, 0:n], in_=tmp[tr][:, 0:n], func=AF.Exp, scale=-0.5),
                         reads=[R('tmp', tr)], writes=[R('tmp', tr)])
                    yield
                    for k in range(4):
                        S.op('dve', lambda e: e.tensor_tensor(out=convy[:, k, 0:n], in0=convy[:, k, 0:n],
                                                              in1=tmp[tm][:, 0:n], op=ALU.subtract),
                             reads=[R('cy', k), R('tmp', tm)], writes=[R('cy', k)])
                        S.op('dve', lambda e: e.tensor_tensor(out=convy[:, k, 0:n], in0=convy[:, k, 0:n],
                                                              in1=tmp[tr][:, 0:n], op=ALU.mult),
                             reads=[R('cy', k), R('tmp', tr)], writes=[R('cy', k)])
                        yield
                    for k in range(4):
                        outfn(k)
                    yield
            def mods_filler():
                if l + 1 < nlayers:
                    for _ in mods_gen(l + 1):
                        yield
                    emit_hcoef(l + 1, 0, O_LN2G, O_LN2B, l)
                emit_hcoef(l, 1, O_LN1G, O_LN1B, l)
                yield
            hold = {'ln': False}
            fillers = [conv_gen(), mods_filler()]
            fstate = {'i': 0}

            def conv_step(k=1):
                for _ in range(k):
                    if not fillers:
                        return
                    fstate['i'] = (fstate['i'] + 1) % len(fillers)
                    if hold['ln']:
                        fstate['i'] = 0
                    g_ = fillers[fstate['i']]
                    try:
                        next(g_)
                    except StopIteration:
                        fillers.remove(g_)

            for qi_, (s, n) in enumerate(QCH):
                if last and qi_ == 0:
                    continue
                ktiles = [0, 1] if qi_ == 0 else list(range(18))
                for i in range(4):
                    h = i // 2
                    stages = [(kp, p) for kp in range(len(ktiles) // 2) for p in range(2)]
                    oacc = [4, 5]
                    sc_state = {}

                    def emit_scores(idx, stg_):
                        kp, p = stg_
                        bpair = 2 * (idx % 2)
                        kts = [ktiles[2 * kp], ktiles[2 * kp + 1]]

                        def fs(e):
                            last_ = None
                            for jj in range(2):
                                last_ = e.matmul(PB(bpair + jj, n), Kz[:, 2 * h + p, kts[jj] * 128:(kts[jj] + 1) * 128],
                                                 qT[:, i, s:s + n], start=True, stop=True)
                            return last_
                        S.op('pe', fs, reads=[R('k', h, kt_ // 2) for kt_ in kts] + [R('q', i, u) for u in units(s, n)],
                             writes=[R('ps', bpair), R('ps', bpair + 1)])
                        pi = nxt('pt', NPT)
                        S.op('act', lambda e: e.activation(
                            out=PT[pi][:, :, 0:n], in_=ps[:, bpair:bpair + 2, 0:n], func=AF.Exp, scale=8.0),
                            reads=[R('ps', bpair), R('ps', bpair + 1)], writes=[R('pt', pi)])
                        sc_state[idx] = pi

                    def emit_pv(idx, stg_):
                        kp, p = stg_
                        pi = sc_state[idx]
                        c0 = 64 if p == 0 else 0
                        kts = [ktiles[2 * kp], ktiles[2 * kp + 1]]
                        nkp = len(ktiles) // 2

                        def fp(e):
                            last_ = None
                            for jj in range(2):
                                last_ = e.matmul(PB(oacc[p], n), Vaug[:, kts[jj], h, c0:c0 + 128], PT[pi][:, jj, 0:n],
                                                 start=(kp == 0 and jj == 0), stop=(kp == nkp - 1 and jj == 1))
                            return last_
                        S.op('pe', fp, reads=[R('v', kt_) for kt_ in kts] + [R('pt', pi)], writes=[R('ps', oacc[p])])
                    for idx, stg_ in enumerate(stages):
                        emit_scores(idx, stg_)
                        if idx > 0:
                            emit_pv(idx - 1, stages[idx - 1])
                        conv_step(1)
                    emit_pv(len(stages) - 1, stages[-1])
                    for p in range(2):
                        srow = slice(64, 128) if p == 0 else slice(0, 64)
                        orow = slice(0, 64) if p == 0 else slice(64, 128)
                        ob = oacc[p]
                        ti = 4 + p
                        S.op('act', lambda e: e.activation(out=tmp[ti][:, 0:n], in_=ps[:, ob, 0:n], func=AF.Copy),
                             reads=[R('ps', ob)], writes=[R('tmp', ti)])
                        S.op('dve', lambda e: e.reciprocal(out=RR[p][srow, 0:n], in_=tmp[ti][srow, 0:n]),
                             reads=[R('tmp', ti)], writes=[R('rr', p)])
                        S.op('sp', lambda e: [e.dma_start(out=RR[p][orow, 0:n], in_=RR[p][srow, 0:n])],
                             reads=[R('rr', p)], writes=[R('rrm', p)], lane=lane_rr[p], ndma=1)
                        S.op('dve', lambda e: e.tensor_tensor(
                            out=hT[orow, i, s:s + n], in0=tmp[ti][orow, 0:n], in1=RR[p][orow, 0:n], op=ALU.mult),
                            reads=[R('tmp', ti), R('rrm', p)], writes=hres(i, s, n))
            conv_step(10 ** 6)

            if l == 0:
                allh = [R('h', k, u) for k in range(KT) for u in range(9)]
                tap(hT[:, 0, :], T, allh)
                tap(hT[:, 4, :], T, allh)
            S.op('pool', lambda e: [e.dma_start(out=woA[:, :, :], in_=wcols(wout_d, l, 0, 512))] +
                 [e.dma_start(out=woB[q_][:, :, :], in_=wout_d[l, 2 * q_ * P:(2 * q_ + 2) * P, 512:1024]
                              .rearrange("(k p) c -> p k c", p=P)) for q_ in range(4)],
                 writes=[R('cy', j_) for j_ in range(4)] + [R('pt', 0), R('pt', 1), R('rr', 0), R('rr', 1),
                                                            R('rrm', 0), R('rrm', 1), R('wo')],
                 lane=lane_wo, ndma=5)
            S.op('pool', lambda e: [e.dma_start(out=w2s[:, 0:11, :],
                                                in_=wf2_d[l, 0:11 * P, :].rearrange("(f p) c -> p f c", p=P)),
                                    e.dma_start(out=w2s[:, 11:22, :],
                                                in_=wf2_d[l, 11 * P:22 * P, :].rearrange("(f p) c -> p f c", p=P))],
                 writes=[R('w2')] + [R('k', h_, u_) for h_ in range(2) for u_ in range(9)] +
                 [R('v', tt_) for tt_ in range(18)] + [R('a', j_, u_) for j_ in range(4) for u_ in range(9)] +
                 [R('apad'), R('tab')], lane=lane_w2, ndma=2)
            w2_ev = R('w2').w
            S.barrier(exclude=(lane_w2,))
            R('w2').w = w2_ev
            ones = cmat[:, 0, :]
            sbk = [0]

            def ln_job(s, n, col, mm_fn, gate_off, outfn_maker, out_dst):
                st = {}

                def A():
                    si = stage_in(xa_d, s, n)
                    st['si'] = si
                    for ot in range(KT):
                        bank = nxt('bank', 4)
                        S.op('pe', lambda e: mm_fn(e, ot, bank), reads=st_reads(ot), writes=[R('ps', bank)])
                        S.op('dve', lambda e: e.scalar_tensor_tensor(
                            out=stg[si][:, ot, 0:n], in0=PB(bank, n), scalar=modT[sl][:, gate_off + ot, col:col + 1],
                            in1=stg[si][:, ot, 0:n], op0=ALU.mult, op1=ALU.add),
                            reads=[R('ps', bank), R('mod', sl), R('stg', si, ot)], writes=[R('stg', si, ot)])
                st_reads = mm_fn.reads

                def B1():
                    si = st['si']
                    b1, b2 = [(6, 7), (4, 5)][sbk[0] % 2]
                    sbk[0] += 1
                    st['banks'] = (b1, b2)

                    def f1(e):
                        last_ = None
                        for k in range(KT):
                            last_ = e.matmul(PB(b1, n), ones, stg[si][:, k, 0:n], start=(k == 0), stop=(k == KT - 1))
                        return last_
                    S.op('pe', f1, reads=[R('stg', si, k) for k in range(KT)] + [R('const')], writes=[R('ps', b1)])
                    for k in range(KT):
                        ti = nxt('sqb', 2)
                        S.op('act', lambda e: e.activation(out=sqb[ti][:, 0:n], in_=stg[si][:, k, 0:n], func=AF.Square),
                             reads=[R('stg', si, k)], writes=[R('sqb', ti)])
                        S.op('pe', lambda e: e.matmul(PB(b2, n), onesb[:, :], sqb[ti][:, 0:n], start=(k == 0),
                                                      stop=(k == KT - 1)),
                             reads=[R('sqb', ti), R('onesb')], writes=[R('ps', b2)])

                def B2():
                    si = st['si']
                    b1, b2 = st['banks']
                    tm = nxt('tmp', NTMP)
                    tr = nxt('tmp', NTMP)
                    inv = 1.0 / 1024.0
                    S.op('dve', lambda e: e.tensor_scalar(out=tmp[tm][:, 0:n], in0=PB(b1, n), scalar1=inv, scalar2=None,
                                                          op0=ALU.mult),
                         reads=[R('ps', b1)], writes=[R('tmp', tm)])
                    S.op('dve', lambda e: e.tensor_tensor(out=tmp[tr][:, 0:n], in0=tmp[tm][:, 0:n], in1=tmp[tm][:, 0:n],
                                                          op=ALU.mult),
                         reads=[R('tmp', tm)], writes=[R('tmp', tr)])
                    S.op('dve', lambda e: e.scalar_tensor_tensor(out=tmp[tr][:, 0:n], in0=PB(b2, n), scalar=inv,
                                                                 in1=tmp[tr][:, 0:n], op0=ALU.mult, op1=ALU.subtract),
                         reads=[R('ps', b2), R('tmp', tr)], writes=[R('tmp', tr)])
                    S.op('act', lambda e: e.activation(out=tmp[tr][:, 0:n], in_=tmp[tr][:, 0:n], func=AF.Ln,
                                                       bias=epsc[:, 0:1]),
                         reads=[R('tmp', tr), R('epsc')], writes=[R('tmp', tr)])
                    S.op('act', lambda e: e.activation(out=tmp[tr][:, 0:n], in_=tmp[tr][:, 0:n], func=AF.Exp, scale=-0.5),
                         reads=[R('tmp', tr)], writes=[R('tmp', tr)])
                    outfn = outfn_maker(si)
                    for k in range(KT):
                        S.op('dve', lambda e: e.tensor_tensor(out=stg[si][:, k, 0:n], in0=stg[si][:, k, 0:n],
                                                              in1=tmp[tm][:, 0:n], op=ALU.subtract),
                             reads=[R('stg', si, k), R('tmp', tm)], writes=[R('stg', si, k)])
                        S.op('dve', lambda e: e.tensor_tensor(out=stg[si][:, k, 0:n], in0=stg[si][:, k, 0:n],
                                                              in1=tmp[tr][:, 0:n], op=ALU.mult),
                             reads=[R('stg', si, k), R('tmp', tr)], writes=[R('stg', si, k)])
                        outfn(k)
                    if out_dst is out_d:
                        stage_out(out_d, si, s - LC, n, s)
                    else:
                        stage_out(xa_d, si, s, n, s)
                return A, B1, B2

            def pipeline_steps(jobs):
                m = len(jobs)
                steps = []
                for i_ in range(m + 2):
                    fs = []
                    if i_ < m:
                        fs.append(jobs[i_][0])
                    if 0 <= i_ - 1 < m:
                        fs.append(jobs[i_ - 1][1])
                    if 0 <= i_ - 2 < m:
                        fs.append(jobs[i_ - 2][2])
                    steps.append(fs)
                return steps

            def run_step(fs):
                for f_ in fs:
                    f_()

            def mk_wout_mm(s, n):
                def mm(e, ot, bank):
                    c0 = (ot % 4) * 128
                    last_ = None
                    for k in range(KT):
                        w_ap = woA[:, k, c0:c0 + 128] if ot < 4 else woB[k // 2][:, k % 2, c0:c0 + 128]
                        last_ = e.matmul(PB(bank, n), w_ap, hT[:, k, s:s + n], start=(k == 0), stop=(k == KT - 1))
                    return last_
                mm.reads = lambda ot: [R('wo')] + [r for k in range(KT) for r in hres(k, s, n)]
                return mm
            ln1_jobs = []
            for ci, (s, n) in enumerate(LNCH):
                if last and ci == 0:
                    continue
                col = 1 if ci == 0 else 0
                ln1_jobs.append(ln_job(
                    s, n, col, mk_wout_mm(s, n), 16,
                    lambda si, s=s, n=n, col=col: ln_outputs(si, s, n, sl, 1, lambda k: pal[:, l, k:k + 1],
                                                             lambda k: pal[:, l, 8 + k:9 + k], col),
                    xa_d))
            lnq = pipeline_steps(ln1_jobs)
            n_pre = 5 if not last else 4
            for _ in range(n_pre):
                run_step(lnq.pop(0))

            for gi, grp in enumerate(GROUPS):
                gch = [(ci, CHUNKS[ci]) for ci in grp if not (last and ci == 0)]
                g0 = gch[0][1][0]
                g1 = gch[-1][1][0] + gch[-1][1][1]
                for blk in range(11):
                    def dm_F(slot, blk=blk):
                        return [(slot[:, :, 0:256], wcols(wf1_d, l, blk * 256, 256)),
                                (slot[:, :, 256:512], wcols(wf3_d, l, blk * 256, 256))]
                    si = wload(dm_F, 2)
                    for f2 in range(2):
                        ft = blk * 2 + f2
                        for ci, (s, n) in gch:
                            b1 = nxt('bank', 4)
                            b3 = nxt('bank', 4)
                            proj(si, f2 * 128, s, n, b1)
                            proj(si, 256 + f2 * 128, s, n, b3)
                            ti = nxt('tmp', NTMP)
                            S.op('act', lambda e: e.activation(out=tmp[ti][:, 0:n], in_=PB(b1, n), func=AF.Silu),
                                 reads=[R('ps', b1)], writes=[R('tmp', ti)])
                            lo = s - g0
                            S.op('dve', lambda e: e.tensor_tensor(
                                out=act[:, ft, lo:lo + n], in0=PB(b3, n), in1=tmp[ti][:, 0:n], op=ALU.mult),
                                reads=[R('ps', b3), R('tmp', ti)],
                                writes=[R('act', ft, lo // 256), R('act', ft, (lo + n - 1) // 256)])
                    if lnq and (gi > 0 or blk % 2 == 1 or len(lnq) > 11 - blk):
                        run_step(lnq.pop(0))
                while lnq:
                    run_step(lnq.pop(0))
                jobs = []
                for (s, n) in LNCH:
                    if s < g0 or s >= g1:
                        continue
                    col = 1 if s == 0 else 0
                    lo = s - g0

                    def mk_w2_mm(lo=lo, n=n):
                        def mm(e, ot, bank):
                            last_ = None
                            for f in range(FT):
                                last_ = e.matmul(PB(bank, n), w2s[:, f, ot * 128:(ot + 1) * 128], act[:, f, lo:lo + n],
                                                 start=(f == 0), stop=(f == FT - 1))
                            return last_
                        mm.reads = lambda ot: [R('w2')] + [R('act', f, lo // 256) for f in range(FT)]
                        return mm
                    if last:
                        om = lambda si, s=s, n=n, col=col: ln_outputs(
                            si, s, n, None, None, None, None, col, final=True,
                            g_ap=lambda k: par[:, l, O_LN2G + k:O_LN2G + k + 1],
                            b_ap=lambda k: par[:, l, O_LN2B + k:O_LN2B + k + 1])
                    else:
                        om = lambda si, s=s, n=n, col=col: ln_outputs(
                            si, s, n, (l + 1) % 2, 0, lambda k: pal[:, l, 16 + k:17 + k],
                            lambda k: pal[:, l, 24 + k:25 + k], col, skip_h=(l == nlayers - 1))
                    jobs.append(ln_job(s, n, col, mk_w2_mm(), 40, om, out_d if last else xa_d))
                for jb in jobs:
                    jb[0]()
                m_ = len(jobs)
                lnq = []
                for i_ in range(m_ + 1):
                    fs = []
                    if i_ < m_:
                        fs.append(jobs[i_][1])
                    if 0 <= i_ - 1 < m_:
                        fs.append(jobs[i_ - 1][2])
                    lnq.append(fs)
                if gi == len(GROUPS) - 1:
                    while lnq:
                        run_step(lnq.pop(0))
                    if l + 1 < nlayers:
                        pre_A[0] = wload(lambda slot: dm_A(slot, l + 1), 5)
            if l == nlayers - 1 and not last:
                S.barrier()
                for (s, n) in LNCH[1:]:
                    si = stage_in(xa_d, s, n)
                    stage_out(out_d, si, s - LC, n, s)

        S.final_wait('sp', stg_out + [lane_dbg])

        with nc.Block() as block:
            @block.tensor
            def _(e):
                for f in S.progs['pe']:
                    f(e)

            @block.scalar
            def _(e):
                for f in S.progs['act']:
                    f(e)

            @block.vector
            def _(e):
                for f in S.progs['dve']:
                    f(e)

            @block.gpsimd
            def _(e):
                for f in S.progs['pool']:
                    f(e)

            @block.sync
            def _(e):
                for f in S.progs['sp']:
                    f(e)
    return nc


def _host_consts():
    ones = np.ones((P, P), np.float32)
    sw = np.zeros((P, P), np.float32)
    for i in range(P):
        sw[i, (i + 64) % P] = 1.0
    rot = np.zeros((P, P), np.float32)
    for blk in range(4):
        for f in range(16):
            a = blk * 32 + f
            rot[a + 16, a] = -1.0
            rot[a, a + 16] = 1.0
    bd = np.zeros((P, P), np.float32)
    bd[:64, :64] = 1.0
    bd[64:, 64:] = 1.0
    cmat = np.concatenate([ones, sw, rot, bd, np.eye(P, dtype=np.float32)], axis=1)
    tC = np.ones((P, T), np.float32)
    tS = np.zeros((P, T), np.float32)
    tok = np.arange(NL)
    pos = np.stack([tok // 64, tok % 64], 0).astype(np.float32)
    freqs = (10000.0 ** (-np.arange(16, dtype=np.float32) / 16.0)).astype(np.float32)
    for p_ in range(P):
        d = p_ % 64
        ang = (pos[d // 32] * freqs[d % 16]).astype(np.float32)
        tC[p_, LC:] = np.cos(ang)
        tS[p_, LC:] = np.sin(ang)
    tabs = np.concatenate([tC, tS], axis=1)
    return cmat, tabs


def _pack_params(b_ada, q_norm_g, k_norm_g, dw_w, dw_b, conv_ln_g, conv_ln_b, ln1_g, ln1_b, ln2_g, ln2_b):
    par = np.zeros((P, DEPTH, NPAR), np.float32)
    for l in range(DEPTH):
        par[:, l, 0:48] = b_ada[l].reshape(48, P).T
        par[:, l, 48:56] = ln1_g[l].reshape(8, P).T
        par[:, l, 56:64] = ln1_b[l].reshape(8, P).T
        par[:, l, 64:72] = ln2_g[l].reshape(8, P).T
        par[:, l, 72:80] = ln2_b[l].reshape(8, P).T
        w = dw_w[l, :, 0, :]
        for j in range(4):
            par[:, l, 80 + j * 31:80 + (j + 1) * 31] = w[:, j * P:(j + 1) * P].T
        par[:, l, 204:208] = dw_b[l].reshape(4, P).T
        par[:, l, 208:212] = conv_ln_g[l].reshape(4, P).T
        par[:, l, 212:216] = conv_ln_b[l].reshape(4, P).T
        par[:, l, 216] = np.tile(q_norm_g[l], 2)
        par[:, l, 217] = np.tile(k_norm_g[l], 2)
    return par.reshape(P, DEPTH * NPAR)


_NC_CACHE = {}


def kernel(x, c, ctx, c_ctx, w_ada, b_ada, w_in, q_norm_g, k_norm_g, dw_w, dw_b,
           conv_ln_g, conv_ln_b, w_out, ln1_g, ln1_b, w_ff1, w_ff3, w_ff2, ln2_g, ln2_b,
           _nlayers=DEPTH, _dbg=None):
    f = lambda a: np.ascontiguousarray(np.asarray(a, dtype=np.float32))
    x, c, ctx, c_ctx = f(x), f(c), f(ctx), f(c_ctx)
    cmat, tabs = _host_consts()
    par = _pack_params(f(b_ada), f(q_norm_g), f(k_norm_g), f(dw_w), f(dw_b), f(conv_ln_g), f(conv_ln_b),
                       f(ln1_g), f(ln1_b), f(ln2_g), f(ln2_b))
    key = (_nlayers, _dbg)
    if key not in _NC_CACHE:
        _NC_CACHE[key] = build(_nlayers, _dbg)
    nc = _NC_CACHE[key]
    shared = {"par": par, "cmat": cmat, "tabs": tabs, "w_ada": f(w_ada), "w_in": f(w_in), "w_out": f(w_out),
              "w_ff1": f(w_ff1), "w_ff3": f(w_ff3), "w_ff2": f(w_ff2)}
    in_maps = []
    for b in range(8):
        xT = np.ascontiguousarray(np.concatenate([ctx[b], x[b]], axis=0).T)
        cs = np.ascontiguousarray(np.stack([c[b], c_ctx], -1).reshape(KT, P, 2).transpose(1, 0, 2).reshape(P, KT * 2))
        m = dict(shared)
        m["xT"] = xT
        m["cs"] = cs
        in_maps.append(m)
    res = run_bass_kernel_spmd(nc, in_maps, core_ids=list(range(8)))
    out = np.stack([np.ascontiguousarray(r["outT"].T) for r in res.results], axis=0)
    if _dbg:
        kernel.dbg = [r["dbg"] for r in res.results]
    return out.astype(np.float32)
```

```python
import numpy as np
from contextlib import ExitStack
import concourse.bass as bass
import concourse.mybir as mybir
from concourse.bass_utils import run_bass_kernel_spmd

F32 = mybir.dt.float32
BF16 = mybir.dt.bfloat16
AF = mybir.ActivationFunctionType
ALU = mybir.AluOpType

P = 128
D = 1024
KT = 8
T = 2304
LC = 256
NL = 2048
DEPTH = 4
DFF = 2816
FT = 22
ALPHA = float(8 ** 0.25)
EPS = 1e-6
CHUNKS = [(0, 256), (256, 512), (768, 512), (1280, 256), (1536, 512), (2048, 256)]
GROUPS = [[0, 1], [2, 3], [4, 5]]
QCH = [(0, 256), (256, 512), (768, 512), (1280, 512), (1792, 512)]
LNCH = [(i * 256, 256) for i in range(9)]
APAD = 286 + 2078
NPAR = 218
ENG = ['pe', 'act', 'dve', 'pool', 'sp']


def units(s, n):
    return range(s // 256, (s + n - 1) // 256 + 1)


def aidx(tok):
    return 15 + tok if tok < 256 else 45 + tok


class _RecInst:
    def __init__(self, idx):
        self.idx = idx


class _Rec:
    def __init__(self):
        self.calls = []

    def __getattr__(self, name):
        def f(*a, **k):
            self.calls.append((name, a, k))
            return _RecInst(len(self.calls) - 1)
        return f


class Lane:
    def __init__(self, sem):
        self.sem = sem
        self.val = 0


class Res:
    __slots__ = ('w', 'r')

    def __init__(self):
        self.w = None
        self.r = {}


class Sched:
    def __init__(self, nc, stack):
        self.nc = nc
        self.stack = stack
        self.progs = {e: [] for e in ENG}
        self.lane = {e: self.newlane('L' + e) for e in ENG}
        self.known = {e: {} for e in ENG}
        self.res = {}
        self.dlanes = []

    def newlane(self, name):
        return Lane(self.stack.enter_context(self.nc.semaphore(name)))

    def dmalane(self, name):
        ln = self.newlane(name)
        self.dlanes.append(ln)
        return ln

    def R(self, *key):
        r = self.res.get(key)
        if r is None:
            r = self.res[key] = Res()
        return r

    def op(self, eng, fn, reads=(), writes=(), lane=None, ndma=0):
        waits = {}

        def need(ev):
            if ev is None:
                return
            ln, v = ev
            if waits.get(ln, 0) < v:
                waits[ln] = v
        for r in reads:
            need(r.w)
        for r in writes:
            need(r.w)
            for ln, v in r.r.items():
                need((ln, v))
        k = self.known[eng]
        wl = []
        for ln, v in waits.items():
            if k.get(ln, 0) < v:
                k[ln] = v
                wl.append((ln.sem, v))
        if lane is None:
            lane = self.lane[eng]
            lane.val += 1
            inc = 1
        else:
            lane.val += 16 * ndma
            inc = 16
        ev = (lane, lane.val)
        for r in reads:
            if r.r.get(lane, 0) < lane.val:
                r.r[lane] = lane.val
        for r in writes:
            r.w = ev
            r.r = {}
        sem = lane.sem
        rec = _Rec()
        out = fn(rec)
        calls = rec.calls
        inc_idx = [o_.idx for o_ in out] if inc == 16 else [out.idx]

        def run(e):
            for s_, v_ in wl:
                e.wait_ge(s_, v_)
            insts = [getattr(e, nm)(*a, **kw) for (nm, a, kw) in calls]
            for i_ in inc_idx:
                insts[i_].then_inc(sem, inc)
        self.progs[eng].append(run)

    def barrier(self, exclude=()):
        for e in ENG:
            k = self.known[e]
            wl = []
            for ln in [self.lane[x] for x in ENG if x != e] + [d_ for d_ in self.dlanes if d_ not in exclude]:
                if k.get(ln, 0) < ln.val:
                    k[ln] = ln.val
                    wl.append((ln.sem, ln.val))

            def run(en, wl=wl):
                for s_, v_ in wl:
                    en.wait_ge(s_, v_)
            self.progs[e].append(run)
        for r in self.res.values():
            r.w = None
            r.r = {}

    def final_wait(self, eng, lanes):
        wl = [(ln.sem, ln.val) for ln in lanes]

        def run(en):
            for s_, v_ in wl:
                en.wait_ge(s_, v_)
        self.progs[eng].append(run)


def build(nlayers=DEPTH, dbg=None):
    nc = bass.Bass("TRN2", target_bir_lowering=False)
    dt = nc.dram_tensor
    xT_d = dt("xT", [D, T], F32, kind="ExternalInput").ap()
    cs_d = dt("cs", [P, KT * 2], F32, kind="ExternalInput").ap()
    par_d = dt("par", [P, DEPTH * NPAR], F32, kind="ExternalInput").ap()
    cm_d = dt("cmat", [P, 5 * P], F32, kind="ExternalInput").ap()
    tab_d = dt("tabs", [P, 2 * T], F32, kind="ExternalInput").ap()
    wada_d = dt("w_ada", [DEPTH, D, 6 * D], F32, kind="ExternalInput").ap()
    win_d = dt("w_in", [DEPTH, D, 1792], F32, kind="ExternalInput").ap()
    wout_d = dt("w_out", [DEPTH, D, D], F32, kind="ExternalInput").ap()
    wf1_d = dt("w_ff1", [DEPTH, D, DFF], F32, kind="ExternalInput").ap()
    wf3_d = dt("w_ff3", [DEPTH, D, DFF], F32, kind="ExternalInput").ap()
    wf2_d = dt("w_ff2", [DEPTH, DFF, D], F32, kind="ExternalInput").ap()
    out_d = dt("outT", [D, NL], F32, kind="ExternalOutput").ap()
    xa_d = dt("xa_scr", [D, T], F32).ap()
    dbg_d = None
    if dbg:
        dbg_d = dt("dbg", [P, dbg], F32, kind="ExternalOutput").ap()

    stack = ExitStack()
    with stack:
        def sb(name, shape, dtp):
            return stack.enter_context(nc.sbuf_tensor(name, shape, dtp))
        S = Sched(nc, stack)
        R = S.R
        hT = sb("hT", [P, KT, T], BF16)
        BIGN = 39424
        big = sb("big", [P, BIGN], BF16)
        qT = big[:, 0:9216].rearrange("p (a t) -> p a t", a=4)
        Kz = big[:, 9216:18432].rearrange("p (a t) -> p a t", a=4)
        Vaug = big[:, 18432:25344].rearrange("p (t h c) -> p t h c", t=18, h=2)
        abuf = big[:, 25344:34800].rearrange("p (a t) -> p a t", a=4)
        tabC = big[:, 34800:37104]
        tabS = big[:, 37104:39408]
        act = big[:, 0:16896].rearrange("p (f t) -> p f t", f=FT)
        w2s = big[:, 16896:39424].rearrange("p (f c) -> p f c", f=FT)
        NSLOT = 3
        wsl = [sb(f"ws{i}", [P, KT, 512], BF16) for i in range(NSLOT)]
        wsl_lane = [S.dmalane(f"wsl{i}") for i in range(NSLOT)]
        NSTG = 4
        stg = [sb(f"stg{i}", [P, KT, 256], F32) for i in range(NSTG)]
        stg_in = [S.dmalane(f"sti{i}") for i in range(NSTG)]
        stg_out = [S.dmalane(f"sto{i}") for i in range(NSTG)]
        cmat = sb("cmatsb", [P, 3, P], F32)
        bdm = sb("bdm", [P, P], BF16)
        identb = sb("identb", [P, P], BF16)
        onesb = sb("onesb", [P, P], BF16)
        stgb = [stg[i][:, :, :].bitcast(BF16) for i in range(NSTG)]

        def dgv(d):
            return stgb[d // 32][:, (d % 32) // 4, (d % 4) * 128:(d % 4) * 128 + 128]

        def dgres(d):
            return R('stg', d // 32, (d % 32) // 4)
        par = sb("parsb", [P, DEPTH, NPAR], F32)
        pal = sb("pal", [P, DEPTH, 32], F32)
        cs = sb("cssb", [P, KT, 2], F32)
        epsc = sb("epsc", [P, 2], F32)
        sbf = sb("sbf", [P, KT, 2], BF16)
        modT = [sb(f"modT{i}", [P, 48, 2], F32) for i in range(2)]
        onesc = [sb(f"onesc{i}", [P, 2, KT, 2], F32) for i in range(2)]
        hA = [sb(f"hA{i}", [P, 2, KT, 2], F32) for i in range(2)]
        hB = [sb(f"hB{i}", [P, 2, KT, 2], F32) for i in range(2)]
        NTMP = 6
        tmp = [sb(f"tmp{i}", [P, 512], F32) for i in range(NTMP)]
        sqb = [sb(f"sqb{i}", [P, 512], BF16) for i in range(2)]
        NPT = 2
        PT = [sb(f"pt{i}", [P, 2, 512], BF16) for i in range(NPT)]
        convy = sb("convy", [P, 4, 512], F32)
        RR = [sb(f"rr{i}", [P, 512], F32) for i in range(2)]
        ps = stack.enter_context(nc.psum_tensor("ps", [P, 8, 512], F32))
        lane_const = S.dmalane("const")
        lane_tab = S.dmalane("tab")
        lane_bdm = S.dmalane("bdm")
        lane_rr = [S.dmalane("rr0"), S.dmalane("rr1")]
        lane_wo = S.dmalane("wo")
        woA = convy[:, :, :].bitcast(BF16).rearrange("p a (b c) -> p (a b) c", b=2)
        woB = [PT[0], PT[1], RR[0][:, :].bitcast(BF16).rearrange("p (a c) -> p a c", a=2),
               RR[1][:, :].bitcast(BF16).rearrange("p (a c) -> p a c", a=2)]
        lane_w2 = S.dmalane("w2")
        lane_dbg = S.dmalane("dbg")

        cnt = {'tmp': 0, 'sqb': 0, 'pt': 0, 'ws': 0, 'stg': 0, 'bank': 0, 'dg': 0, 'cb': 0}

        def nxt(kind, n):
            i = cnt[kind] % n
            cnt[kind] += 1
            return i

        def PB(b, n):
            return ps[:, b, 0:n]

        def ld_const(e):
            return [e.dma_start(out=cmat[:, :, :], in_=cm_d[:, 0:3 * P].rearrange("p (a c) -> p a c", a=3)),
                    e.dma_start(out=par[:, :, :], in_=par_d.rearrange("p (l c) -> p l c", l=DEPTH)),
                    e.dma_start(out=cs[:, :, :], in_=cs_d.rearrange("p (k c) -> p k c", k=KT))]
        S.op('sp', ld_const, writes=[R('const')], lane=lane_const, ndma=3)
        S.op('pool', lambda e: [e.dma_start(out=bdm[:, :], in_=cm_d[:, 3 * P:4 * P]),
                                e.dma_start(out=identb[:, :], in_=cm_d[:, 4 * P:5 * P])],
             writes=[R('bdm')], lane=lane_bdm, ndma=2)
        S.op('act', lambda e: e.activation(out=sbf[:, :, :], in_=cs[:, :, :], func=AF.Silu),
             reads=[R('const')], writes=[R('sbf')])
        S.op('dve', lambda e: e.tensor_scalar(out=pal[:, :, :], in0=par[:, :, 48:80], scalar1=ALPHA,
                                              scalar2=None, op0=ALU.mult),
             reads=[R('const')], writes=[R('pal')])
        S.op('dve', lambda e: e.memset(onesb[:, :], 1.0), writes=[R('onesb')])
        S.op('dve', lambda e: e.memset(epsc[:, 0:1], EPS), writes=[R('epsc')])
        S.op('dve', lambda e: e.memset(epsc[:, 1:2], 64 * EPS), writes=[R('epsc')])
        S.op('dve', lambda e: e.memset(RR[0][:, :], 0.0), writes=[R('rr', 0)])
        S.op('dve', lambda e: e.memset(RR[1][:, :], 0.0), writes=[R('rr', 1)])

        def pc(l, off, n=1):
            return par[:, l, off:off + n]
        O_BADA, O_LN1G, O_LN1B, O_LN2G, O_LN2B, O_DWW, O_DWB, O_CLG, O_CLB, O_QG, O_KG = \
            0, 48, 56, 64, 72, 80, 204, 208, 212, 216, 217

        def wload(dmas, nd):
            si = nxt('ws', NSLOT)
            pairs = dmas(wsl[si])

            def fn(e):
                return [e.dma_start(out=o, in_=i) for (o, i) in pairs]
            S.op('pool', fn, writes=[R('ws', si)], lane=wsl_lane[si], ndma=len(pairs))
            return si

        def wcols(wd, l, c0, w):
            return wd[l, :, c0:c0 + w].rearrange("(k p) c -> p k c", p=P)

        def mods_gen(l):
            m = modT[l % 2]
            mres = R('mod', l % 2)
            bank = 7
            for blk in range(12):
                si = wload(lambda slot: [(slot[:, :, :], wcols(wada_d, l, blk * 512, 512))], 1)

                def fn(e):
                    last_ = None
                    for f4 in range(4):
                        for k in range(KT):
                            last_ = e.matmul(ps[:, bank, 2 * f4:2 * f4 + 2], wsl[si][:, k, f4 * 128:(f4 + 1) * 128],
                                             sbf[:, k, :], start=(k == 0), stop=(k == KT - 1))
                    return last_
                S.op('pe', fn, reads=[R('ws', si), R('sbf')], writes=[R('ps', bank)])
                for col in range(2):
                    S.op('dve', lambda e: e.tensor_tensor(
                        out=m[:, blk * 4:blk * 4 + 4, col],
                        in0=ps[:, bank, 0:8].rearrange("p (f c) -> p f c", c=2)[:, :, col],
                        in1=par[:, l, blk * 4:blk * 4 + 4], op=ALU.add),
                        reads=[R('ps', bank), R('const')], writes=[mres])
                yield
            osc = onesc[l % 2]
            for j, g in enumerate((1, 4)):
                S.op('dve', lambda e: e.tensor_scalar(
                    out=osc[:, j, :, :], in0=m[:, g * 8:(g + 1) * 8, :], scalar1=1.0, scalar2=None, op0=ALU.add),
                    reads=[mres], writes=[R('osc', l % 2)])

        def emit_mods(l):
            for _ in mods_gen(l):
                pass

        def emit_hcoef(l, which, goff, boff, lprev):
            m = modT[l % 2]
            osc = onesc[l % 2]
            A = hA[l % 2]
            B = hB[l % 2]
            shg = 0 if which == 0 else 3
            rd = [R('mod', l % 2), R('osc', l % 2), R('const')]
            wr = [R('hc', l % 2, which)]
            for col in range(2):
                if lprev is None:
                    S.op('dve', lambda e, col=col: e.tensor_copy(out=A[:, which, :, col], in_=osc[:, which, :, col]),
                         reads=rd, writes=wr)
                    S.op('dve', lambda e, col=col: e.tensor_copy(out=B[:, which, :, col],
                                                                  in_=m[:, shg * 8:shg * 8 + 8, col]),
                         reads=rd, writes=wr)
                else:
                    S.op('dve', lambda e, col=col: e.tensor_tensor(
                        out=A[:, which, :, col], in0=osc[:, which, :, col], in1=par[:, lprev, goff:goff + 8],
                        op=ALU.mult), reads=rd, writes=wr)
                    S.op('dve', lambda e, col=col: e.tensor_tensor(
                        out=B[:, which, :, col], in0=osc[:, which, :, col], in1=par[:, lprev, boff:boff + 8],
                        op=ALU.mult), reads=rd, writes=wr)
                    S.op('dve', lambda e, col=col: e.tensor_tensor(
                        out=B[:, which, :, col], in0=B[:, which, :, col], in1=m[:, shg * 8:shg * 8 + 8, col],
                        op=ALU.add), reads=rd + wr, writes=wr)

        def layernorm(nt, vt, vres, n, nfeat, outfn, banks=(6, 7)):
            b1, b2 = banks
            ones = cmat[:, 0, :]
            def f1(e):
                last = None
                for k in range(nt):
                    last = e.matmul(PB(b1, n), ones, vt(k), start=(k == 0), stop=(k == nt - 1))
                return last
            S.op('pe', f1, reads=[vres(k) for k in range(nt)] + [R('const')], writes=[R('ps', b1)])
            for k in range(nt):
                ti = nxt('tmp', NTMP)
                S.op('act', lambda e, k=k, ti=ti: e.activation(out=tmp[ti][:, 0:n], in_=vt(k), func=AF.Square),
                     reads=[vres(k)], writes=[R('tmp', ti)])
                S.op('pe', lambda e, k=k, ti=ti: e.matmul(PB(b2, n), ones, tmp[ti][:, 0:n], start=(k == 0),
                                                          stop=(k == nt - 1)),
                     reads=[R('tmp', ti), R('const')], writes=[R('ps', b2)])
            tm = nxt('tmp', NTMP)
            tr = nxt('tmp', NTMP)
            inv = 1.0 / nfeat
            S.op('dve', lambda e: e.tensor_scalar(out=tmp[tm][:, 0:n], in0=PB(b1, n), scalar1=inv, scalar2=None,
                                                  op0=ALU.mult),
                 reads=[R('ps', b1)], writes=[R('tmp', tm)])
            S.op('dve', lambda e: e.tensor_tensor(out=tmp[tr][:, 0:n], in0=tmp[tm][:, 0:n], in1=tmp[tm][:, 0:n],
                                                  op=ALU.mult),
                 reads=[R('tmp', tm)], writes=[R('tmp', tr)])
            S.op('dve', lambda e: e.scalar_tensor_tensor(out=tmp[tr][:, 0:n], in0=PB(b2, n), scalar=inv,
                                                         in1=tmp[tr][:, 0:n], op0=ALU.mult, op1=ALU.subtract),
                 reads=[R('ps', b2), R('tmp', tr)], writes=[R('tmp', tr)])
            S.op('act', lambda e: e.activation(out=tmp[tr][:, 0:n], in_=tmp[tr][:, 0:n], func=AF.Ln, bias=epsc[:, 0:1]),
                 reads=[R('tmp', tr), R('epsc')], writes=[R('tmp', tr)])
            S.op('act', lambda e: e.activation(out=tmp[tr][:, 0:n], in_=tmp[tr][:, 0:n], func=AF.Exp, scale=-0.5),
                 reads=[R('tmp', tr)], writes=[R('tmp', tr)])
            for k in range(nt):
                S.op('dve', lambda e, k=k: e.tensor_tensor(out=vt(k), in0=vt(k), in1=tmp[tm][:, 0:n], op=ALU.subtract),
                     reads=[vres(k), R('tmp', tm)], writes=[vres(k)])
                S.op('dve', lambda e, k=k: e.tensor_tensor(out=vt(k), in0=vt(k), in1=tmp[tr][:, 0:n], op=ALU.mult),
                     reads=[vres(k), R('tmp', tr)], writes=[vres(k)])
                outfn(k)

        dbg_off = [0]

        def tap(ap_f32, n, reads, parts=P):
            if dbg_d is None:
                return
            o = dbg_off[0]
            dbg_off[0] += n
            S.op('pool', lambda e: [e.dma_start(out=dbg_d[0:parts, o:o + n], in_=ap_f32)], reads=reads,
                 lane=lane_dbg, ndma=1)

        def stage_in(src_d, s, n):
            si = nxt('stg', NSTG)
            S.op('sp', lambda e: [e.dma_start(out=stg[si][:, :, 0:n],
                                              in_=src_d[:, s:s + n].rearrange("(k p) t -> p k t", p=P))],
                 reads=[R('xa', u) for u in units(s, n)] if src_d is xa_d else [],
                 writes=[R('stg', si, k) for k in range(KT)], lane=stg_in[si], ndma=1)
            return si

        def stage_out(dst_d, si, s_dst, n, s_tok):
            S.op('act', lambda e: [e.dma_start(out=dst_d[:, s_dst:s_dst + n].rearrange("(k p) t -> p k t", p=P),
                                              in_=stg[si][:, :, 0:n])],
                 reads=[R('stg', si, k) for k in range(KT)],
                 writes=[R('xa', u) for u in units(s_tok, n)] if dst_d is xa_d else [R('outd')],
                 lane=stg_out[si], ndma=1)

        def hres(k, s, n):
            return [R('h', k, u) for u in units(s, n)]

        def ln_outputs(si, s, n, l_next_slot, which, ga_ap, ba_ap, col, final=False, g_ap=None, b_ap=None,
                       skip_h=False):
            def outfn(k):
                t_ap = stg[si][:, k, 0:n]
                tres = R('stg', si, k)
                if final:
                    S.op('act', lambda e: e.activation(out=t_ap, in_=t_ap, func=AF.Identity,
                                                       scale=g_ap(k), bias=b_ap(k)),
                         reads=[tres, R('const')], writes=[tres])
                    return
                A = hA[l_next_slot]
                B = hB[l_next_slot]
                if not skip_h:
                    S.op('act', lambda e: e.activation(out=hT[:, k, s:s + n], in_=t_ap, func=AF.Identity,
                                                       scale=A[:, which, k, col:col + 1],
                                                       bias=B[:, which, k, col:col + 1]),
                         reads=[tres, R('hc', l_next_slot, which)], writes=hres(k, s, n))
                S.op('act', lambda e: e.activation(out=t_ap, in_=t_ap, func=AF.Identity,
                                                   scale=ga_ap(k), bias=ba_ap(k)),
                     reads=[tres, R('pal'), R('const')], writes=[tres])
            return outfn

        emit_mods(0)
        emit_hcoef(0, 0, None, None, None)
        for ci, (s, n) in enumerate(LNCH):
            si = stage_in(xT_d, s, n)
            col = 1 if ci == 0 else 0
            for k in range(KT):
                t_ap = stg[si][:, k, 0:n]
                tres = R('stg', si, k)
                S.op('act', lambda e, t_ap=t_ap, k=k, col=col, s=s, n=n: e.activation(
                    out=hT[:, k, s:s + n], in_=t_ap, func=AF.Identity,
                    scale=hA[0][:, 0, k, col:col + 1], bias=hB[0][:, 0, k, col:col + 1]),
                    reads=[tres, R('hc', 0, 0)], writes=hres(k, s, n))
                S.op('dve', lambda e, t_ap=t_ap: e.tensor_scalar(out=t_ap, in0=t_ap, scalar1=ALPHA, scalar2=None,
                                                                 op0=ALU.mult),
                     reads=[tres], writes=[tres])
            stage_out(xa_d, si, s, n, s)

        def dm_A(slot, ll):
            prs = []
            for h in range(2):
                for d2 in range(2):
                    prs.append((slot[:, :, (2 * h + d2) * 64:(2 * h + d2 + 1) * 64], wcols(win_d, ll, 512 + 64 * h, 64)))
            prs.append((slot[:, :, 256:384], wcols(win_d, ll, 640, 128)))
            return prs
        pre_A = [None]

        for l in range(nlayers):
            last = (l == DEPTH - 1)
            sl = l % 2
            chunks = [c for ci, c in enumerate(CHUNKS) if not (last and ci == 0)]
            if l > 0:
                S.barrier()
            S.op('pool', lambda e: [e.dma_start(out=tabC, in_=tab_d[:, 0:T]),
                                    e.dma_start(out=tabS, in_=tab_d[:, T:2 * T])],
                 writes=[R('tab')], lane=lane_tab, ndma=2)
            S.op('dve', lambda e: e.memset(abuf[:, :, :], 0.0), writes=[R('apad')])
            S.op('dve', lambda e: e.memset(Kz[:, :, :], 0.0), writes=[R('k', h_, u) for h_ in range(2) for u in range(9)])
            S.op('dve', lambda e: e.memset(Vaug[:, :, :, 0:64], 1.0), writes=[R('v', tt) for tt in range(18)])
            S.op('dve', lambda e: e.memset(Vaug[:, :, :, 128:192], 1.0), writes=[R('v', tt) for tt in range(18)])

            dg_next = [0]

            def emit_diags(k_):
                for _ in range(k_):
                    d_ = dg_next[0]
                    if d_ >= 124:
                        return
                    dg_next[0] += 1
                    if (d_ // 4) % 2 == 0:
                        S.op('dve', lambda e: e.tensor_scalar(out=dgv(d_), in0=identb[:, :], scalar1=pc(l, O_DWW + d_),
                                                              scalar2=None, op0=ALU.mult),
                             reads=[R('bdm'), R('const')], writes=[dgres(d_)])
                    else:
                        S.op('act', lambda e: e.activation(out=dgv(d_), in_=identb[:, :], func=AF.Identity,
                                                           scale=pc(l, O_DWW + d_)),
                             reads=[R('bdm'), R('const')], writes=[dgres(d_)])
            rp = [(tmp[i_], R('tmp', i_)) for i_ in range(6)] + \
                 [(convy[:, j_, :], R('cy', j_)) for j_ in range(3)]
            inst_no = [0]
            stageA = []
            stageB = []

            def nr_1b(it):
                bank, s, n, gcol, qi = it['bank'], it['s'], it['n'], it['gcol'], it['qi']
                (T0, r0), (T1, r1), _ = it['tiles']
                b2 = 4 + nxt('bank', 4)
                S.op('pe', lambda e: e.matmul(PB(b2, n), bdm[:, :], sqb[qi][:, 0:n], start=True, stop=True),
                     reads=[R('sqb', qi), R('bdm')], writes=[R('ps', b2)])
                S.op('act', lambda e: e.activation(out=T0[:, 0:n], in_=PB(b2, n), func=AF.Ln, bias=epsc[:, 1:2]),
                     reads=[R('ps', b2), R('epsc')], writes=[r0])
                S.op('act', lambda e: e.activation(out=T0[:, 0:n], in_=T0[:, 0:n], func=AF.Exp, scale=-0.5),
                     reads=[r0], writes=[r0])
                S.op('dve', lambda e: e.scalar_tensor_tensor(out=T1[:, 0:n], in0=PB(bank, n), scalar=gcol,
                                                             in1=T0[:, 0:n], op0=ALU.mult, op1=ALU.mult),
                     reads=[R('ps', bank), r0, R('const')], writes=[r1])

            def nr_2(it):
                s, n, dst_ap, dst_res = it['s'], it['n'], it['dst_ap'], it['dst_res']
                (T0, r0), (T1, r1), (T2, r2) = it['tiles']
                b3 = 4 + nxt('bank', 4)
                S.op('pe', lambda e: e.matmul(PB(b3, n), cmat[:, 2, :], T1[:, 0:n], start=True, stop=True),
                     reads=[r1, R('const')], writes=[R('ps', b3)])
                S.op('dve', lambda e: e.tensor_tensor(out=T0[:, 0:n], in0=T1[:, 0:n], in1=tabC[:, s:s + n], op=ALU.mult),
                     reads=[r1, R('tab')], writes=[r0])
                S.op('dve', lambda e: e.tensor_tensor(out=T2[:, 0:n], in0=PB(b3, n), in1=tabS[:, s:s + n], op=ALU.mult),
                     reads=[R('ps', b3), R('tab')], writes=[r2])
                if isinstance(dst_ap, tuple):
                    for hp, d_ap in enumerate(dst_ap):
                        rows = slice(64 * hp, 64 * hp + 64)
                        S.op('dve', lambda e: e.tensor_tensor(out=d_ap, in0=T0[rows, 0:n], in1=T2[rows, 0:n], op=ALU.add),
                             reads=[r0, r2], writes=dst_res)
                else:
                    S.op('dve', lambda e: e.tensor_tensor(out=dst_ap, in0=T0[:, 0:n], in1=T2[:, 0:n], op=ALU.add),
                         reads=[r0, r2], writes=dst_res)

            def nr_submit(bank, s, n, gcol, dst_ap, dst_res):
                k_ = inst_no[0]
                inst_no[0] += 1
                qi = nxt('sqb', 2)
                S.op('act', lambda e: e.activation(out=sqb[qi][:, 0:n], in_=PB(bank, n), func=AF.Square),
                     reads=[R('ps', bank)], writes=[R('sqb', qi)])
                stageA.append(dict(bank=bank, s=s, n=n, gcol=gcol, qi=qi, dst_ap=dst_ap, dst_res=dst_res,
                                   tiles=[rp[(3 * k_ + j_) % 9] for j_ in range(3)]))
                if len(stageA) > 1:
                    it = stageA.pop(0)
                    nr_1b(it)
                    stageB.append(it)
                if len(stageB) > 1:
                    nr_2(stageB.pop(0))
                emit_diags(4)

            def flush_pend():
                while stageA or stageB:
                    if stageA:
                        it = stageA.pop(0)
                        nr_1b(it)
                        stageB.append(it)
                    if stageB and (len(stageB) > 1 or not stageA):
                        nr_2(stageB.pop(0))

            def proj(si, c0, s, n, bank):
                pairs = [(wsl[si][:, k, c0:c0 + 128], hT[:, k, s:s + n]) for k in range(KT)]

                def fn(e):
                    last_ = None
                    for i, (a_, b_) in enumerate(pairs):
                        last_ = e.matmul(PB(bank, n), a_, b_, start=(i == 0), stop=(i == KT - 1))
                    return last_
                S.op('pe', fn, reads=[R('ws', si)] + [r for k in range(KT) for r in hres(k, s, n)],
                     writes=[R('ps', bank)])

            if pre_A[0] is not None:
                si = pre_A[0]
                pre_A[0] = None
            else:
                si = wload(lambda slot: dm_A(slot, l), 5)
            for h in range(2):
                for (s, n) in CHUNKS:
                    bank = nxt('bank', 4)
                    proj(si, 128 * h, s, n, bank)
                    nr_submit(bank, s, n, pc(l, O_KG), (Kz[0:64, 2 * h, s:s + n], Kz[64:128, 2 * h + 1, s:s + n]),
                              [R('k', h, u) for u in units(s, n)])
            flush_pend()
            for tt in range(18):
                bank = nxt('bank', 4)
                pairs = [(hT[:, k, tt * 128:(tt + 1) * 128], wsl[si][:, k, 256:384]) for k in range(KT)]

                def fnv(e, pairs=pairs, bank=bank):
                    last_ = None
                    for i, (a_, b_) in enumerate(pairs):
                        last_ = e.matmul(PB(bank, 128), a_, b_, start=(i == 0), stop=(i == KT - 1))
                    return last_
                S.op('pe', fnv, reads=[R('ws', si)] + [r for k in range(KT) for r in hres(k, tt * 128, 128)],
                     writes=[R('ps', bank)])
                S.op('act', lambda e, tt=tt, bank=bank: e.activation(
                    out=Vaug[:, tt, :, 64:128], in_=PB(bank, 128).rearrange("p (h c) -> p h c", h=2), func=AF.Copy),
                    reads=[R('ps', bank)], writes=[R('v', tt)])
            for jb in range(2):
                def dm_B(slot, jb=jb):
                    prs = []
                    for jj in range(2):
                        j = 2 * jb + jj
                        prs.append((slot[:, :, jj * 256:jj * 256 + 128], wcols(win_d, l, 768 + 128 * j, 128)))
                        prs.append((slot[:, :, jj * 256 + 128:jj * 256 + 256], wcols(win_d, l, 1280 + 128 * j, 128)))
                    return prs
                si = wload(dm_B, 4)
                for jj in range(2):
                    j = 2 * jb + jj
                    for (s, n) in chunks:
                        bu = nxt('bank', 4)
                        bg = nxt('bank', 4)
                        proj(si, jj * 256, s, n, bu)
                        proj(si, jj * 256 + 128, s, n, bg)
                        ti = nxt('tmp', NTMP)
                        S.op('act', lambda e, bg=bg, ti=ti, n=n: e.activation(out=tmp[ti][:, 0:n], in_=PB(bg, n),
                                                                                func=AF.Sigmoid),
                             reads=[R('ps', bg)], writes=[R('tmp', ti)])
                        a0 = aidx(s)
                        S.op('dve', lambda e, bu=bu, ti=ti, n=n, j=j, a0=a0: e.tensor_tensor(
                            out=abuf[:, j, a0:a0 + n], in0=PB(bu, n), in1=tmp[ti][:, 0:n], op=ALU.mult),
                            reads=[R('ps', bu), R('tmp', ti), R('apad')], writes=[R('a', j, u) for u in units(s, n)])
            si = wload(lambda slot: [(slot[:, :, :], wcols(win_d, l, 0, 512))], 1)
            for i in range(4):
                for (s, n) in chunks:
                    bank = nxt('bank', 4)
                    proj(si, 128 * i, s, n, bank)
                    nr_submit(bank, s, n, pc(l, O_QG), qT[:, i, s:s + n], [R('q', i, u) for u in units(s, n)])
            flush_pend()

            if l == 0:
                allh = [R('h', k, u) for k in range(KT) for u in range(9)]
                tap(hT[:, 0, :], T, allh)
                tap(Kz[:, 0, :], T, [R('k', 0, u) for u in range(9)])
                tap(Kz[:, 1, :], T, [R('k', 0, u) for u in range(9)])
                tap(qT[:, 0, :], T, [R('q', 0, u) for u in range(9)])
                tap(Vaug[:, :, 0, 64:128], 18 * 64, [R('v', tt) for tt in range(18)])
                tap(abuf[:, 0, :], APAD, [R('a', 0, u) for u in range(9)] + [R('apad')])

            emit_diags(124)
            def conv_gen():
                for ci, (s, n) in enumerate(CHUNKS):
                    if last and ci == 0:
                        continue
                    seg0, seg1 = (0, 256) if s < 256 else (256, T)
                    rd_units = units(max(seg0, s - 15), min(seg1, s + n + 15) - max(seg0, s - 15))
                    base = aidx(s) - 15
                    for j in range(4):
                        rd = [R('a', j, u) for u in rd_units] + [R('apad')]
                        cb = 6
                        for k0 in range(0, 31, 3):
                            ks = list(range(k0, min(31, k0 + 3)))

                            def fnc(e):
                                last_ = None
                                for k in ks:
                                    last_ = e.matmul(PB(cb, n), dgv(j * 31 + k), abuf[:, j, base + k:base + k + n],
                                                     start=(k == 0), stop=(k == 30))
                                return last_
                            S.op('pe', fnc, reads=rd + [dgres(j * 31 + k) for k in ks], writes=[R('ps', cb)])
                            yield
                        S.op('act', lambda e: e.activation(out=convy[:, j, 0:n], in_=PB(cb, n), func=AF.Identity,
                                                           bias=pc(l, O_DWB + j)),
                             reads=[R('ps', cb), R('const')], writes=[R('cy', j)])

                    def outfn(j, s=s, n=n):
                        S.op('act', lambda e: e.activation(out=hT[:, 4 + j, s:s + n], in_=convy[:, j, 0:n], func=AF.Silu,
                                                           scale=pc(l, O_CLG + j), bias=pc(l, O_CLB + j)),
                             reads=[R('cy', j), R('const')], writes=hres(4 + j, s, n))
                    hold['ln'] = True
                    ones = cmat[:, 0, :]

                    def f1c(e):
                        last_ = None
                        for k in range(4):
                            last_ = e.matmul(PB(7, n), ones, convy[:, k, 0:n], start=(k == 0), stop=(k == 3))
                        return last_
                    S.op('pe', f1c, reads=[R('cy', k) for k in range(4)] + [R('const')], writes=[R('ps', 7)])
                    yield
                    for k in range(4):
                        ti = nxt('sqb', 2)
                        S.op('act', lambda e: e.activation(out=sqb[ti][:, 0:n], in_=convy[:, k, 0:n], func=AF.Square),
                             reads=[R('cy', k)], writes=[R('sqb', ti)])
                        S.op('pe', lambda e: e.matmul(PB(6, n), onesb[:, :], sqb[ti][:, 0:n], start=(k == 0), stop=(k == 3)),
                             reads=[R('sqb', ti), R('onesb')], writes=[R('ps', 6)])
                        yield
                    tm, tr = 2, 3
                    inv = 1.0 / 512.0
                    S.op('dve', lambda e: e.tensor_scalar(out=tmp[tm][:, 0:n], in0=PB(7, n), scalar1=inv, scalar2=None,
                                                          op0=ALU.mult),
                         reads=[R('ps', 7)], writes=[R('tmp', tm)])
                    S.op('dve', lambda e: e.tensor_tensor(out=tmp[tr][:, 0:n], in0=tmp[tm][:, 0:n], in1=tmp[tm][:, 0:n],
                                                          op=ALU.mult),
                         reads=[R('tmp', tm)], writes=[R('tmp', tr)])
                    S.op('dve', lambda e: e.scalar_tensor_tensor(out=tmp[tr][:, 0:n], in0=PB(6, n), scalar=inv,
                                                                 in1=tmp[tr][:, 0:n], op0=ALU.mult, op1=ALU.subtract),
                         reads=[R('ps', 6), R('tmp', tr)], writes=[R('tmp', tr)])
                    hold['ln'] = False
                    yield
                    S.op('act', lambda e: e.activation(out=tmp[tr][:, 0:n], in_=tmp[tr][:, 0:n], func=AF.Ln,
                                                       bias=epsc[:, 0:1]),
                         reads=[R('tmp', tr), R('epsc')], writes=[R('tmp', tr)])
                    S.op('act', lambda e: e.activation(out=tmp[tr][:, 0:n], in_=tmp[tr][:, 0:n], func=AF.Exp, scale=-0.5),
                         reads=[R('tmp', tr)], writes=[R('tmp', tr)])
                    yield
                    for k in range(4):
                        S.op('dve', lambda e: e.tensor_tensor(out=convy[:, k, 0:n], in0=convy[:, k, 0:n],
                                                              in1=tmp[tm][:, 0:n], op=ALU.subtract),
                             reads=[R('cy', k), R('tmp', tm)], writes=[R('cy', k)])
                        S.op('dve', lambda e: e.tensor_tensor(out=convy[:, k, 0:n], in0=convy[:, k, 0:n],
                                                              in1=tmp[tr][:, 0:n], op=ALU.mult),
                             reads=[R('cy', k), R('tmp', tr)], writes=[R('cy', k)])
                        yield
                    for k in range(4):
                        outfn(k)
                    yield
            def mods_filler():
                if l + 1 < nlayers:
                    for _ in mods_gen(l + 1):
                        yield
                    emit_hcoef(l + 1, 0, O_LN2G, O_LN2B, l)
                emit_hcoef(l, 1, O_LN1G, O_LN1B, l)
                yield
            hold = {'ln': False}
            fillers = [conv_gen(), mods_filler()]
            fstate = {'i': 0}

            def conv_step(k=1):
                for _ in range(k):
                    if not fillers:
                        return
                    fstate['i'] = (fstate['i'] + 1) % len(fillers)
                    if hold['ln']:
                        fstate['i'] = 0
                    g_ = fillers[fstate['i']]
                    try:
                        next(g_)
                    except StopIteration:
                        fillers.remove(g_)

            for qi_, (s, n) in enumerate(QCH):
                if last and qi_ == 0:
                    continue
                ktiles = [0, 1] if qi_ == 0 else list(range(18))
                for i in range(4):
                    h = i // 2
                    stages = [(kp, p) for kp in range(len(ktiles) // 2) for p in range(2)]
                    oacc = [4, 5]
                    sc_state = {}

                    def emit_scores(idx, stg_):
                        kp, p = stg_
                        bpair = 2 * (idx % 2)
                        kts = [ktiles[2 * kp], ktiles[2 * kp + 1]]

                        def fs(e):
                            last_ = None
                            for jj in range(2):
                                last_ = e.matmul(PB(bpair + jj, n), Kz[:, 2 * h + p, kts[jj] * 128:(kts[jj] + 1) * 128],
                                                 qT[:, i, s:s + n], start=True, stop=True)
                            return last_
                        S.op('pe', fs, reads=[R('k', h, kt_ // 2) for kt_ in kts] + [R('q', i, u) for u in units(s, n)],
                             writes=[R('ps', bpair), R('ps', bpair + 1)])
                        pi = nxt('pt', NPT)
                        S.op('act', lambda e: e.activation(
                            out=PT[pi][:, :, 0:n], in_=ps[:, bpair:bpair + 2, 0:n], func=AF.Exp, scale=8.0),
                            reads=[R('ps', bpair), R('ps', bpair + 1)], writes=[R('pt', pi)])
                        sc_state[idx] = pi

                    def emit_pv(idx, stg_):
                        kp, p = stg_
                        pi = sc_state[idx]
                        c0 = 64 if p == 0 else 0
                        kts = [ktiles[2 * kp], ktiles[2 * kp + 1]]
                        nkp = len(ktiles) // 2

                        def fp(e):
                            last_ = None
                            for jj in range(2):
                                last_ = e.matmul(PB(oacc[p], n), Vaug[:, kts[jj], h, c0:c0 + 128], PT[pi][:, jj, 0:n],
                                                 start=(kp == 0 and jj == 0), stop=(kp == nkp - 1 and jj == 1))
                            return last_
                        S.op('pe', fp, reads=[R('v', kt_) for kt_ in kts] + [R('pt', pi)], writes=[R('ps', oacc[p])])
                    for idx, stg_ in enumerate(stages):
                        emit_scores(idx, stg_)
                        if idx > 0:
                            emit_pv(idx - 1, stages[idx - 1])
                        conv_step(1)
                    emit_pv(len(stages) - 1, stages[-1])
                    for p in range(2):
                        srow = slice(64, 128) if p == 0 else slice(0, 64)
                        orow = slice(0, 64) if p == 0 else slice(64, 128)
                        ob = oacc[p]
                        ti = 4 + p
                        S.op('act', lambda e: e.activation(out=tmp[ti][:, 0:n], in_=ps[:, ob, 0:n], func=AF.Copy),
                             reads=[R('ps', ob)], writes=[R('tmp', ti)])
                        S.op('dve', lambda e: e.reciprocal(out=RR[p][srow, 0:n], in_=tmp[ti][srow, 0:n]),
                             reads=[R('tmp', ti)], writes=[R('rr', p)])
                        S.op('sp', lambda e: [e.dma_start(out=RR[p][orow, 0:n], in_=RR[p][srow, 0:n])],
                             reads=[R('rr', p)], writes=[R('rrm', p)], lane=lane_rr[p], ndma=1)
                        S.op('dve', lambda e: e.tensor_tensor(
                            out=hT[orow, i, s:s + n], in0=tmp[ti][orow, 0:n], in1=RR[p][orow, 0:n], op=ALU.mult),
                            reads=[R('tmp', ti), R('rrm', p)], writes=hres(i, s, n))
            conv_step(10 ** 6)

            if l == 0:
                allh = [R('h', k, u) for k in range(KT) for u in range(9)]
                tap(hT[:, 0, :], T, allh)
                tap(hT[:, 4, :], T, allh)
            S.op('pool', lambda e: [e.dma_start(out=woA[:, :, :], in_=wcols(wout_d, l, 0, 512))] +
                 [e.dma_start(out=woB[q_][:, :, :], in_=wout_d[l, 2 * q_ * P:(2 * q_ + 2) * P, 512:1024]
                              .rearrange("(k p) c -> p k c", p=P)) for q_ in range(4)],
                 writes=[R('cy', j_) for j_ in range(4)] + [R('pt', 0), R('pt', 1), R('rr', 0), R('rr', 1),
                                                            R('rrm', 0), R('rrm', 1), R('wo')],
                 lane=lane_wo, ndma=5)
            S.barrier()
            S.op('pool', lambda e: [e.dma_start(out=w2s[:, 0:11, :],
                                                in_=wf2_d[l, 0:11 * P, :].rearrange("(f p) c -> p f c", p=P)),
                                    e.dma_start(out=w2s[:, 11:22, :],
                                                in_=wf2_d[l, 11 * P:22 * P, :].rearrange("(f p) c -> p f c", p=P))],
                 writes=[R('w2')], lane=lane_w2, ndma=2)
            ones = cmat[:, 0, :]
            sbk = [0]

            def ln_job(s, n, col, mm_fn, gate_off, outfn_maker, out_dst):
                st = {}

                def A():
                    si = stage_in(xa_d, s, n)
                    st['si'] = si
                    for ot in range(KT):
                        bank = nxt('bank', 4)
                        S.op('pe', lambda e: mm_fn(e, ot, bank), reads=st_reads(ot), writes=[R('ps', bank)])
                        S.op('dve', lambda e: e.scalar_tensor_tensor(
                            out=stg[si][:, ot, 0:n], in0=PB(bank, n), scalar=modT[sl][:, gate_off + ot, col:col + 1],
                            in1=stg[si][:, ot, 0:n], op0=ALU.mult, op1=ALU.add),
                            reads=[R('ps', bank), R('mod', sl), R('stg', si, ot)], writes=[R('stg', si, ot)])
                st_reads = mm_fn.reads

                def B1():
                    si = st['si']
                    b1, b2 = [(6, 7), (4, 5)][sbk[0] % 2]
                    sbk[0] += 1
                    st['banks'] = (b1, b2)

                    def f1(e):
                        last_ = None
                        for k in range(KT):
                            last_ = e.matmul(PB(b1, n), ones, stg[si][:, k, 0:n], start=(k == 0), stop=(k == KT - 1))
                        return last_
                    S.op('pe', f1, reads=[R('stg', si, k) for k in range(KT)] + [R('const')], writes=[R('ps', b1)])
                    for k in range(KT):
                        ti = nxt('sqb', 2)
                        S.op('act', lambda e: e.activation(out=sqb[ti][:, 0:n], in_=stg[si][:, k, 0:n], func=AF.Square),
                             reads=[R('stg', si, k)], writes=[R('sqb', ti)])
                        S.op('pe', lambda e: e.matmul(PB(b2, n), onesb[:, :], sqb[ti][:, 0:n], start=(k == 0),
                                                      stop=(k == KT - 1)),
                             reads=[R('sqb', ti), R('onesb')], writes=[R('ps', b2)])

                def B2():
                    si = st['si']
                    b1, b2 = st['banks']
                    tm = nxt('tmp', NTMP)
                    tr = nxt('tmp', NTMP)
                    inv = 1.0 / 1024.0
                    S.op('dve', lambda e: e.tensor_scalar(out=tmp[tm][:, 0:n], in0=PB(b1, n), scalar1=inv, scalar2=None,
                                                          op0=ALU.mult),
                         reads=[R('ps', b1)], writes=[R('tmp', tm)])
                    S.op('dve', lambda e: e.tensor_tensor(out=tmp[tr][:, 0:n], in0=tmp[tm][:, 0:n], in1=tmp[tm][:, 0:n],
                                                          op=ALU.mult),
                         reads=[R('tmp', tm)], writes=[R('tmp', tr)])
                    S.op('dve', lambda e: e.scalar_tensor_tensor(out=tmp[tr][:, 0:n], in0=PB(b2, n), scalar=inv,
                                                                 in1=tmp[tr][:, 0:n], op0=ALU.mult, op1=ALU.subtract),
                         reads=[R('ps', b2), R('tmp', tr)], writes=[R('tmp', tr)])
                    S.op('act', lambda e: e.activation(out=tmp[tr][:, 0:n], in_=tmp[tr][:, 0:n], func=AF.Ln,
                                                       bias=epsc[:, 0:1]),
                         reads=[R('tmp', tr), R('epsc')], writes=[R('tmp', tr)])
                    S.op('act', lambda e: e.activation(out=tmp[tr][:, 0:n], in_=tmp[tr][:, 0:n], func=AF.Exp, scale=-0.5),
                         reads=[R('tmp', tr)], writes=[R('tmp', tr)])
                    outfn = outfn_maker(si)
                    for k in range(KT):
                        S.op('dve', lambda e: e.tensor_tensor(out=stg[si][:, k, 0:n], in0=stg[si][:, k, 0:n],
                                                              in1=tmp[tm][:, 0:n], op=ALU.subtract),
                             reads=[R('stg', si, k), R('tmp', tm)], writes=[R('stg', si, k)])
                        S.op('dve', lambda e: e.tensor_tensor(out=stg[si][:, k, 0:n], in0=stg[si][:, k, 0:n],
                                                              in1=tmp[tr][:, 0:n], op=ALU.mult),
                             reads=[R('stg', si, k), R('tmp', tr)], writes=[R('stg', si, k)])
                        outfn(k)
                    if out_dst is out_d:
                        stage_out(out_d, si, s - LC, n, s)
                    else:
                        stage_out(xa_d, si, s, n, s)
                return A, B1, B2

            def pipeline_steps(jobs):
                m = len(jobs)
                steps = []
                for i_ in range(m + 2):
                    fs = []
                    if i_ < m:
                        fs.append(jobs[i_][0])
                    if 0 <= i_ - 1 < m:
                        fs.append(jobs[i_ - 1][1])
                    if 0 <= i_ - 2 < m:
                        fs.append(jobs[i_ - 2][2])
                    steps.append(fs)
                return steps

            def run_step(fs):
                for f_ in fs:
                    f_()

            def mk_wout_mm(s, n):
                def mm(e, ot, bank):
                    c0 = (ot % 4) * 128
                    last_ = None
                    for k in range(KT):
                        w_ap = woA[:, k, c0:c0 + 128] if ot < 4 else woB[k // 2][:, k % 2, c0:c0 + 128]
                        last_ = e.matmul(PB(bank, n), w_ap, hT[:, k, s:s + n], start=(k == 0), stop=(k == KT - 1))
                    return last_
                mm.reads = lambda ot: [R('wo')] + [r for k in range(KT) for r in hres(k, s, n)]
                return mm
            ln1_jobs = []
            for ci, (s, n) in enumerate(LNCH):
                if last and ci == 0:
                    continue
                col = 1 if ci == 0 else 0
                ln1_jobs.append(ln_job(
                    s, n, col, mk_wout_mm(s, n), 16,
                    lambda si, s=s, n=n, col=col: ln_outputs(si, s, n, sl, 1, lambda k: pal[:, l, k:k + 1],
                                                             lambda k: pal[:, l, 8 + k:9 + k], col),
                    xa_d))
            lnq = pipeline_steps(ln1_jobs)
            n_pre = 5 if not last else 4
            for _ in range(n_pre):
                run_step(lnq.pop(0))

            for gi, grp in enumerate(GROUPS):
                gch = [(ci, CHUNKS[ci]) for ci in grp if not (last and ci == 0)]
                g0 = gch[0][1][0]
                g1 = gch[-1][1][0] + gch[-1][1][1]
                for blk in range(11):
                    def dm_F(slot, blk=blk):
                        return [(slot[:, :, 0:256], wcols(wf1_d, l, blk * 256, 256)),
                                (slot[:, :, 256:512], wcols(wf3_d, l, blk * 256, 256))]
                    si = wload(dm_F, 2)
                    for f2 in range(2):
                        ft = blk * 2 + f2
                        for ci, (s, n) in gch:
                            b1 = nxt('bank', 4)
                            b3 = nxt('bank', 4)
                            proj(si, f2 * 128, s, n, b1)
                            proj(si, 256 + f2 * 128, s, n, b3)
                            ti = nxt('tmp', NTMP)
                            S.op('act', lambda e: e.activation(out=tmp[ti][:, 0:n], in_=PB(b1, n), func=AF.Silu),
                                 reads=[R('ps', b1)], writes=[R('tmp', ti)])
                            lo = s - g0
                            S.op('dve', lambda e: e.tensor_tensor(
                                out=act[:, ft, lo:lo + n], in0=PB(b3, n), in1=tmp[ti][:, 0:n], op=ALU.mult),
                                reads=[R('ps', b3), R('tmp', ti)],
                                writes=[R('act', ft, lo // 256), R('act', ft, (lo + n - 1) // 256)])
                    if lnq and (gi > 0 or blk % 2 == 1 or len(lnq) > 11 - blk):
                        run_step(lnq.pop(0))
                while lnq:
                    run_step(lnq.pop(0))
                jobs = []
                for (s, n) in LNCH:
                    if s < g0 or s >= g1:
                        continue
                    col = 1 if s == 0 else 0
                    lo = s - g0

                    def mk_w2_mm(lo=lo, n=n):
                        def mm(e, ot, bank):
                            last_ = None
                            for f in range(FT):
                                last_ = e.matmul(PB(bank, n), w2s[:, f, ot * 128:(ot + 1) * 128], act[:, f, lo:lo + n],
                                                 start=(f == 0), stop=(f == FT - 1))
                            return last_
                        mm.reads = lambda ot: [R('w2')] + [R('act', f, lo // 256) for f in range(FT)]
                        return mm
                    if last:
                        om = lambda si, s=s, n=n, col=col: ln_outputs(
                            si, s, n, None, None, None, None, col, final=True,
                            g_ap=lambda k: par[:, l, O_LN2G + k:O_LN2G + k + 1],
                            b_ap=lambda k: par[:, l, O_LN2B + k:O_LN2B + k + 1])
                    else:
                        om = lambda si, s=s, n=n, col=col: ln_outputs(
                            si, s, n, (l + 1) % 2, 0, lambda k: pal[:, l, 16 + k:17 + k],
                            lambda k: pal[:, l, 24 + k:25 + k], col, skip_h=(l == nlayers - 1))
                    jobs.append(ln_job(s, n, col, mk_w2_mm(), 40, om, out_d if last else xa_d))
                for jb in jobs:
                    jb[0]()
                m_ = len(jobs)
                lnq = []
                for i_ in range(m_ + 1):
                    fs = []
                    if i_ < m_:
                        fs.append(jobs[i_][1])
                    if 0 <= i_ - 1 < m_:
                        fs.append(jobs[i_ - 1][2])
                    lnq.append(fs)
                if gi == len(GROUPS) - 1:
                    while lnq:
                        run_step(lnq.pop(0))
                    if l + 1 < nlayers:
                        pre_A[0] = wload(lambda slot: dm_A(slot, l + 1), 5)
            if l == nlayers - 1 and not last:
                S.barrier()
                for (s, n) in LNCH[1:]:
                    si = stage_in(xa_d, s, n)
                    stage_out(out_d, si, s - LC, n, s)

        S.final_wait('sp', stg_out + [lane_dbg])

        with nc.Block() as block:
            @block.tensor
            def _(e):
                for f in S.progs['pe']:
                    f(e)

            @block.scalar
            def _(e):
                for f in S.progs['act']:
                    f(e)

            @block.vector
            def _(e):
                for f in S.progs['dve']:
                    f(e)

            @block.gpsimd
            def _(e):
                for f in S.progs['pool']:
                    f(e)

            @block.sync
            def _(e):
                for f in S.progs['sp']:
                    f(e)
    return nc


def _host_consts():
    ones = np.ones((P, P), np.float32)
    sw = np.zeros((P, P), np.float32)
    for i in range(P):
        sw[i, (i + 64) % P] = 1.0
    rot = np.zeros((P, P), np.float32)
    for blk in range(4):
        for f in range(16):
            a = blk * 32 + f
            rot[a + 16, a] = -1.0
            rot[a, a + 16] = 1.0
    bd = np.zeros((P, P), np.float32)
    bd[:64, :64] = 1.0
    bd[64:, 64:] = 1.0
    cmat = np.concatenate([ones, sw, rot, bd, np.eye(P, dtype=np.float32)], axis=1)
    tC = np.ones((P, T), np.float32)
    tS = np.zeros((P, T), np.float32)
    tok = np.arange(NL)
    pos = np.stack([tok // 64, tok % 64], 0).astype(np.float32)
    freqs = (10000.0 ** (-np.arange(16, dtype=np.float32) / 16.0)).astype(np.float32)
    for p_ in range(P):
        d = p_ % 64
        ang = (pos[d // 32] * freqs[d % 16]).astype(np.float32)
        tC[p_, LC:] = np.cos(ang)
        tS[p_, LC:] = np.sin(ang)
    tabs = np.concatenate([tC, tS], axis=1)
    return cmat, tabs


def _pack_params(b_ada, q_norm_g, k_norm_g, dw_w, dw_b, conv_ln_g, conv_ln_b, ln1_g, ln1_b, ln2_g, ln2_b):
    par = np.zeros((P, DEPTH, NPAR), np.float32)
    for l in range(DEPTH):
        par[:, l, 0:48] = b_ada[l].reshape(48, P).T
        par[:, l, 48:56] = ln1_g[l].reshape(8, P).T
        par[:, l, 56:64] = ln1_b[l].reshape(8, P).T
        par[:, l, 64:72] = ln2_g[l].reshape(8, P).T
        par[:, l, 72:80] = ln2_b[l].reshape(8, P).T
        w = dw_w[l, :, 0, :]
        for j in range(4):
            par[:, l, 80 + j * 31:80 + (j + 1) * 31] = w[:, j * P:(j + 1) * P].T
        par[:, l, 204:208] = dw_b[l].reshape(4, P).T
        par[:, l, 208:212] = conv_ln_g[l].reshape(4, P).T
        par[:, l, 212:216] = conv_ln_b[l].reshape(4, P).T
        par[:, l, 216] = np.tile(q_norm_g[l], 2)
        par[:, l, 217] = np.tile(k_norm_g[l], 2)
    return par.reshape(P, DEPTH * NPAR)


_NC_CACHE = {}


def kernel(x, c, ctx, c_ctx, w_ada, b_ada, w_in, q_norm_g, k_norm_g, dw_w, dw_b,
           conv_ln_g, conv_ln_b, w_out, ln1_g, ln1_b, w_ff1, w_ff3, w_ff2, ln2_g, ln2_b,
           _nlayers=DEPTH, _dbg=None):
    f = lambda a: np.ascontiguousarray(np.asarray(a, dtype=np.float32))
    x, c, ctx, c_ctx = f(x), f(c), f(ctx), f(c_ctx)
    cmat, tabs = _host_consts()
    par = _pack_params(f(b_ada), f(q_norm_g), f(k_norm_g), f(dw_w), f(dw_b), f(conv_ln_g), f(conv_ln_b),
                       f(ln1_g), f(ln1_b), f(ln2_g), f(ln2_b))
    key = (_nlayers, _dbg)
    if key not in _NC_CACHE:
        _NC_CACHE[key] = build(_nlayers, _dbg)
    nc = _NC_CACHE[key]
    shared = {"par": par, "cmat": cmat, "tabs": tabs, "w_ada": f(w_ada), "w_in": f(w_in), "w_out": f(w_out),
              "w_ff1": f(w_ff1), "w_ff3": f(w_ff3), "w_ff2": f(w_ff2)}
    in_maps = []
    for b in range(8):
        xT = np.ascontiguousarray(np.concatenate([ctx[b], x[b]], axis=0).T)
        cs = np.ascontiguousarray(np.stack([c[b], c_ctx], -1).reshape(KT, P, 2).transpose(1, 0, 2).reshape(P, KT * 2))
        m = dict(shared)
        m["xT"] = xT
        m["cs"] = cs
        in_maps.append(m)
    res = run_bass_kernel_spmd(nc, in_maps, core_ids=list(range(8)))
    out = np.stack([np.ascontiguousarray(r["outT"].T) for r in res.results], axis=0)
    if _dbg:
        kernel.dbg = [r["dbg"] for r in res.results]
    return out.astype(np.float32)
```
